# Optimizing a Trainium2 kernel written in Bass

```python
import math
import jax, jax.numpy as jnp
from jax import lax
import numpy as np

D_MODEL = 2048
BATCH = 2
SEQ = 8192
DEPTH = 1

HEAD_DIM = 128
ROPE_FRAC_DIV = 4
ROPE_DIM = HEAD_DIM // ROPE_FRAC_DIV
ROPE_THETA = 500000.0
DSA_HEADS = 8
DSA_W = DSA_HEADS * HEAD_DIM
DSA_Q_RANK = 512
IDX_HEADS = 16
IDX_DIM = 64
IDX_ROPE_DIM = IDX_DIM // ROPE_FRAC_DIV
IDX_TOPK = 256
IDX_Q_BLOCK = 128
MOBA_HEADS = 8
MOBA_W = MOBA_HEADS * HEAD_DIM
MOBA_BLOCK = 256
MOBA_TOPK = 3
MOBA_Q_BLOCK = 32
MEM_LEN = 256
MEM_HEADS = 4
MEM_HEAD_DIM = 128
MEM_W = MEM_HEADS * MEM_HEAD_DIM
D_FF = 4 * D_MODEL
EPS = 1e-6
IN_SIZES = (DSA_Q_RANK, DSA_W, DSA_W, IDX_DIM, IDX_HEADS, MOBA_W, MOBA_W, MOBA_W, D_MODEL, D_MODEL)
IN_COLS = DSA_Q_RANK + 2 * DSA_W + IDX_DIM + IDX_HEADS + 3 * MOBA_W + 2 * D_MODEL

kernel_name = "hybrid_dsa_moba_gated_block"


def _split_points():
    pts, acc = [], 0
    for s in IN_SIZES[:-1]:
        acc += s
        pts.append(acc)
    return pts


def rmsnorm(x, g):
    x32 = x.astype(jnp.float32)
    y = x32 * lax.rsqrt(jnp.mean(x32 * x32, axis=-1, keepdims=True) + EPS)
    return (y * g.astype(jnp.float32)).astype(x.dtype)


def rope_partial(x, pos, rot_dim):
    half = rot_dim // 2
    inv = ROPE_THETA ** (-jnp.arange(half, dtype=jnp.float32) * (2.0 / rot_dim))
    ang = pos.astype(jnp.float32)[..., None] * inv
    cos = jnp.cos(ang)[:, :, None, :]
    sin = jnp.sin(ang)[:, :, None, :]
    xr = x[..., :rot_dim].astype(jnp.float32)
    x1, x2 = xr[..., :half], xr[..., half:]
    rot = jnp.concatenate([x1 * cos - x2 * sin, x2 * cos + x1 * sin], axis=-1).astype(x.dtype)
    return jnp.concatenate([rot, x[..., rot_dim:]], axis=-1)


def dsa_branch(c_q, k, v, k_idx, w_idx, g_cq, w_uq, w_iq, pos):
    B, S, _ = k.shape
    c_q = rmsnorm(c_q, g_cq)
    q = rope_partial((c_q @ w_uq).reshape(B, S, DSA_HEADS, HEAD_DIM), pos, ROPE_DIM)
    q_idx = rope_partial((c_q @ w_iq).reshape(B, S, IDX_HEADS, IDX_DIM), pos, IDX_ROPE_DIM)
    k = rope_partial(k.reshape(B, S, DSA_HEADS, HEAD_DIM), pos, ROPE_DIM)
    v = v.reshape(B, S, DSA_HEADS, HEAD_DIM)
    k_idx = rope_partial(k_idx[:, :, None, :], pos, IDX_ROPE_DIM)[:, :, 0, :]
    w_idx = w_idx.astype(jnp.float32) * (IDX_HEADS ** -0.5 * IDX_DIM ** -0.5)
    n_sel = min(IDX_TOPK, S // 4)
    n_blocks = S // IDX_Q_BLOCK
    key_pos = jnp.arange(S)
    scale = HEAD_DIM ** -0.5

    def block(i):
        start = i * IDX_Q_BLOCK
        qb = lax.dynamic_slice_in_dim(q, start, IDX_Q_BLOCK, axis=1)
        qib = lax.dynamic_slice_in_dim(q_idx, start, IDX_Q_BLOCK, axis=1)
        wb = lax.dynamic_slice_in_dim(w_idx, start, IDX_Q_BLOCK, axis=1)
        t = start + jnp.arange(IDX_Q_BLOCK)
        logits = jnp.einsum('bqhd,bsd->bqhs', qib, k_idx).astype(jnp.float32)
        score = jnp.einsum('bqh,bqhs->bqs', wb, jax.nn.relu(logits))
        causal = key_pos[None, :] <= t[:, None]
        score = jnp.where(causal[None], score, -jnp.inf)
        _, sel = lax.top_k(score, n_sel)
        valid = sel <= t[None, :, None]
        kg = jax.vmap(lambda kk, ii: kk[ii])(k, sel)
        vg = jax.vmap(lambda vv, ii: vv[ii])(v, sel)
        s = jnp.einsum('bqhd,bqnhd->bhqn', qb, kg).astype(jnp.float32) * scale
        s = jnp.where(valid[:, None], s, -jnp.inf)
        p = jax.nn.softmax(s, axis=-1).astype(v.dtype)
        o = jnp.einsum('bhqn,bqnhd->bqhd', p, vg)
        return o.reshape(B, IDX_Q_BLOCK, DSA_W)

    out = lax.map(block, jnp.arange(n_blocks))
    return out.transpose(1, 0, 2, 3).reshape(B, S, DSA_W)


def moba_branch(q, k, v, pos):
    B, S, _ = q.shape
    q = rope_partial(q.reshape(B, S, MOBA_HEADS, HEAD_DIM), pos, ROPE_DIM)
    k = rope_partial(k.reshape(B, S, MOBA_HEADS, HEAD_DIM), pos, ROPE_DIM)
    v = v.reshape(B, S, MOBA_HEADS, HEAD_DIM)
    nb = -(-S // MOBA_BLOCK)
    pad = nb * MOBA_BLOCK - S
    kp = jnp.pad(k, ((0, 0), (0, pad), (0, 0), (0, 0)))
    vp = jnp.pad(v, ((0, 0), (0, pad), (0, 0), (0, 0)))
    kblk = kp.reshape(B, nb, MOBA_BLOCK, MOBA_HEADS, HEAD_DIM)
    vblk = vp.reshape(B, nb, MOBA_BLOCK, MOBA_HEADS, HEAD_DIM)
    kmean = jnp.mean(kblk.astype(jnp.float32), axis=2).astype(k.dtype)
    kbh = kblk.transpose(0, 3, 1, 2, 4)
    vbh = vblk.transpose(0, 3, 1, 2, 4)
    n_sel = min(MOBA_TOPK, nb - 1)
    n_qblocks = S // MOBA_Q_BLOCK
    scale = HEAD_DIM ** -0.5
    gather2 = jax.vmap(jax.vmap(lambda kk, ii: kk[ii]))

    def block(i):
        start = i * MOBA_Q_BLOCK
        qb = lax.dynamic_slice_in_dim(q, start, MOBA_Q_BLOCK, axis=1)
        t = start + jnp.arange(MOBA_Q_BLOCK)
        cur = start // MOBA_BLOCK
        own_k = lax.dynamic_slice_in_dim(kp, cur * MOBA_BLOCK, MOBA_BLOCK, axis=1)
        own_v = lax.dynamic_slice_in_dim(vp, cur * MOBA_BLOCK, MOBA_BLOCK, axis=1)
        own_pos = cur * MOBA_BLOCK + jnp.arange(MOBA_BLOCK)
        s_own = jnp.einsum('bqhd,bkhd->bhqk', qb, own_k).astype(jnp.float32) * scale
        s_own = jnp.where((own_pos[None, :] <= t[:, None])[None, None], s_own, -jnp.inf)
        if n_sel > 0:
            gate = jnp.einsum('bqhd,bnhd->bhqn', qb, kmean).astype(jnp.float32)
            gate = jnp.where(jnp.arange(nb) < cur, gate, -jnp.inf)
            _, sel = lax.top_k(gate, n_sel)
            valid = sel < cur
            kg = gather2(kbh, sel)
            vg = gather2(vbh, sel)
            qbh = qb.transpose(0, 2, 1, 3)
            s_sel = jnp.einsum('bhqd,bhqnkd->bhqnk', qbh, kg).astype(jnp.float32) * scale
            s_sel = jnp.where(valid[..., None], s_sel, -jnp.inf)
            s_sel = s_sel.reshape(B, MOBA_HEADS, MOBA_Q_BLOCK, n_sel * MOBA_BLOCK)
            p = jax.nn.softmax(jnp.concatenate([s_sel, s_own], axis=-1), axis=-1).astype(v.dtype)
            p_sel = p[..., :n_sel * MOBA_BLOCK].reshape(B, MOBA_HEADS, MOBA_Q_BLOCK, n_sel, MOBA_BLOCK)
            p_own = p[..., n_sel * MOBA_BLOCK:]
            o = (jnp.einsum('bhqnk,bhqnkd->bqhd', p_sel, vg)
                 + jnp.einsum('bhqk,bkhd->bqhd', p_own, own_v))
        else:
            p_own = jax.nn.softmax(s_own, axis=-1).astype(v.dtype)
            o = jnp.einsum('bhqk,bkhd->bqhd', p_own, own_v)
        return o.reshape(B, MOBA_Q_BLOCK, MOBA_W)

    out = lax.map(block, jnp.arange(n_qblocks))
    return out.transpose(1, 0, 2, 3).reshape(B, S, MOBA_W)


def mem_cross_attn(h, mem_n, w_q, w_kv, w_o):
    B, S, _ = h.shape
    M = mem_n.shape[1]
    q = (h @ w_q).reshape(B, S, MEM_HEADS, MEM_HEAD_DIM)
    kv = mem_n @ w_kv
    k = kv[..., :MEM_W].reshape(B, M, MEM_HEADS, MEM_HEAD_DIM)
    v = kv[..., MEM_W:].reshape(B, M, MEM_HEADS, MEM_HEAD_DIM)
    s = jnp.einsum('bshd,bmhd->bhsm', q, k).astype(jnp.float32) * (MEM_HEAD_DIM ** -0.5)
    p = jax.nn.softmax(s, axis=-1).astype(v.dtype)
    o = jnp.einsum('bhsm,bmhd->bshd', p, v).reshape(B, S, MEM_W)
    return o @ w_o


def setup_inputs(seed: int = 0) -> dict:
    key = jax.random.key(seed)
    ks = jax.random.split(key, 24)

    def dense(k, shape, fan_in):
        return jax.random.normal(k, shape, jnp.float32) * (fan_in ** -0.5)

    def gain(k, shape):
        return 1.0 + 0.02 * jax.random.normal(k, shape, jnp.float32)

    x = jax.random.normal(ks[0], (BATCH, SEQ, D_MODEL), jnp.float32)
    mem = jax.random.normal(ks[1], (BATCH, MEM_LEN, D_MODEL), jnp.float32)
    offsets = jax.random.randint(ks[2], (BATCH, 1), 0, 1024, jnp.int32)
    positions = (offsets + jnp.arange(SEQ, dtype=jnp.int32)[None, :]).astype(jnp.int32)
    L = DEPTH
    return {
        "x": x,
        "mem": mem,
        "positions": positions,
        "g_mix": gain(ks[3], (L, D_MODEL)),
        "w_in": dense(ks[4], (L, D_MODEL, IN_COLS), D_MODEL),
        "g_cq": gain(ks[5], (L, DSA_Q_RANK)),
        "w_uq": dense(ks[6], (L, DSA_Q_RANK, DSA_W), DSA_Q_RANK),
        "w_iq": dense(ks[7], (L, DSA_Q_RANK, IDX_HEADS * IDX_DIM), DSA_Q_RANK),
        "w_dsa_o": dense(ks[8], (L, DSA_W, D_MODEL), DSA_W),
        "w_moba_o": dense(ks[9], (L, MOBA_W, D_MODEL), MOBA_W),
        "w_out": dense(ks[10], (L, D_MODEL, D_MODEL), D_MODEL),
        "g_mem_q": gain(ks[11], (L, D_MODEL)),
        "g_mem_kv": gain(ks[12], (L, D_MODEL)),
        "w_mem_q": dense(ks[13], (L, D_MODEL, MEM_W), D_MODEL),
        "w_mem_kv": dense(ks[14], (L, D_MODEL, 2 * MEM_W), D_MODEL),
        "w_mem_o": dense(ks[15], (L, MEM_W, D_MODEL), MEM_W),
        "g_ff": gain(ks[16], (L, D_MODEL)),
        "w_ff1": dense(ks[17], (L, D_MODEL, D_FF), D_MODEL),
        "w_ff2": dense(ks[18], (L, D_FF, D_MODEL), D_FF),
        "g_final": gain(ks[19], (D_MODEL,)),
    }


def reference(x, mem, positions, g_mix, w_in, g_cq, w_uq, w_iq, w_dsa_o, w_moba_o, w_out,
              g_mem_q, g_mem_kv, w_mem_q, w_mem_kv, w_mem_o, g_ff, w_ff1, w_ff2, g_final):
    pts = _split_points()
    for l in range(DEPTH):
        h = rmsnorm(x, g_mix[l])
        proj = h @ w_in[l]
        c_q, k_a, v_a, k_idx, w_idx, q_b, k_b, v_b, gl_a, gl_b = jnp.split(proj, pts, axis=-1)
        y_a = dsa_branch(c_q, k_a, v_a, k_idx, w_idx, g_cq[l], w_uq[l], w_iq[l], positions) @ w_dsa_o[l]
        y_b = moba_branch(q_b, k_b, v_b, positions) @ w_moba_o[l]
        merged = jax.nn.sigmoid(gl_a) * y_a + jax.nn.sigmoid(gl_b) * y_b
        x = x + merged @ w_out[l]
        hm = rmsnorm(x, g_mem_q[l])
        mem_n = rmsnorm(mem, g_mem_kv[l])
        x = x + mem_cross_attn(hm, mem_n, w_mem_q[l], w_mem_kv[l], w_mem_o[l])
        hf = rmsnorm(x, g_ff[l])
        x = x + jnp.square(jax.nn.relu(hf @ w_ff1[l])) @ w_ff2[l]
    return rmsnorm(x, g_final)
```

```python
import math
import numpy as np
import ml_dtypes
import concourse.bass as bass
import concourse.mybir as mybir
from concourse.bass_utils import run_bass_kernel_spmd
from contextlib import ExitStack

F32 = mybir.dt.float32
F32R = mybir.dt.float32r
BF16 = mybir.dt.bfloat16
I32 = mybir.dt.int32
AF = mybir.ActivationFunctionType
ALU = mybir.AluOpType
AX = mybir.AxisListType
ENG = ["pe", "act", "dve", "pool", "sp"]

D = 2048
KC = 16
TB = 128
NEG = -1.0e30
MB = -30000.0
EPS = 1e-6
PI = math.pi


class Buf:
    def __init__(self, name, t=None):
        self.name = name
        self.t = t
        self.writer = None
        self.readers = {}
        self.dma_sem = None
        self.dma_cnt = 0

    def __getitem__(self, k):
        return self.t[k]


class Op:
    __slots__ = ("eng", "fn", "deps", "dmadeps", "signal", "count", "dma", "dmasem")

    def __init__(self, eng, fn):
        self.eng = eng
        self.fn = fn
        self.deps = set()
        self.dmadeps = {}
        self.signal = False
        self.count = 0
        self.dma = False
        self.dmasem = None


class Prog:
    def __init__(self, nc):
        self.nc = nc
        self.ops = {e: [] for e in ENG}
        self.es = ExitStack()
        self.semh = {}

    def sbuf(self, name, shape, dt=F32):
        return Buf(name, self.es.enter_context(self.nc.sbuf_tensor(name, list(shape), dt)))

    def psum(self, name, shape, dt=F32):
        return Buf(name, self.es.enter_context(self.nc.psum_tensor(name, list(shape), dt)))

    def view(self, name, ap):
        return Buf(name, ap)

    def _new_sem(self, name):
        h = self.es.enter_context(self.nc.semaphore(name))
        self.semh[name] = h
        return name

    def _dep(self, op, w):
        if w[0] == "dma":
            _, sem, val = w
            if op.dmadeps.get(sem, 0) < val:
                op.dmadeps[sem] = val
        else:
            if op.eng == "pe" and w[0] == "pe":
                return
            op.deps.add(w)

    def op(self, eng, fn, reads=(), writes=()):
        op = Op(eng, fn)
        me = (eng, len(self.ops[eng]))
        for b in reads:
            if b.writer is not None:
                self._dep(op, b.writer)
        for b in writes:
            if b.writer is not None:
                self._dep(op, b.writer)
            for r in b.readers.values():
                self._dep(op, r)
        for b in reads:
            b.readers[eng] = me
        for b in writes:
            b.writer = me
            b.readers = {}
        self.ops[eng].append(op)
        return op

    def dma(self, eng, out_ap, in_ap, reads=(), writes=(), sembuf=None):
        sb = sembuf if sembuf is not None else (writes[0] if writes else reads[0])
        if sb.dma_sem is None:
            sb.dma_sem = self._new_sem("d_" + sb.name)
        op = Op(eng, None)
        for b in reads:
            if b.writer is not None:
                self._dep(op, b.writer)
        for b in writes:
            if b.writer is not None and not (b.writer[0] == "dma" and not b.readers):
                self._dep(op, b.writer)
            for r in b.readers.values():
                self._dep(op, r)
        sb.dma_cnt += 16
        tok = ("dma", sb.dma_sem, sb.dma_cnt)
        for b in reads:
            b.readers[("dma", sb.dma_sem)] = tok
        for b in writes:
            b.writer = tok
            b.readers = {}
        op.dma = True
        op.dmasem = sb.dma_sem
        op.fn = lambda e, o=out_ap, i=in_ap: e.dma_start(out=o, in_=i)
        self.ops[eng].append(op)
        return op

    def wait_all_dma(self, eng, bufs):
        op = Op(eng, lambda e: e.nop())
        for b in bufs:
            if b.dma_sem is not None:
                op.dmadeps[b.dma_sem] = b.dma_cnt
        self.ops[eng].append(op)
        return op

    def alias(self, old, new):
        merged = {}
        for b in old:
            toks = list(b.readers.items())
            if b.writer is not None:
                w = b.writer
                toks.append(((("dma", w[1]) if w[0] == "dma" else w[0]), w))
            for k, t in toks:
                cur = merged.get(k)
                if cur is None or (t[0] == "dma" and t[2] > cur[2]) or (t[0] != "dma" and t[1] > cur[1]):
                    merged[k] = t
        for b in new:
            b.writer = None
            b.readers = dict(merged)

    def emit(self):
        nc = self.nc
        for e in ENG:
            for op in self.ops[e]:
                for (de, di) in op.deps:
                    self.ops[de][di].signal = True
        esem = {}
        for e in ENG:
            c = 0
            for op in self.ops[e]:
                if op.signal and not op.dma:
                    c += 1
                    op.count = c
            if c > 0:
                esem[e] = self._new_sem("s_" + e)
        engmap = {"pe": "tensor", "act": "scalar", "dve": "vector", "pool": "gpsimd", "sp": "sync"}
        prog = self

        def make(e):
            def body(eng):
                waited = {}
                for op in prog.ops[e]:
                    need = {}
                    for (de, di) in op.deps:
                        d = prog.ops[de][di]
                        s = esem[de]
                        if need.get(s, 0) < d.count:
                            need[s] = d.count
                    for s, v in op.dmadeps.items():
                        if need.get(s, 0) < v:
                            need[s] = v
                    for s, v in need.items():
                        if waited.get(s, 0) < v:
                            eng.wait_ge(prog.semh[s], v)
                            waited[s] = v
                    ins = op.fn(eng)
                    if op.dma:
                        ins.then_inc(prog.semh[op.dmasem], 16)
                    elif op.signal:
                        ins.then_inc(prog.semh[esem[e]], 1)
            return body

        with nc.Block() as block:
            for e in ENG:
                if self.ops[e]:
                    getattr(block, engmap[e])(make(e))


C_CQ, C_KA, C_VA, C_KI, C_WI, C_QB, C_KB, C_VB, C_GA, C_GB = 0, 512, 1536, 2560, 2624, 2640, 3664, 4688, 5712, 7760


def build(S, stage=99, dbg=()):
    NR = S // 512
    NG = S // 512
    NBLK = S // 256
    nc = bass.Bass("TRN2", target_bir_lowering=False)
    nc.dge_precook = False

    def din(name, shape, dt=F32):
        return nc.dram_tensor(name, list(shape), dt, kind="ExternalInput").ap()

    xall = din("xall", [S, D]); posall = din("posall", [128, S // 128], I32)
    xown = din("xown", [NR * TB, D]); posown = din("posown", [128, max(4, NR)], I32)
    tpos = din("tpos", [128, NR]); tcur = din("tcur", [128, NR])
    mem = din("mem", [256, D])
    w_in = din("w_in", [D, 9808], F32R); w_uq = din("w_uq", [512, 1024], F32R); w_iq = din("w_iq", [512, 1024], F32R)
    w_dsa_o = din("w_dsa_o", [1024, D], F32R); w_moba_o = din("w_moba_o", [1024, D], F32R)
    w_out = din("w_out", [D, D], F32R); w_mem_q = din("w_mem_q", [D, 512], F32R)
    w_mem_kv = din("w_mem_kv", [D, 1024], F32R); w_mem_o = din("w_mem_o", [512, D], F32R)
    w_ff1 = din("w_ff1", [D, 8192], F32R); w_ff2 = din("w_ff2", [8192, D], F32R)
    g_mix = din("g_mix", [1, D]); g_cq = din("g_cq", [1, 512]); g_mem_q = din("g_mem_q", [1, D])
    g_mem_kv = din("g_mem_kv", [1, D]); g_ff = din("g_ff", [1, D]); g_final = din("g_final", [1, D])
    c_ident = din("c_ident", [128, 128]); c_identr = din("c_identr", [128, 128], F32R); c_identb = din("c_identb", [128, 128], BF16)
    c_ones = din("c_ones", [128, 128], F32R); c_iotaw = din("c_iotaw", [128, 512]); c_iotan = din("c_iotan", [128, 32])
    c_invf = din("c_invf", [128, 16]); c_invfi = din("c_invfi", [128, 8]); c_pow2 = din("c_pow2", [128, 48])
    out = nc.dram_tensor("out", [NR * TB, D], F32, kind="ExternalOutput").ap()
    KaT = nc.dram_tensor("KaT", [8, 128, S], F32R, kind="Internal").ap()
    KbT = nc.dram_tensor("KbT", [8, 128, S], F32R, kind="Internal").ap()
    Va = nc.dram_tensor("Va", [S, 1024], F32R, kind="Internal").ap()
    Vb = nc.dram_tensor("Vb", [S, 1024], F32R, kind="Internal").ap()
    KiT = nc.dram_tensor("KiT", [64, S], F32R, kind="Internal").ap()
    x2s = nc.dram_tensor("x2s", [128, D], F32, kind="Internal").ap()

    P = Prog(nc)
    with P.es:
        Wt = [P.sbuf("W%d" % i, [128, KC, 512], F32R) for i in range(2)]
        wi = [0]

        def wslot():
            wi[0] ^= 1
            return Wt[wi[0]]

        XT = [P.sbuf("XT0", [128, D])]
        G = P.sbuf("G", [128, D]); Gq = G
        HTt = P.sbuf("HT", [128, KC, TB], F32R)
        HT = P.view("HTv", HTt.t)
        A1 = P.sbuf("A1", [128, 8192])
        A2 = P.sbuf("A2", [128, 4096])
        A3 = P.sbuf("A3", [128, 4096])
        ident = P.sbuf("ident", [128, 128]); identb = P.sbuf("identb", [128, 128], BF16)
        ones = P.sbuf("ones", [128, 128], F32R); iotaw = P.sbuf("iotaw", [128, 512]); iotan = P.sbuf("iotan", [128, 32])
        invf = P.sbuf("invf", [128, 16]); invfi = P.sbuf("invfi", [128, 8]); pow2 = P.sbuf("pow2", [128, 48])
        kmT = P.sbuf("kmT", [128, 8, 32], F32R)
        KMT = P.sbuf("KMT", [128, 4, 256], F32R); VM = P.sbuf("VM", [128, 2, 512], F32R)
        QAT = P.sbuf("QAT", [128, 8, TB], F32R); QIT = P.sbuf("QIT", [128, 8, TB], F32R)
        QBT = QIT; CQT = P.sbuf("CQT", [128, 4, TB], F32R)
        OAT = P.sbuf("OAT", [128, 8, TB], F32R); OBT = P.sbuf("OBT", [128, 8, TB], F32R)
        QMT = CQT; OMT = QAT
        PT = [P.sbuf("PT%d" % i, [128, TB], F32R) for i in range(2)]
        RB = [P.sbuf("RB%d" % i, [128, 512], F32R) for i in range(2)]
        KIc = [P.sbuf("KIc%d" % i, [128, 512], F32R) for i in range(1)]
        identr = P.sbuf("identr", [128, 128], F32R); LO = P.sbuf("LO", [128, 16]); HI = P.sbuf("HI", [128, 16])
        CB16 = P.sbuf("CB16", [128, 512], BF16)
        TM = [P.sbuf("TM%d" % i, [128, 512]) for i in range(3)]
        T64 = [P.sbuf("T64%d" % i, [128, 64]) for i in range(4)]
        GBT = P.sbuf("GBT", [32, 8, TB], BF16)
        GS = P.sbuf("GS", [128, 8, 32]); GB = P.sbuf("GB", [128, 8, 32]); NM = P.sbuf("NM", [128, 32]); EQ = P.sbuf("EQ", [128, 32])
        M8 = P.sbuf("M8", [128, 8, 8]); TH = P.sbuf("TH", [128, 8])
        sm = P.sbuf("sm", [128, 64])
        Wk = P.sbuf("Wk", [128, 48])
        posf = P.sbuf("posf", [128, 8]); posA = P.sbuf("posA", [128, S // 128], I32); posO = P.sbuf("posO", [128, max(4, NR)], I32)
        ang = P.sbuf("ang", [128, 4, 16]); kint = P.sbuf("kint", [128, 4, 16], I32); kf = P.sbuf("kf", [128, 4, 16]); mk = P.sbuf("mk", [128, 4, 16])
        cosA = P.sbuf("cosA", [128, 4, 16]); sinA = P.sbuf("sinA", [128, 4, 16])
        cosI = P.sbuf("cosI", [128, 4, 8]); sinI = P.sbuf("sinI", [128, 4, 8])
        WI = P.sbuf("WI", [128, 16]); AW = P.sbuf("AW", [128, 16]); SG = P.sbuf("SG", [128, 16])
        REC = P.sbuf("REC", [128, TB])
        tposs = P.sbuf("tposs", [128, NR]); tcurs = P.sbuf("tcurs", [128, NR])
        pa = [P.psum("pa%d" % i, [128, 512]) for i in range(2)]
        pt = [P.psum("pt%d" % i, [128, 512]) for i in range(2)]
        pL = [P.psum("pL%d" % i, [128, 512]) for i in range(2)]
        pO = P.psum("pO", [128, 512]); pR = P.psum("pR", [128, 512])
        cnt = {"pa": 0, "pt": 0, "pL": 0, "TM": 0, "PT": 0, "RB": 0, "KI": 0, "XT": 0, "cp": 0}

        def rot(lst, key):
            cnt[key] += 1
            return lst[cnt[key] % len(lst)]

        a1 = A1.t[:]
        a2 = A2.t[:]
        a3 = A3.t[:]
        HTG = [P.view("HTG%d" % i, a1[:, i * 2048:(i + 1) * 2048].bitcast(F32R).rearrange("p (k t) -> p k t", k=KC)) for i in range(4)]
        score = P.view("score", a1)
        scoreW = a1.bitcast(F32R)
        MG = P.view("MG", a1[:, 0:2048]); MGW = a1[:, 0:2048].bitcast(F32R); X1 = XT[0]
        AT = P.view("AT", a1[:, 4096:6144].bitcast(F32R).rearrange("p (k t) -> p k t", k=KC))
        AT2 = P.view("AT2", a1[:, 4096:8192].bitcast(F32R).rearrange("p (k t) -> p k t", k=KC))
        MG2 = P.view("MG2", a1[:, 2048:4096]); MG2W = a1[:, 2048:4096].bitcast(F32R)
        MHT = P.view("MHT", a1[:, 0:4096].bitcast(F32R).rearrange("p (k t) -> p k t", k=KC))
        KST = [P.view("KST%d" % i, a2[:, i * 2048:(i + 1) * 2048].bitcast(F32R).rearrange("p (h t) -> p h t", h=4)) for i in range(2)]
        VST = [P.view("VST%d" % i, a3[:, i * 2048:(i + 1) * 2048].bitcast(F32R).rearrange("p (i c) -> p i c", i=4)) for i in range(2)]
        biasM = P.view("biasM", a2[:, 0:4096].bitcast(BF16))
        X2e = P.view("X2e", a2[:, 0:2048])
        HT2 = P.view("HT2", a3[:, 0:4096].bitcast(F32R).rearrange("p (k t) -> p k t", k=KC))
        x2sV = P.view("x2sV", None); x2sS = P.view("x2sS", None)
        KBf = [P.view("KBf%d" % i, a3[:, i * 1024:(i + 1) * 1024].bitcast(F32R)) for i in range(2)]
        VBf = [P.view("VBf%d" % i, a3[:, 2048 + i * 1024:2048 + (i + 1) * 1024].bitcast(F32R).rearrange("p (j d) -> p j d", j=8)) for i in range(2)]

        def ld(dst, src_ap, dst_ap=None, eng="sp"):
            P.dma(eng, dst_ap if dst_ap is not None else dst[:], src_ap, writes=[dst])

        def copy(dst_ap, src_ap, reads, writes):
            cnt["cp"] += 1
            if cnt["cp"] % 2:
                P.op("act", lambda e: e.activation(out=dst_ap, in_=src_ap, func=AF.Copy), reads=reads, writes=writes)
            else:
                P.op("dve", lambda e: e.tensor_copy(out=dst_ap, in_=src_ap), reads=reads, writes=writes)

        def load_w(w_ap, k0, kc, c0, n):
            slot = wslot()
            src = w_ap[k0:k0 + kc * 128, c0:c0 + n].rearrange("(k p) n -> p k n", p=128)
            P.dma("sp", slot[:, 0:kc, 0:n], src, writes=[slot])
            return slot

        def load_g(dst, g_ap, n):
            P.dma("sp", dst[:, 0:n], g_ap[0, :].partition_broadcast(128), writes=[dst])

        def norm(Xb, x_ap, Gb, g_ap, F, out_ap, outb, final_row=None):
            scr = rot(TM, "TM")
            if F > 512:
                npc = F // 512
                for i in range(npc):
                    P.op("act", lambda e, i=i: e.activation(out=scr[:, 0:512], in_=x_ap[:, i * 512:(i + 1) * 512], func=AF.Square,
                                                            accum_out=sm[:, 8 + i:9 + i]), reads=[Xb], writes=[scr, sm])
                P.op("dve", lambda e: e.tensor_reduce(out=sm[:, 0:1], in_=sm[:, 8:8 + npc], axis=AX.X, op=ALU.add), reads=[sm], writes=[sm])
            else:
                P.op("act", lambda e: e.activation(out=scr[:, 0:F], in_=x_ap, func=AF.Square, accum_out=sm[:, 0:1]), reads=[Xb], writes=[scr, sm])
            P.op("dve", lambda e: e.tensor_scalar(out=sm[:, 1:2], in0=sm[:, 0:1], scalar1=1.0 / F, scalar2=EPS, op0=ALU.mult, op1=ALU.add), reads=[sm], writes=[sm])
            P.op("act", lambda e: e.activation(out=sm[:, 2:3], in_=sm[:, 1:2], func=AF.Sqrt), reads=[sm], writes=[sm])
            P.op("dve", lambda e: e.reciprocal(out=sm[:, 3:4], in_=sm[:, 2:3]), reads=[sm], writes=[sm])
            if final_row is not None:
                for i in range(4):
                    o = rot(TM, "TM")
                    P.op("dve", lambda e, o=o, i=i: e.scalar_tensor_tensor(out=o[:, :], in0=x_ap[:, i * 512:(i + 1) * 512], scalar=sm[:, 3:4], in1=Gb[:, i * 512:(i + 1) * 512],
                                                                          op0=ALU.mult, op1=ALU.mult), reads=[Xb, sm, Gb], writes=[o])
                    P.dma("pool", out[final_row * 128:(final_row + 1) * 128, i * 512:(i + 1) * 512], o[:, :], reads=[o], sembuf=o)
                return
            P.op("dve", lambda e: e.scalar_tensor_tensor(out=out_ap, in0=x_ap, scalar=sm[:, 3:4], in1=Gb[:, 0:F], op0=ALU.mult, op1=ALU.mult),
                 reads=[Xb, sm, Gb], writes=[outb])

        def transposes(srcb, src_ap, ncols, dstb, dst_fn, rows=128):
            nb = ncols // rows
            j = 0
            while j < nb:
                nj = min(4, nb - j)
                ps = rot(pt, "pt")
                for jj in range(nj):
                    P.op("pe", lambda e, jj=jj, j=j, ps=ps: e.transpose(out=ps[0:rows, jj * 128:(jj + 1) * 128],
                                                                        in_=src_ap[:, (j + jj) * rows:(j + jj + 1) * rows], identity=ident[:]),
                         reads=[srcb, ident], writes=[ps])
                copy(dst_fn(j, nj), ps[0:rows, 0:nj * 128].rearrange("p (j t) -> p j t", j=nj), [ps], [dstb])
                j += nj

        def mm(ps_ap, psb, lhs_fn, rhs_fn, kc, reads):
            for k in range(kc):
                P.op("pe", lambda e, k=k: e.matmul(ps_ap, lhsT=lhs_fn(k), rhs=rhs_fn(k), start=(k == 0), stop=(k == kc - 1)),
                     reads=reads, writes=[psb])

        def rope_tables(posb, pos_ap, n):
            if 'norope' in dbg:
                return
            if 'notables' in dbg:
                for tb_ in (cosA, sinA, cosI, sinI):
                    P.op("dve", lambda e, tb_=tb_: e.memset(tb_[:], 0.5), writes=[tb_])
                return
            P.op("dve", lambda e: e.tensor_copy(out=posf[:, 0:n], in_=pos_ap), reads=[posb], writes=[posf])
            for (inv, half, cs, sn) in ((invf, 16, cosA, sinA), (invfi, 8, cosI, sinI)):
                a = ang[:, 0:n, 0:half]; ki = kint[:, 0:n, 0:half]; kk = kf[:, 0:n, 0:half]; m_ = mk[:, 0:n, 0:half]
                for i_ in range(n):
                    P.op("dve", lambda e, i_=i_, inv=inv, half=half: e.tensor_scalar(out=ang[:, i_, 0:half], in0=inv[:, 0:half], scalar1=posf[:, i_:i_ + 1], scalar2=None, op0=ALU.mult),
                         reads=[posf, inv], writes=[ang])
                for (shift, dstt) in ((0.0, sn), (PI / 2, cs)):
                    P.op("dve", lambda e, a=a, ki=ki, shift=shift: e.tensor_scalar(out=ki, in0=a, scalar1=shift, scalar2=1.0 / (2 * PI), op0=ALU.add, op1=ALU.mult),
                         reads=[ang], writes=[kint])
                    P.op("dve", lambda e, ki=ki, kk=kk: e.tensor_copy(out=kk, in_=ki), reads=[kint], writes=[kf])
                    P.op("dve", lambda e, a=a, kk=kk: e.scalar_tensor_tensor(out=kk, in0=kk, scalar=-2 * PI, in1=a, op0=ALU.mult, op1=ALU.add),
                         reads=[kf, ang], writes=[kf])
                    if shift != 0.0:
                        P.op("dve", lambda e, kk=kk, shift=shift: e.tensor_scalar(out=kk, in0=kk, scalar1=shift, scalar2=None, op0=ALU.add), reads=[kf], writes=[kf])
                    P.op("dve", lambda e, kk=kk, m_=m_: e.tensor_scalar(out=m_, in0=kk, scalar1=PI, scalar2=-2 * PI, op0=ALU.is_gt, op1=ALU.mult), reads=[kf], writes=[mk])
                    P.op("dve", lambda e, kk=kk, m_=m_: e.tensor_tensor(out=kk, in0=kk, in1=m_, op=ALU.add), reads=[kf, mk], writes=[kf])
                    P.op("dve", lambda e, kk=kk, m_=m_: e.tensor_scalar(out=m_, in0=kk, scalar1=-PI, scalar2=2 * PI, op0=ALU.is_lt, op1=ALU.mult), reads=[kf], writes=[mk])
                    P.op("dve", lambda e, kk=kk, m_=m_: e.tensor_tensor(out=kk, in0=kk, in1=m_, op=ALU.add), reads=[kf, mk], writes=[kf])
                    P.op("dve", lambda e, kk=kk: e.tensor_scalar(out=kk, in0=kk, scalar1=-3.14159, scalar2=3.14159, op0=ALU.max, op1=ALU.min), reads=[kf], writes=[kf])
                    P.op("act", lambda e, kk=kk, dstt=dstt, half=half: e.activation(out=dstt[:, 0:n, 0:half], in_=kk, func=AF.Sin), reads=[kf], writes=[dstt])

        def rope(psb, ps_ap, nh, hd, half, cs_ap, sn_ap, csb, snb, dstb, dst_ap):
            P.op("act", lambda e: e.activation(out=dst_ap, in_=ps_ap, func=AF.Copy), reads=[psb], writes=[dstb])
            if 'norope' in dbg or 'noapply' in dbg:
                return
            p3 = ps_ap.rearrange("p (h d) -> p h d", h=nh)
            d3 = dst_ap.rearrange("p (h d) -> p h d", h=nh)
            x1 = p3[:, :, 0:half]; x2 = p3[:, :, half:2 * half]
            C = cs_ap.unsqueeze(1).broadcast_to([128, nh, half]); Sn = sn_ap.unsqueeze(1).broadcast_to([128, nh, half])
            t = [T64[i][:, 0:nh * half].rearrange("p (h d) -> p h d", h=nh) for i in range(4)]
            P.op("dve", lambda e: e.tensor_tensor(out=t[0], in0=x1, in1=C, op=ALU.mult), reads=[psb, csb, dstb], writes=[T64[0]])
            P.op("dve", lambda e: e.tensor_tensor(out=t[1], in0=x2, in1=Sn, op=ALU.mult), reads=[psb, snb, dstb], writes=[T64[1]])
            P.op("dve", lambda e: e.tensor_tensor(out=t[2], in0=x2, in1=C, op=ALU.mult), reads=[psb, csb, dstb], writes=[T64[2]])
            P.op("dve", lambda e: e.tensor_tensor(out=t[3], in0=x1, in1=Sn, op=ALU.mult), reads=[psb, snb, dstb], writes=[T64[3]])
            if 'apply4' in dbg:
                return
            P.op("dve", lambda e: e.tensor_tensor(out=d3[:, :, 0:half], in0=t[0], in1=t[1], op=ALU.subtract), reads=[T64[0], T64[1]], writes=[dstb])
            P.op("dve", lambda e: e.tensor_tensor(out=d3[:, :, half:2 * half], in0=t[2], in1=t[3], op=ALU.add), reads=[T64[2], T64[3]], writes=[dstb])

        for (b_, a_) in ((ident, c_ident), (identr, c_identr), (identb, c_identb), (ones, c_ones), (iotaw, c_iotaw), (iotan, c_iotan), (invf, c_invf),
                         (invfi, c_invfi), (pow2, c_pow2), (tposs, tpos), (tcurs, tcur), (posA, posall), (posO, posown)):
            ld(b_, a_)

        kv_chunks = [("ka", C_KA, 0), ("ka", C_KA + 512, 4), ("va", C_VA, 0), ("va", C_VA + 512, 512),
                     ("kb", C_KB, 0), ("kb", C_KB + 512, 4), ("vb", C_VB, 0), ("vb", C_VB + 512, 512)]
        stbufs = []
        kisS = P.view("kisS", None)
        load_g(G, g_mix, D)
        for g in range(NG if (stage >= 2 and 'nophaseA' not in dbg) else 0):
            s0 = g * 512
            rope_tables(posA, posA[:, g * 4:(g + 1) * 4], 4)
            for i in range(4):
                X = rot(XT, "XT")
                ld(X, xall[s0 + i * 128:s0 + (i + 1) * 128, :])
                norm(X, X[:], G, g_mix, D, X[:], X)
                transposes(X, X[:], D, HTG[i], lambda j, nj, i=i: HTG[i][:, j:j + nj, :])
            for (kind, c0, aux) in kv_chunks:
                slot = load_w(w_in, 0, KC, c0, 512)
                if kind in ("ka", "kb"):
                    st = rot(KST, "cp")
                else:
                    st = rot(VST, "cp")
                for i in range(4):
                    ps = rot(pa, "pa")
                    mm(ps[:, :], ps, lambda k, i=i: HTG[i][:, k, :], lambda k, slot=slot: slot[:, k, :], KC, [HTG[i], slot])
                    if kind in ("ka", "kb"):
                        tm = rot(TM, "TM")
                        rope(ps, ps[:, :], 4, 128, 16, cosA[:, i, :], sinA[:, i, :], cosA, sinA, tm, tm[:, :])
                        transposes(tm, tm[:, :], 512, st, lambda j, nj, st=st, i=i: st[:, j:j + nj, i * 128:(i + 1) * 128])
                    else:
                        copy(st[:, i, :], ps[:, :], [ps], [st])
                if kind in ("ka", "kb"):
                    dst = KaT if kind == "ka" else KbT
                    P.dma("pool", dst[aux:aux + 4, :, s0:s0 + 512].rearrange("h d s -> d h s"), st[:, :, :], reads=[st], sembuf=st)
                    if kind == "kb":
                        P.op("dve", lambda e, st=st: e.tensor_reduce(out=sm[:, 32:40].rearrange("p (h b) -> p h b", h=4),
                                                                     in_=st[:, :, :].bitcast(F32).rearrange("p h (b t) -> p h b t", b=2), axis=AX.X, op=ALU.add),
                             reads=[st], writes=[sm])
                        P.op("dve", lambda e, aux=aux, g=g: e.tensor_scalar(out=kmT[:, aux:aux + 4, 2 * g:2 * g + 2], in0=sm[:, 32:40].rearrange("p (h b) -> p h b", h=4),
                                                                            scalar1=1.0 / 256, scalar2=None, op0=ALU.mult), reads=[sm], writes=[kmT])
                else:
                    dst = Va if kind == "va" else Vb
                    P.dma("pool", dst[s0:s0 + 512, aux:aux + 512].rearrange("(i p) c -> p i c", p=128), st[:, :, :], reads=[st], sembuf=st)
                if st not in stbufs:
                    stbufs.append(st)
            slot = load_w(w_in, 0, KC, C_KI, 64)
            kis = rot(KIc, "KI")
            for i in range(4):
                ps = rot(pa, "pa")
                mm(ps[:, 0:64], ps, lambda k, i=i: HTG[i][:, k, :], lambda k, slot=slot: slot[:, k, 0:64], KC, [HTG[i], slot])
                tm = rot(TM, "TM")
                rope(ps, ps[:, 0:64], 1, 64, 8, cosI[:, i, :], sinI[:, i, :], cosI, sinI, tm, tm[:, 0:64])
                psq = rot(pt, "pt")
                P.op("pe", lambda e, tm=tm, psq=psq: e.transpose(out=psq[0:64, 0:128], in_=tm[:, 0:64], identity=ident[:]), reads=[tm, ident], writes=[psq])
                copy(kis[0:64, i * 128:(i + 1) * 128], psq[0:64, 0:128], [psq], [kis])
            P.dma("pool", KiT[:, s0:s0 + 512], kis[0:64, :], reads=[kis], sembuf=kisS)
            if kisS not in stbufs:
                stbufs.append(kisS)
        if stage >= 2:
            P.wait_all_dma("sp", stbufs)

        P.alias(HTG, [MHT])
        load_g(G, g_mem_kv, D)
        for mt in range(2):
            X = rot(XT, "XT")
            ld(X, mem[mt * 128:(mt + 1) * 128, :])
            norm(X, X[:], G, g_mem_kv, D, X[:], X)
            transposes(X, X[:], D, MHT, lambda j, nj, mt=mt: MHT[:, j:j + nj, mt * 128:(mt + 1) * 128])
        for c in range(2):
            slot = load_w(w_mem_kv, 0, KC, c * 512, 512)
            for mt in range(2):
                ps = rot(pa, "pa")
                mm(ps[:, :], ps, lambda k, mt=mt: MHT[:, k, mt * 128:(mt + 1) * 128], lambda k, slot=slot: slot[:, k, :], KC, [MHT, slot])
                if c == 0:
                    tm = rot(TM, "TM")
                    copy(tm[:, :], ps[:, :], [ps], [tm])
                    transposes(tm, tm[:, :], 512, KMT, lambda j, nj, mt=mt: KMT[:, j:j + nj, mt * 128:(mt + 1) * 128])
                else:
                    copy(VM[:, mt, :], ps[:, :], [ps], [VM])
        P.alias([MHT] + KST + VST, [score, MG, AT, biasM] + KBf + VBf)

        sc_att = 1.0 / math.sqrt(128.0)

        def attention(QT, nheads, nchunks, kload, bias_fn, OT):
            for h in range(nheads):
                for j in range(nchunks):
                    kap, vap, kb_, vb_ = kload(h, j)
                    L = rot(pL, "pL")
                    extra = bias_fn(h, j)
                    P.op("pe", lambda e, kap=kap, L=L, h=h, extra=extra: e.matmul(L[:, 0:TB], lhsT=kap, rhs=QT[:, h, :], start=True, stop=(len(extra) == 0)),
                         reads=[kb_, QT], writes=[L])
                    for xi, (lh, rh, rb) in enumerate(extra):
                        P.op("pe", lambda e, lh=lh, rh=rh, L=L, xi=xi, extra=extra: e.matmul(L[:, 0:TB], lhsT=lh, rhs=rh, start=False, stop=(xi == len(extra) - 1)),
                             reads=rb, writes=[L])
                    p_ = rot(PT, "PT")
                    P.op("act", lambda e, p_=p_, L=L: e.activation(out=p_[:, :], in_=L[:, 0:TB], func=AF.Exp, scale=sc_att), reads=[L], writes=[p_])
                    P.op("pe", lambda e, vap=vap, p_=p_, j=j: e.matmul(pO[:, 0:TB], lhsT=vap, rhs=p_[:, :], start=(j == 0), stop=(j == nchunks - 1)),
                         reads=[vb_, p_], writes=[pO])
                    P.op("pe", lambda e, p_=p_, j=j: e.matmul(pR[:, 0:TB], lhsT=ones[:, :], rhs=p_[:, :], start=(j == 0), stop=(j == nchunks - 1)),
                         reads=[ones, p_], writes=[pR])
                P.op("dve", lambda e: e.reciprocal(out=REC[:, :], in_=pR[:, 0:TB]), reads=[pR], writes=[REC])
                P.op("dve", lambda e, h=h: e.tensor_tensor(out=OT[:, h, :], in0=pO[:, 0:TB], in1=REC[:, :], op=ALU.mult), reads=[pO, REC], writes=[OT])

        for r in range(NR):
            EXT = 512 * (r + 1)
            NCH = EXT // 512
            X = rot(XT, "XT")
            ld(X, xown[r * 128:(r + 1) * 128, :])
            if stage >= 2 and 'norounds2' not in dbg:
                load_g(G, g_mix, D)
                norm(X, X[:], G, g_mix, D, X[:], X)
                transposes(X, X[:], D, HT, lambda j, nj: HT[:, j:j + nj, :])
                if r % 4 == 0:
                    rope_tables(posO, posO[:, r:r + 4], 4)
                rc = r % 4
                load_g(Gq, g_cq, 512)
                slot = load_w(w_in, 0, KC, C_CQ, 512)
                ps = rot(pa, "pa")
                mm(ps[:, :], ps, lambda k: HT[:, k, :], lambda k, slot=slot: slot[:, k, :], KC, [HT, slot])
                tm = rot(TM, "TM")
                copy(tm[:, :], ps[:, :], [ps], [tm])
                norm(tm, tm[:, :], Gq, g_cq, 512, tm[:, :], tm)
                transposes(tm, tm[:, :], 512, CQT, lambda j, nj: CQT[:, j:j + nj, :])
                for c in range(2):
                    slot = load_w(w_uq, 0, 4, c * 512, 512)
                    ps = rot(pa, "pa")
                    mm(ps[:, :], ps, lambda k: CQT[:, k, :], lambda k, slot=slot: slot[:, k, :], 4, [CQT, slot])
                    tm = rot(TM, "TM")
                    rope(ps, ps[:, :], 4, 128, 16, cosA[:, rc, :], sinA[:, rc, :], cosA, sinA, tm, tm[:, :])
                    transposes(tm, tm[:, :], 512, QAT, lambda j, nj, c=c: QAT[:, 4 * c + j:4 * c + j + nj, :])
            if stage >= 2.2:
                slot = load_w(w_in, 0, KC, C_WI, 16)
                ps = rot(pa, "pa")
                mm(ps[:, 0:16], ps, lambda k: HT[:, k, :], lambda k, slot=slot: slot[:, k, 0:16], KC, [HT, slot])
                P.op("dve", lambda e, ps=ps: e.tensor_scalar(out=WI[:, :], in0=ps[:, 0:16], scalar1=1.0 / 32.0, scalar2=None, op0=ALU.mult), reads=[ps], writes=[WI])
                P.op("dve", lambda e: e.tensor_scalar(out=SG[:, :], in0=WI[:, :], scalar1=0.0, scalar2=2.0, op0=ALU.is_ge, op1=ALU.mult), reads=[WI], writes=[SG])
                P.op("dve", lambda e: e.tensor_scalar(out=SG[:, :], in0=SG[:, :], scalar1=-1.0, scalar2=None, op0=ALU.add), reads=[SG], writes=[SG])
                P.op("dve", lambda e: e.tensor_tensor(out=AW[:, :], in0=WI[:, :], in1=SG[:, :], op=ALU.mult), reads=[WI, SG], writes=[AW])
                P.op("dve", lambda e: e.tensor_scalar(out=LO[:, :], in0=SG[:, :], scalar1=-1.0, scalar2=0.5e30, op0=ALU.add, op1=ALU.mult), reads=[SG], writes=[LO])
                P.op("dve", lambda e: e.tensor_scalar(out=HI[:, :], in0=SG[:, :], scalar1=1.0, scalar2=0.5e30, op0=ALU.add, op1=ALU.mult), reads=[SG], writes=[HI])
                for c in range(2):
                    slot = load_w(w_iq, 0, 4, c * 512, 512)
                    ps = rot(pa, "pa")
                    mm(ps[:, :], ps, lambda k: CQT[:, k, :], lambda k, slot=slot: slot[:, k, :], 4, [CQT, slot])
                    tm = rot(TM, "TM")
                    rope(ps, ps[:, :], 8, 64, 8, cosI[:, rc, :], sinI[:, rc, :], cosI, sinI, tm, tm[:, :])
                    P.op("dve", lambda e, tm=tm, c=c: e.tensor_tensor(out=tm[:, :].rearrange("p (h d) -> p h d", h=8), in0=tm[:, :].rearrange("p (h d) -> p h d", h=8),
                                                                   in1=WI[:, 8 * c:8 * c + 8].unsqueeze(2).broadcast_to([128, 8, 64]), op=ALU.mult), reads=[tm, WI], writes=[tm])
                    transposes(tm, tm[:, :], 512, QIT, lambda j, nj, c=c: QIT[:, 4 * c + j:4 * c + j + nj, :])
                P.op("dve", lambda e, r=r: e.tensor_scalar(out=sm[:, 20:21], in0=tposs[:, r:r + 1], scalar1=-float(512 * r), scalar2=None, op0=ALU.add), reads=[tposs], writes=[sm])
                P.op("dve", lambda e: e.tensor_scalar(out=CB16[:, :], in0=iotaw[:, :], scalar1=sm[:, 20:21], scalar2=MB, op0=ALU.is_gt, op1=ALU.mult), reads=[iotaw, sm], writes=[CB16])
                for c in range(NCH):
                    kc_ = rot(KIc, "KI")
                    P.dma("sp", kc_[0:64, :], KiT[:, c * 512:(c + 1) * 512], writes=[kc_])
                    P.dma("sp", kc_[64:128, :], KiT[:, c * 512:(c + 1) * 512], writes=[kc_])
                    for h in range(16):
                        pr, hf = h // 2, h % 2
                        L = rot(pL, "pL")
                        P.op("pe", lambda e, L=L, pr=pr, hf=hf, kc_=kc_: e.matmul(L[:, :], lhsT=QIT[64 * hf:64 * hf + 64, pr, :], rhs=kc_[64 * hf:64 * hf + 64, :], start=True, stop=True),
                             reads=[QIT, kc_], writes=[L])
                        rb = rot(RB, "RB")
                        P.op("dve", lambda e, L=L, rb=rb, h=h: e.tensor_scalar(out=rb[:, :], in0=L[:, :], scalar1=LO[:, h:h + 1], scalar2=HI[:, h:h + 1], op0=ALU.max, op1=ALU.min),
                             reads=[L, LO, HI], writes=[rb])
                        P.op("pe", lambda e, rb=rb, h=h: e.matmul(pO[:, :], lhsT=identr[:, :], rhs=rb[:, :], start=(h == 0), stop=(h == 15)), reads=[identr, rb], writes=[pO])
                    if c == NCH - 1:
                        CB = rot(TM, "TM")
                        P.op("dve", lambda e, CB=CB: e.tensor_scalar(out=CB[:, :], in0=iotaw[:, :], scalar1=sm[:, 20:21], scalar2=NEG, op0=ALU.is_gt, op1=ALU.mult), reads=[iotaw, sm], writes=[CB])
                        P.op("dve", lambda e, c=c, CB=CB: e.tensor_tensor(out=scoreW[:, c * 512:(c + 1) * 512], in0=pO[:, :], in1=CB[:, :], op=ALU.add), reads=[pO, CB], writes=[score])
                    else:
                        P.op("act", lambda e, c=c: e.activation(out=scoreW[:, c * 512:(c + 1) * 512], in_=pO[:, :], func=AF.Copy), reads=[pO], writes=[score])
                NIT = 36 if r == 0 else 26
                if r == 0:
                    P.op("dve", lambda e: e.memset(sm[:, 21:22], -1.0e4), writes=[sm])
                else:
                    P.op("dve", lambda e: e.tensor_reduce(out=sm[:, 21:22], in_=score[:, 0:512], axis=AX.X, op=ALU.min), reads=[score], writes=[sm])
                P.op("dve", lambda e, EXT=EXT: e.tensor_reduce(out=sm[:, 22:23], in_=score[:, 0:EXT], axis=AX.X, op=ALU.max), reads=[score], writes=[sm])
                P.op("dve", lambda e: e.tensor_tensor(out=sm[:, 23:24], in0=sm[:, 22:23], in1=sm[:, 21:22], op=ALU.subtract), reads=[sm], writes=[sm])
                P.op("dve", lambda e: e.tensor_scalar(out=Wk[:, :], in0=pow2[:, :], scalar1=sm[:, 23:24], scalar2=None, op0=ALU.mult), reads=[pow2, sm], writes=[Wk])
                P.op("dve", lambda e: e.tensor_tensor(out=sm[:, 24:25], in0=sm[:, 21:22], in1=Wk[:, 0:1], op=ALU.add), reads=[sm, Wk], writes=[sm])
                jk = biasM
                for k in range(NIT):
                    P.op("dve", lambda e, EXT=EXT: e.tensor_scalar(out=jk[:, 0:EXT], in0=score[:, 0:EXT], scalar1=sm[:, 24:25], scalar2=None, op0=ALU.is_ge, op1=ALU.add,
                                                                   accum_out=sm[:, 25:26]), reads=[score, sm], writes=[jk, sm])
                    P.op("dve", lambda e, k=k: e.scalar_tensor_tensor(out=sm[:, 26:27], in0=sm[:, 25:26], scalar=256.0, in1=Wk[:, k:k + 1], op0=ALU.is_ge, op1=ALU.mult),
                         reads=[sm, Wk], writes=[sm])
                    P.op("dve", lambda e, k=k: e.scalar_tensor_tensor(out=sm[:, 24:25], in0=sm[:, 26:27], scalar=Wk[:, k + 1:k + 2], in1=sm[:, 24:25], op0=ALU.subtract, op1=ALU.add),
                         reads=[sm, Wk], writes=[sm])
                P.op("dve", lambda e, NIT=NIT: e.tensor_tensor(out=sm[:, 27:28], in0=sm[:, 24:25], in1=Wk[:, NIT:NIT + 1], op=ALU.subtract), reads=[sm, Wk], writes=[sm])
                P.op("dve", lambda e, EXT=EXT: e.tensor_scalar(out=biasM[:, 0:EXT], in0=score[:, 0:EXT], scalar1=sm[:, 27:28], scalar2=MB, op0=ALU.is_lt, op1=ALU.mult),
                     reads=[score, sm], writes=[biasM])

            if stage >= 2.4:
                def kload_a(h, j, EXT=EXT):
                    jj = j % 8
                    if jj == 0:
                        n = min(1024, EXT - j * 128)
                        kb_ = rot(KBf, "cp"); vb_ = rot(VBf, "cp")
                        kload_a.cur = (kb_, vb_)
                        P.dma("sp", kb_[:, 0:n], KaT[h, :, j * 128:j * 128 + n], writes=[kb_])
                        P.dma("sp", vb_[:, 0:n // 128, :], Va[j * 128:j * 128 + n, h * 128:(h + 1) * 128].rearrange("(j p) d -> p j d", p=128), writes=[vb_])
                    kb_, vb_ = kload_a.cur
                    return kb_[:, jj * 128:(jj + 1) * 128], vb_[:, jj, :], kb_, vb_

                attention(QAT, 8, EXT // 128, kload_a,
                          lambda h, j: [(biasM[:, j * 128:(j + 1) * 128], identb[:, :], [biasM, identb])], OAT)

            if stage >= 2.6:
                for c in range(2):
                    slot = load_w(w_in, 0, KC, C_QB + c * 512, 512)
                    ps = rot(pa, "pa")
                    mm(ps[:, :], ps, lambda k: HT[:, k, :], lambda k, slot=slot: slot[:, k, :], KC, [HT, slot])
                    tm = rot(TM, "TM")
                    rope(ps, ps[:, :], 4, 128, 16, cosA[:, rc, :], sinA[:, rc, :], cosA, sinA, tm, tm[:, :])
                    transposes(tm, tm[:, :], 512, QBT, lambda j, nj, c=c: QBT[:, 4 * c + j:4 * c + j + nj, :])
                for h in range(8):
                    P.op("pe", lambda e, h=h: e.matmul(pR[:, h * 32:h * 32 + NBLK], lhsT=QBT[:, h, :], rhs=kmT[:, h, 0:NBLK], start=True, stop=True), reads=[QBT, kmT], writes=[pR])
                P.op("dve", lambda e, r=r: e.tensor_scalar(out=NM[:, :], in0=iotan[:, :], scalar1=tcurs[:, r:r + 1], scalar2=NEG, op0=ALU.is_ge, op1=ALU.mult), reads=[iotan, tcurs], writes=[NM])
                P.op("dve", lambda e, r=r: e.tensor_scalar(out=EQ[:, :], in0=iotan[:, :], scalar1=tcurs[:, r:r + 1], scalar2=None, op0=ALU.not_equal), reads=[iotan, tcurs], writes=[EQ])
                P.op("dve", lambda e: e.memset(GS[:, :, :], NEG), writes=[GS])
                P.op("dve", lambda e: e.tensor_tensor(out=GS[:, :, 0:NBLK], in0=pR[:, 0:256].rearrange("p (h n) -> p h n", h=8)[:, :, 0:NBLK],
                                                      in1=NM[:, 0:NBLK].unsqueeze(1).broadcast_to([128, 8, NBLK]), op=ALU.add), reads=[pR, NM], writes=[GS])
                for h in range(8):
                    P.op("dve", lambda e, h=h: e.max(out=M8[:, h, :], in_=GS[:, h, :]), reads=[GS], writes=[M8])
                P.op("dve", lambda e: e.tensor_scalar(out=TH[:, :], in0=M8[:, :, 2], scalar1=-1.0e29, scalar2=None, op0=ALU.max), reads=[M8], writes=[TH])
                P.op("dve", lambda e: e.tensor_tensor(out=GB[:, :, :], in0=GS[:, :, :], in1=TH[:, :].unsqueeze(2).broadcast_to([128, 8, 32]), op=ALU.is_lt), reads=[GS, TH], writes=[GB])
                P.op("dve", lambda e: e.scalar_tensor_tensor(out=GB[:, :, :], in0=GB[:, :, :], scalar=MB, in1=EQ[:, :].unsqueeze(1).broadcast_to([128, 8, 32]), op0=ALU.mult, op1=ALU.mult),
                     reads=[GB, EQ], writes=[GB])
                for h in range(8):
                    psq = rot(pt, "pt")
                    P.op("pe", lambda e, h=h, psq=psq: e.transpose(out=psq[0:32, 0:128], in_=GB[:, h, :], identity=ident[:]), reads=[GB, ident], writes=[psq])
                    copy(GBT[:, h, :], psq[0:32, 0:128], [psq], [GBT])

                def kload_b(h, j, EXT=EXT):
                    jj = j % 8
                    if jj == 0:
                        n = min(1024, EXT - j * 128)
                        kb_ = rot(KBf, "cp"); vb_ = rot(VBf, "cp")
                        kload_b.cur = (kb_, vb_)
                        P.dma("sp", kb_[:, 0:n], KbT[h, :, j * 128:j * 128 + n], writes=[kb_])
                        P.dma("sp", vb_[:, 0:n // 128, :], Vb[j * 128:j * 128 + n, h * 128:(h + 1) * 128].rearrange("(j p) d -> p j d", p=128), writes=[vb_])
                    kb_, vb_ = kload_b.cur
                    return kb_[:, jj * 128:(jj + 1) * 128], vb_[:, jj, :], kb_, vb_

                def bias_b(h, j, r=r):
                    n = j // 2
                    ex = [(identb[0:32, n:n + 1].broadcast_to([32, 128]), GBT[:, h, :], [identb, GBT])]
                    if j >= 4 * r:
                        w = j - 4 * r
                        ex.append((CB16[:, w * 128:(w + 1) * 128], identb[:, :], [CB16, identb]))
                    return ex

                attention(QBT, 8, EXT // 128, kload_b, bias_b, OBT)

            if stage < 3:
                ld(X1, xown[r * 128:(r + 1) * 128, :])
            if stage >= 3:
                for c in range(4):
                    for bi, (OT_, wo, cg) in enumerate(((OAT, w_dsa_o, C_GA), (OBT, w_moba_o, C_GB))):
                        slot = load_w(wo, 0, 8, c * 512, 512)
                        ps = rot(pa, "pa")
                        mm(ps[:, :], ps, lambda k, OT_=OT_: OT_[:, k, :], lambda k, slot=slot: slot[:, k, :], 8, [OT_, slot])
                        y = rot(TM, "TM")
                        P.op("act", lambda e, y=y, ps=ps: e.activation(out=y[:, :], in_=ps[:, :], func=AF.Copy), reads=[ps], writes=[y])
                        slot2 = load_w(w_in, 0, KC, cg + c * 512, 512)
                        ps2 = rot(pa, "pa")
                        mm(ps2[:, :], ps2, lambda k: HT[:, k, :], lambda k, slot2=slot2: slot2[:, k, :], KC, [HT, slot2])
                        sg = rot(TM, "TM")
                        P.op("act", lambda e, sg=sg, ps2=ps2: e.activation(out=sg[:, :], in_=ps2[:, :], func=AF.Sigmoid), reads=[ps2], writes=[sg])
                        if bi == 0:
                            P.op("dve", lambda e, y=y, sg=sg, c=c: e.tensor_tensor(out=MGW[:, c * 512:(c + 1) * 512], in0=y[:, :], in1=sg[:, :], op=ALU.mult), reads=[y, sg], writes=[MG])
                        else:
                            P.op("dve", lambda e, y=y, sg=sg: e.tensor_tensor(out=y[:, :], in0=y[:, :], in1=sg[:, :], op=ALU.mult), reads=[y, sg], writes=[y])
                            P.op("dve", lambda e, y=y, c=c: e.tensor_tensor(out=MGW[:, c * 512:(c + 1) * 512], in0=MG[:, c * 512:(c + 1) * 512], in1=y[:, :], op=ALU.add), reads=[y, MG], writes=[MG])
                transposes(MG, MG[:, :], D, HT, lambda j, nj: HT[:, j:j + nj, :])
                ld(X1, xown[r * 128:(r + 1) * 128, :])
                for c in range(4):
                    slot = load_w(w_out, 0, KC, c * 512, 512)
                    ps = rot(pa, "pa")
                    mm(ps[:, :], ps, lambda k: HT[:, k, :], lambda k, slot=slot: slot[:, k, :], KC, [HT, slot])
                    P.op("dve", lambda e, ps=ps, c=c: e.tensor_tensor(out=X1[:, c * 512:(c + 1) * 512], in0=ps[:, :], in1=X1[:, c * 512:(c + 1) * 512], op=ALU.add), reads=[ps, X1], writes=[X1])

            load_g(G, g_mem_q, D)
            norm(X1, X1[:, :], G, g_mem_q, D, MGW, MG)
            transposes(MG, MG[:, :], D, HT, lambda j, nj: HT[:, j:j + nj, :])
            slot = load_w(w_mem_q, 0, KC, 0, 512)
            ps = rot(pa, "pa")
            mm(ps[:, :], ps, lambda k: HT[:, k, :], lambda k, slot=slot: slot[:, k, :], KC, [HT, slot])
            tm = rot(TM, "TM")
            copy(tm[:, :], ps[:, :], [ps], [tm])
            transposes(tm, tm[:, :], 512, QMT, lambda j, nj: QMT[:, j:j + nj, :])
            attention(QMT, 4, 2, lambda h, j: (KMT[:, h, j * 128:(j + 1) * 128], VM[:, j, h * 128:(h + 1) * 128], KMT, VM), lambda h, j: [], OMT)
            for c in range(4):
                slot = load_w(w_mem_o, 0, 4, c * 512, 512)
                ps = rot(pa, "pa")
                mm(ps[:, :], ps, lambda k: OMT[:, k, :], lambda k, slot=slot: slot[:, k, :], 4, [OMT, slot])
                P.op("dve", lambda e, ps=ps, c=c: e.tensor_tensor(out=X1[:, c * 512:(c + 1) * 512], in0=ps[:, :], in1=X1[:, c * 512:(c + 1) * 512], op=ALU.add), reads=[ps, X1], writes=[X1])

            pair = NR >= 2
            if pair and r % 2 == 0:
                P.dma("pool", x2s, X1[:, :], reads=[X1], writes=[x2sV], sembuf=x2sS)
                continue
            tiles = [(r, X1, MG, MGW)]
            if pair:
                P.alias([biasM], [X2e])
                P.dma("sp", X2e[:, :], x2s, reads=[x2sV], writes=[X2e])
                tiles = [(r - 1, X2e, MG, MGW), (r, X1, MG2, MG2W)]
            nt = len(tiles)
            P.alias(KBf + VBf, [HT2])
            load_g(G, g_ff, D)
            for m, (row, Xb, NB, NBW) in enumerate(tiles):
                norm(Xb, Xb[:, :], G, g_ff, D, NBW, NB)
                transposes(NB, NB[:, :], D, HT2, lambda j, nj, m=m: HT2[:, j:j + nj, m * 128:(m + 1) * 128])
            for qd in range(4):
                for j in range(4):
                    slot = load_w(w_ff1, 0, KC, qd * 2048 + j * 512, 512)
                    for fs in range(4):
                        ps = rot(pa, "pa")
                        mm(ps[:, 0:nt * 128], ps, lambda k, slot=slot, fs=fs: slot[:, k, fs * 128:(fs + 1) * 128], lambda k: HT2[:, k, 0:nt * 128], KC, [HT2, slot])
                        tm = rot(TM, "TM")
                        P.op("act", lambda e, tm=tm, ps=ps: e.activation(out=tm[:, 0:nt * 128], in_=ps[:, 0:nt * 128], func=AF.Relu), reads=[ps], writes=[tm])
                        P.op("dve", lambda e, tm=tm, j=j, fs=fs: e.tensor_tensor(out=AT2[:, j * 4 + fs, 0:nt * 128], in0=tm[:, 0:nt * 128], in1=tm[:, 0:nt * 128], op=ALU.mult), reads=[tm], writes=[AT2])
                for c in range(4):
                    slot = wslot()
                    P.dma("sp", slot[:, :, :], w_ff2[qd * 2048:(qd + 1) * 2048, c * 512:(c + 1) * 512].rearrange("(k p) n -> p k n", p=128), writes=[slot])
                    for m, (row, Xb, NB, NBW) in enumerate(tiles):
                        ps = rot(pa, "pa")
                        mm(ps[:, :], ps, lambda k, m=m: AT2[:, k, m * 128:(m + 1) * 128], lambda k, slot=slot: slot[:, k, :], KC, [AT2, slot])
                        P.op("dve", lambda e, ps=ps, c=c, Xb=Xb: e.tensor_tensor(out=Xb[:, c * 512:(c + 1) * 512], in0=ps[:, :], in1=Xb[:, c * 512:(c + 1) * 512], op=ALU.add), reads=[ps, Xb], writes=[Xb])

            load_g(G, g_final, D)
            for m, (row, Xb, NB, NBW) in enumerate(tiles):
                norm(Xb, Xb[:, :], G, g_final, D, None, None, final_row=row)
            P.alias([HT2], KBf + VBf)
            if pair:
                P.alias([X2e], [biasM])
        P.wait_all_dma("pool", TM)
        P.emit()
    return nc


def host_inputs(inputs, S):
    NR = S // 512
    f32 = np.float32
    consts = {
        "c_ident": np.eye(128, dtype=f32),
        "c_identr": np.eye(128, dtype=f32),
        "c_identb": np.eye(128, dtype=f32).astype(ml_dtypes.bfloat16),
        "c_ones": np.ones((128, 128), f32),
        "c_iotaw": np.tile(np.arange(512, dtype=f32)[None, :], (128, 1)),
        "c_iotan": np.tile(np.arange(32, dtype=f32)[None, :], (128, 1)),
        "c_invf": np.tile((np.float32(500000.0) ** (-np.arange(16, dtype=f32) * f32(2.0 / 32)))[None, :], (128, 1)).astype(f32),
        "c_invfi": np.tile((np.float32(500000.0) ** (-np.arange(8, dtype=f32) * f32(2.0 / 16)))[None, :], (128, 1)).astype(f32),
        "c_pow2": np.tile((0.5 ** np.arange(1, 49, dtype=np.float64)).astype(f32)[None, :], (128, 1)),
    }
    wnames = ["w_in", "w_uq", "w_iq", "w_dsa_o", "w_moba_o", "w_out", "w_mem_q", "w_mem_kv", "w_mem_o", "w_ff1", "w_ff2"]
    gnames = ["g_mix", "g_cq", "g_mem_q", "g_mem_kv", "g_ff"]
    shared = dict(consts)
    for n in wnames:
        shared[n] = np.ascontiguousarray(np.asarray(inputs[n], f32)[0])
    for n in gnames:
        shared[n] = np.ascontiguousarray(np.asarray(inputs[n], f32)[0][None, :])
    shared["g_final"] = np.ascontiguousarray(np.asarray(inputs["g_final"], f32)[None, :])
    x = np.asarray(inputs["x"], f32)
    pos = np.asarray(inputs["positions"], np.int32)
    memv = np.asarray(inputs["mem"], f32)
    maps, owners = [], []
    for c in range(8):
        b, q = c // 4, c % 4
        blks = [4 * r + (q if r % 2 == 0 else 3 - q) for r in range(NR)]
        rows = np.concatenate([np.arange(bk * 128, (bk + 1) * 128) for bk in blks])
        m = dict(shared)
        m["xall"] = np.ascontiguousarray(x[b])
        m["posall"] = np.ascontiguousarray(pos[b].reshape(S // 128, 128).T)
        m["xown"] = np.ascontiguousarray(x[b][rows])
        po = np.zeros((128, max(4, NR)), np.int32)
        po[:, :NR] = pos[b][rows].reshape(NR, 128).T
        m["posown"] = po
        m["tpos"] = np.ascontiguousarray(rows.reshape(NR, 128).T.astype(f32))
        m["tcur"] = np.ascontiguousarray((rows // 256).reshape(NR, 128).T.astype(f32))
        m["mem"] = np.ascontiguousarray(memv[b])
        maps.append(m)
        owners.append((b, rows))
    return maps, owners


_NC_CACHE = {}


def kernel(**inputs):
    S = int(np.asarray(inputs["x"]).shape[1])
    if S not in _NC_CACHE:
        _NC_CACHE[S] = build(S)
    nc = _NC_CACHE[S]
    maps, owners = host_inputs(inputs, S)
    res = run_bass_kernel_spmd(nc, maps, core_ids=list(range(8)))
    outp = np.zeros((2, S, D), np.float32)
    for c, (b, rows) in enumerate(owners):
        outp[b, rows] = np.asarray(res.results[c]["out"], np.float32)
    return outp
```

```python
import math
import numpy as np
import ml_dtypes
import concourse.bass as bass
import concourse.mybir as mybir
from concourse.bass_utils import run_bass_kernel_spmd
from contextlib import ExitStack

F32 = mybir.dt.float32
F32R = mybir.dt.float32r
BF16 = mybir.dt.bfloat16
I32 = mybir.dt.int32
AF = mybir.ActivationFunctionType
ALU = mybir.AluOpType
AX = mybir.AxisListType
ENG = ["pe", "act", "dve", "pool", "sp"]

D = 2048
KC = 16
TB = 128
NEG = -1.0e30
MB = -30000.0
ABATCH = 4
EPS = 1e-6
PI = math.pi


class Buf:
    def __init__(self, name, t=None):
        self.name = name
        self.t = t
        self.writer = None
        self.readers = {}
        self.dma_sem = None
        self.dma_cnt = 0

    def __getitem__(self, k):
        return self.t[k]


class Op:
    __slots__ = ("eng", "fn", "deps", "dmadeps", "signal", "count", "dma", "dmasem")

    def __init__(self, eng, fn):
        self.eng = eng
        self.fn = fn
        self.deps = set()
        self.dmadeps = {}
        self.signal = False
        self.count = 0
        self.dma = False
        self.dmasem = None


class Prog:
    def __init__(self, nc):
        self.nc = nc
        self.ops = {e: [] for e in ENG}
        self.es = ExitStack()
        self.semh = {}

    def sbuf(self, name, shape, dt=F32):
        return Buf(name, self.es.enter_context(self.nc.sbuf_tensor(name, list(shape), dt)))

    def psum(self, name, shape, dt=F32):
        return Buf(name, self.es.enter_context(self.nc.psum_tensor(name, list(shape), dt)))

    def view(self, name, ap):
        return Buf(name, ap)

    def _new_sem(self, name):
        h = self.es.enter_context(self.nc.semaphore(name))
        self.semh[name] = h
        return name

    def _dep(self, op, w):
        if w[0] == "dma":
            _, sem, val = w
            if op.dmadeps.get(sem, 0) < val:
                op.dmadeps[sem] = val
        else:
            if op.eng == "pe" and w[0] == "pe":
                return
            op.deps.add(w)

    def op(self, eng, fn, reads=(), writes=()):
        op = Op(eng, fn)
        me = (eng, len(self.ops[eng]))
        for b in reads:
            if b.writer is not None:
                self._dep(op, b.writer)
        for b in writes:
            if b.writer is not None:
                self._dep(op, b.writer)
            for r in b.readers.values():
                self._dep(op, r)
        for b in reads:
            b.readers[eng] = me
        for b in writes:
            b.writer = me
            b.readers = {}
        self.ops[eng].append(op)
        return op

    def dma(self, eng, out_ap, in_ap, reads=(), writes=(), sembuf=None):
        sb = sembuf if sembuf is not None else (writes[0] if writes else reads[0])
        if sb.dma_sem is None:
            sb.dma_sem = self._new_sem("d_" + sb.name)
        op = Op(eng, None)
        for b in reads:
            if b.writer is not None:
                self._dep(op, b.writer)
        for b in writes:
            if b.writer is not None and not (b.writer[0] == "dma" and not b.readers):
                self._dep(op, b.writer)
            for r in b.readers.values():
                self._dep(op, r)
        sb.dma_cnt += 16
        tok = ("dma", sb.dma_sem, sb.dma_cnt)
        for b in reads:
            b.readers[("dma", sb.dma_sem)] = tok
        for b in writes:
            b.writer = tok
            b.readers = {}
        op.dma = True
        op.dmasem = sb.dma_sem
        op.fn = lambda e, o=out_ap, i=in_ap: e.dma_start(out=o, in_=i)
        self.ops[eng].append(op)
        return op

    def wait_all_dma(self, eng, bufs):
        op = Op(eng, lambda e: e.nop())
        for b in bufs:
            if b.dma_sem is not None:
                op.dmadeps[b.dma_sem] = b.dma_cnt
        self.ops[eng].append(op)
        return op

    def alias(self, old, new):
        merged = {}
        for b in old:
            toks = list(b.readers.items())
            if b.writer is not None:
                w = b.writer
                toks.append(((("dma", w[1]) if w[0] == "dma" else w[0]), w))
            for k, t in toks:
                cur = merged.get(k)
                if cur is None or (t[0] == "dma" and t[2] > cur[2]) or (t[0] != "dma" and t[1] > cur[1]):
                    merged[k] = t
        for b in new:
            b.writer = None
            b.readers = dict(merged)

    def emit(self):
        nc = self.nc
        for e in ENG:
            for op in self.ops[e]:
                for (de, di) in op.deps:
                    self.ops[de][di].signal = True
        esem = {}
        for e in ENG:
            c = 0
            for op in self.ops[e]:
                if op.signal and not op.dma:
                    c += 1
                    op.count = c
            if c > 0:
                esem[e] = self._new_sem("s_" + e)
        engmap = {"pe": "tensor", "act": "scalar", "dve": "vector", "pool": "gpsimd", "sp": "sync"}
        prog = self

        def make(e):
            def body(eng):
                waited = {}
                for op in prog.ops[e]:
                    need = {}
                    for (de, di) in op.deps:
                        d = prog.ops[de][di]
                        s = esem[de]
                        if need.get(s, 0) < d.count:
                            need[s] = d.count
                    for s, v in op.dmadeps.items():
                        if need.get(s, 0) < v:
                            need[s] = v
                    for s, v in need.items():
                        if waited.get(s, 0) < v:
                            eng.wait_ge(prog.semh[s], v)
                            waited[s] = v
                    ins = op.fn(eng)
                    if op.dma:
                        ins.then_inc(prog.semh[op.dmasem], 16)
                    elif op.signal:
                        ins.then_inc(prog.semh[esem[e]], 1)
            return body

        with nc.Block() as block:
            for e in ENG:
                if self.ops[e]:
                    getattr(block, engmap[e])(make(e))


C_CQ, C_KA, C_VA, C_KI, C_WI, C_QB, C_KB, C_VB, C_GA, C_GB = 0, 512, 1536, 2560, 2624, 2640, 3664, 4688, 5712, 7760


def build(S, stage=99, dbg=()):
    NR = S // 512
    NG = S // 512
    NBLK = S // 256
    nc = bass.Bass("TRN2", target_bir_lowering=False)
    nc.dge_precook = False

    def din(name, shape, dt=F32):
        return nc.dram_tensor(name, list(shape), dt, kind="ExternalInput").ap()

    xall = din("xall", [S, D]); posall = din("posall", [128, S // 128], I32)
    xown = din("xown", [NR * TB, D]); posown = din("posown", [128, max(4, NR)], I32)
    tpos = din("tpos", [128, NR]); tcur = din("tcur", [128, NR])
    mem = din("mem", [256, D])
    w_in = din("w_in", [D, 9808], F32R); w_uq = din("w_uq", [512, 1024], F32R); w_iq = din("w_iq", [512, 1024], F32R)
    w_dsa_o = din("w_dsa_o", [1024, D], F32R); w_moba_o = din("w_moba_o", [1024, D], F32R)
    w_out = din("w_out", [D, D], F32R); w_mem_q = din("w_mem_q", [D, 512], F32R)
    w_mem_kv = din("w_mem_kv", [D, 1024], F32R); w_mem_o = din("w_mem_o", [512, D], F32R)
    w_ff1 = din("w_ff1", [D, 8192], F32R); w_ff2 = din("w_ff2", [8192, D], F32R)
    g_mix = din("g_mix", [1, D]); g_cq = din("g_cq", [1, 512]); g_mem_q = din("g_mem_q", [1, D])
    g_mem_kv = din("g_mem_kv", [1, D]); g_ff = din("g_ff", [1, D]); g_final = din("g_final", [1, D])
    c_ident = din("c_ident", [128, 128]); c_identr = din("c_identr", [128, 128], F32R); c_identb = din("c_identb", [128, 128], BF16)
    c_ones = din("c_ones", [128, 128], F32R); c_iotaw = din("c_iotaw", [128, 512]); c_iotan = din("c_iotan", [128, 32])
    c_invf = din("c_invf", [128, 16]); c_invfi = din("c_invfi", [128, 8]); c_pow2 = din("c_pow2", [128, 48])
    out = nc.dram_tensor("out", [NR * TB, D], F32, kind="ExternalOutput").ap()
    KaT = nc.dram_tensor("KaT", [8, 128, S], F32R, kind="Internal").ap()
    KbT = nc.dram_tensor("KbT", [8, 128, S], F32R, kind="Internal").ap()
    Va = nc.dram_tensor("Va", [S, 1024], F32R, kind="Internal").ap()
    Vb = nc.dram_tensor("Vb", [S, 1024], F32R, kind="Internal").ap()
    KiT = nc.dram_tensor("KiT", [64, S], F32R, kind="Internal").ap()
    x2s = nc.dram_tensor("x2s", [128, D], F32, kind="Internal").ap()

    P = Prog(nc)
    with P.es:
        Wt = [P.sbuf("W%d" % i, [128, KC, 512], F32R) for i in range(2)]
        wi = [0]

        def wslot():
            wi[0] ^= 1
            return Wt[wi[0]]

        XT = [P.sbuf("XT0", [128, D])]
        G = P.sbuf("G", [128, D]); Gq = G
        HTt = P.sbuf("HT", [128, KC, TB], F32R)
        HT = P.view("HTv", HTt.t)
        A1 = P.sbuf("A1", [128, 8192])
        A2 = P.sbuf("A2", [128, 4096])
        A3 = P.sbuf("A3", [128, 4096])
        ident = P.sbuf("ident", [128, 128]); identb = P.sbuf("identb", [128, 128], BF16)
        ones = P.sbuf("ones", [128, 128], F32R); iotaw = P.sbuf("iotaw", [128, 512]); iotan = P.sbuf("iotan", [128, 32])
        invf = P.sbuf("invf", [128, 16]); invfi = P.sbuf("invfi", [128, 8]); pow2 = P.sbuf("pow2", [128, 48])
        kmT = P.sbuf("kmT", [128, 8, 32], F32R)
        KMT = P.sbuf("KMT", [128, 4, 256], F32R); VM = P.sbuf("VM", [128, 2, 512], F32R)
        QAT = P.sbuf("QAT", [128, 8, TB], F32R); QIT = P.sbuf("QIT", [128, 8, TB], F32R)
        QBT = QIT; CQT = P.sbuf("CQT", [128, 4, TB], F32R)
        OAT = P.sbuf("OAT", [128, 8, TB], F32R); OBT = P.sbuf("OBT", [128, 8, TB], F32R)
        QMT = CQT; OMT = QAT
        PT = [P.sbuf("PT%d" % i, [128, ABATCH * TB], F32R) for i in range(2)]
        RB = [P.sbuf("RB%d" % i, [128, 512], F32R) for i in range(2)]
        KIc = [P.sbuf("KIc%d" % i, [128, 512], F32R) for i in range(1)]
        identr = P.sbuf("identr", [128, 128], F32R); LO = P.sbuf("LO", [128, 16]); HI = P.sbuf("HI", [128, 16])
        CB16 = P.sbuf("CB16", [128, 512], BF16)
        TM = [P.sbuf("TM%d" % i, [128, 512]) for i in range(3)]
        T64 = [P.sbuf("T64%d" % i, [128, 64]) for i in range(4)]
        GBT = P.sbuf("GBT", [32, 8, TB], BF16)
        GS = P.sbuf("GS", [128, 8, 32]); GB = GS; NM = P.sbuf("NM", [128, 32]); EQ = P.sbuf("EQ", [128, 32])
        M8 = P.sbuf("M8", [128, 8, 8]); TH = P.sbuf("TH", [128, 8])
        sm = P.sbuf("sm", [128, 64])
        Wk = P.sbuf("Wk", [128, 48])
        posf = P.sbuf("posf", [128, 8]); posA = P.sbuf("posA", [128, S // 128], I32); posO = P.sbuf("posO", [128, max(4, NR)], I32)
        ang = P.sbuf("ang", [128, 4, 16]); kint = P.sbuf("kint", [128, 4, 16], I32); kf = P.sbuf("kf", [128, 4, 16]); mk = P.sbuf("mk", [128, 4, 16])
        cosA = P.sbuf("cosA", [128, 4, 16]); sinA = P.sbuf("sinA", [128, 4, 16])
        cosI = P.sbuf("cosI", [128, 4, 8]); sinI = P.sbuf("sinI", [128, 4, 8])
        WI = P.sbuf("WI", [128, 16]); SG = P.sbuf("SG", [128, 16])
        tposs = P.sbuf("tposs", [128, NR]); tcurs = P.sbuf("tcurs", [128, NR])
        pa = [P.psum("pa%d" % i, [128, 512]) for i in range(2)]
        pt = [P.psum("pt%d" % i, [128, 512]) for i in range(2)]
        pL = [P.psum("pL%d" % i, [128, 512]) for i in range(2)]
        pO = P.psum("pO", [128, 512]); pR = P.psum("pR", [128, 512])
        cnt = {"pa": 0, "pt": 0, "pL": 0, "TM": 0, "PT": 0, "RB": 0, "KI": 0, "XT": 0, "cp": 0}

        def rot(lst, key):
            cnt[key] += 1
            return lst[cnt[key] % len(lst)]

        a1 = A1.t[:]
        a2 = A2.t[:]
        a3 = A3.t[:]
        HTG = [P.view("HTG%d" % i, a1[:, i * 2048:(i + 1) * 2048].bitcast(F32R).rearrange("p (k t) -> p k t", k=KC)) for i in range(4)]
        score = P.view("score", a1)
        scoreW = a1.bitcast(F32R)
        MG = P.view("MG", a1[:, 0:2048]); MGW = a1[:, 0:2048].bitcast(F32R); X1 = XT[0]
        AT = P.view("AT", a1[:, 4096:6144].bitcast(F32R).rearrange("p (k t) -> p k t", k=KC))
        AT2 = P.view("AT2", a1[:, 4096:8192].bitcast(F32R).rearrange("p (k t) -> p k t", k=KC))
        MG2 = P.view("MG2", a1[:, 2048:4096]); MG2W = a1[:, 2048:4096].bitcast(F32R)
        MHT = P.view("MHT", a1[:, 0:4096].bitcast(F32R).rearrange("p (k t) -> p k t", k=KC))
        KST = [P.view("KST%d" % i, a2[:, i * 2048:(i + 1) * 2048].bitcast(F32R).rearrange("p (h t) -> p h t", h=4)) for i in range(2)]
        VST = [P.view("VST%d" % i, a3[:, i * 2048:(i + 1) * 2048].bitcast(F32R).rearrange("p (i c) -> p i c", i=4)) for i in range(2)]
        biasM = P.view("biasM", a2[:, 0:4096].bitcast(BF16))
        X2e = P.view("X2e", a2[:, 0:2048])
        HT2 = P.view("HT2", a3[:, 0:4096].bitcast(F32R).rearrange("p (k t) -> p k t", k=KC))
        x2sV = P.view("x2sV", None); x2sS = P.view("x2sS", None)
        KBf = [P.view("KBf%d" % i, a3[:, i * 1024:(i + 1) * 1024].bitcast(F32R)) for i in range(2)]
        VBf = [P.view("VBf%d" % i, a3[:, 2048 + i * 1024:2048 + (i + 1) * 1024].bitcast(F32R).rearrange("p (j d) -> p j d", j=8)) for i in range(2)]

        def ld(dst, src_ap, dst_ap=None, eng="sp"):
            P.dma(eng, dst_ap if dst_ap is not None else dst[:], src_ap, writes=[dst])

        def copy(dst_ap, src_ap, reads, writes):
            cnt["cp"] += 1
            if cnt["cp"] % 2:
                P.op("act", lambda e: e.activation(out=dst_ap, in_=src_ap, func=AF.Copy), reads=reads, writes=writes)
            else:
                P.op("dve", lambda e: e.tensor_copy(out=dst_ap, in_=src_ap), reads=reads, writes=writes)

        def load_w(w_ap, k0, kc, c0, n):
            slot = wslot()
            src = w_ap[k0:k0 + kc * 128, c0:c0 + n].rearrange("(k p) n -> p k n", p=128)
            P.dma("sp", slot[:, 0:kc, 0:n], src, writes=[slot])
            return slot

        def load_g(dst, g_ap, n):
            P.dma("sp", dst[:, 0:n], g_ap[0, :].partition_broadcast(128), writes=[dst])

        def norm(Xb, x_ap, Gb, g_ap, F, out_ap, outb, final_row=None):
            scr = rot(TM, "TM")
            if F > 512:
                npc = F // 512
                for i in range(npc):
                    P.op("act", lambda e, i=i: e.activation(out=scr[:, 0:512], in_=x_ap[:, i * 512:(i + 1) * 512], func=AF.Square,
                                                            accum_out=sm[:, 8 + i:9 + i]), reads=[Xb], writes=[scr, sm])
                P.op("dve", lambda e: e.tensor_reduce(out=sm[:, 0:1], in_=sm[:, 8:8 + npc], axis=AX.X, op=ALU.add), reads=[sm], writes=[sm])
            else:
                P.op("act", lambda e: e.activation(out=scr[:, 0:F], in_=x_ap, func=AF.Square, accum_out=sm[:, 0:1]), reads=[Xb], writes=[scr, sm])
            P.op("dve", lambda e: e.tensor_scalar(out=sm[:, 1:2], in0=sm[:, 0:1], scalar1=1.0 / F, scalar2=EPS, op0=ALU.mult, op1=ALU.add), reads=[sm], writes=[sm])
            P.op("act", lambda e: e.activation(out=sm[:, 2:3], in_=sm[:, 1:2], func=AF.Sqrt), reads=[sm], writes=[sm])
            P.op("dve", lambda e: e.reciprocal(out=sm[:, 3:4], in_=sm[:, 2:3]), reads=[sm], writes=[sm])
            if final_row is not None:
                for i in range(4):
                    o = rot(TM, "TM")
                    P.op("dve", lambda e, o=o, i=i: e.scalar_tensor_tensor(out=o[:, :], in0=x_ap[:, i * 512:(i + 1) * 512], scalar=sm[:, 3:4], in1=Gb[:, i * 512:(i + 1) * 512],
                                                                          op0=ALU.mult, op1=ALU.mult), reads=[Xb, sm, Gb], writes=[o])
                    P.dma("pool", out[final_row * 128:(final_row + 1) * 128, i * 512:(i + 1) * 512], o[:, :], reads=[o], sembuf=o)
                return
            P.op("dve", lambda e: e.scalar_tensor_tensor(out=out_ap, in0=x_ap, scalar=sm[:, 3:4], in1=Gb[:, 0:F], op0=ALU.mult, op1=ALU.mult),
                 reads=[Xb, sm, Gb], writes=[outb])

        def transposes(srcb, src_ap, ncols, dstb, dst_fn, rows=128):
            nb = ncols // rows
            j = 0
            while j < nb:
                nj = min(4, nb - j)
                ps = rot(pt, "pt")
                for jj in range(nj):
                    P.op("pe", lambda e, jj=jj, j=j, ps=ps: e.transpose(out=ps[0:rows, jj * 128:(jj + 1) * 128],
                                                                        in_=src_ap[:, (j + jj) * rows:(j + jj + 1) * rows], identity=ident[:]),
                         reads=[srcb, ident], writes=[ps])
                copy(dst_fn(j, nj), ps[0:rows, 0:nj * 128].rearrange("p (j t) -> p j t", j=nj), [ps], [dstb])
                j += nj

        def mm(ps_ap, psb, lhs_fn, rhs_fn, kc, reads):
            for k in range(kc):
                P.op("pe", lambda e, k=k: e.matmul(ps_ap, lhsT=lhs_fn(k), rhs=rhs_fn(k), start=(k == 0), stop=(k == kc - 1)),
                     reads=reads, writes=[psb])

        def rope_tables(posb, pos_ap, n):
            if 'norope' in dbg:
                return
            if 'notables' in dbg:
                for tb_ in (cosA, sinA, cosI, sinI):
                    P.op("dve", lambda e, tb_=tb_: e.memset(tb_[:], 0.5), writes=[tb_])
                return
            P.op("dve", lambda e: e.tensor_copy(out=posf[:, 0:n], in_=pos_ap), reads=[posb], writes=[posf])
            for (inv, half, cs, sn) in ((invf, 16, cosA, sinA), (invfi, 8, cosI, sinI)):
                a = ang[:, 0:n, 0:half]; ki = kint[:, 0:n, 0:half]; kk = kf[:, 0:n, 0:half]; m_ = mk[:, 0:n, 0:half]
                for i_ in range(n):
                    P.op("dve", lambda e, i_=i_, inv=inv, half=half: e.tensor_scalar(out=ang[:, i_, 0:half], in0=inv[:, 0:half], scalar1=posf[:, i_:i_ + 1], scalar2=None, op0=ALU.mult),
                         reads=[posf, inv], writes=[ang])
                for (shift, dstt) in ((0.0, sn), (PI / 2, cs)):
                    P.op("dve", lambda e, a=a, ki=ki, shift=shift: e.tensor_scalar(out=ki, in0=a, scalar1=shift, scalar2=1.0 / (2 * PI), op0=ALU.add, op1=ALU.mult),
                         reads=[ang], writes=[kint])
                    P.op("dve", lambda e, ki=ki, kk=kk: e.tensor_copy(out=kk, in_=ki), reads=[kint], writes=[kf])
                    P.op("dve", lambda e, a=a, kk=kk: e.scalar_tensor_tensor(out=kk, in0=kk, scalar=-2 * PI, in1=a, op0=ALU.mult, op1=ALU.add),
                         reads=[kf, ang], writes=[kf])
                    if shift != 0.0:
                        P.op("dve", lambda e, kk=kk, shift=shift: e.tensor_scalar(out=kk, in0=kk, scalar1=shift, scalar2=None, op0=ALU.add), reads=[kf], writes=[kf])
                    P.op("dve", lambda e, kk=kk, m_=m_: e.tensor_scalar(out=m_, in0=kk, scalar1=PI, scalar2=-2 * PI, op0=ALU.is_gt, op1=ALU.mult), reads=[kf], writes=[mk])
                    P.op("dve", lambda e, kk=kk, m_=m_: e.tensor_tensor(out=kk, in0=kk, in1=m_, op=ALU.add), reads=[kf, mk], writes=[kf])
                    P.op("dve", lambda e, kk=kk, m_=m_: e.tensor_scalar(out=m_, in0=kk, scalar1=-PI, scalar2=2 * PI, op0=ALU.is_lt, op1=ALU.mult), reads=[kf], writes=[mk])
                    P.op("dve", lambda e, kk=kk, m_=m_: e.tensor_tensor(out=kk, in0=kk, in1=m_, op=ALU.add), reads=[kf, mk], writes=[kf])
                    P.op("dve", lambda e, kk=kk: e.tensor_scalar(out=kk, in0=kk, scalar1=-3.14159, scalar2=3.14159, op0=ALU.max, op1=ALU.min), reads=[kf], writes=[kf])
                    P.op("act", lambda e, kk=kk, dstt=dstt, half=half: e.activation(out=dstt[:, 0:n, 0:half], in_=kk, func=AF.Sin), reads=[kf], writes=[dstt])

        def rope(psb, ps_ap, nh, hd, half, cs_ap, sn_ap, csb, snb, dstb, dst_ap):
            P.op("act", lambda e: e.activation(out=dst_ap, in_=ps_ap, func=AF.Copy), reads=[psb], writes=[dstb])
            if 'norope' in dbg or 'noapply' in dbg:
                return
            p3 = ps_ap.rearrange("p (h d) -> p h d", h=nh)
            d3 = dst_ap.rearrange("p (h d) -> p h d", h=nh)
            x1 = p3[:, :, 0:half]; x2 = p3[:, :, half:2 * half]
            C = cs_ap.unsqueeze(1).broadcast_to([128, nh, half]); Sn = sn_ap.unsqueeze(1).broadcast_to([128, nh, half])
            t = [T64[i][:, 0:nh * half].rearrange("p (h d) -> p h d", h=nh) for i in range(4)]
            P.op("dve", lambda e: e.tensor_tensor(out=t[0], in0=x1, in1=C, op=ALU.mult), reads=[psb, csb, dstb], writes=[T64[0]])
            P.op("dve", lambda e: e.tensor_tensor(out=t[1], in0=x2, in1=Sn, op=ALU.mult), reads=[psb, snb, dstb], writes=[T64[1]])
            P.op("dve", lambda e: e.tensor_tensor(out=t[2], in0=x2, in1=C, op=ALU.mult), reads=[psb, csb, dstb], writes=[T64[2]])
            P.op("dve", lambda e: e.tensor_tensor(out=t[3], in0=x1, in1=Sn, op=ALU.mult), reads=[psb, snb, dstb], writes=[T64[3]])
            if 'apply4' in dbg:
                return
            P.op("dve", lambda e: e.tensor_tensor(out=d3[:, :, 0:half], in0=t[0], in1=t[1], op=ALU.subtract), reads=[T64[0], T64[1]], writes=[dstb])
            P.op("dve", lambda e: e.tensor_tensor(out=d3[:, :, half:2 * half], in0=t[2], in1=t[3], op=ALU.add), reads=[T64[2], T64[3]], writes=[dstb])

        for (b_, a_) in ((ident, c_ident), (identr, c_identr), (identb, c_identb), (ones, c_ones), (iotaw, c_iotaw), (iotan, c_iotan), (invf, c_invf),
                         (invfi, c_invfi), (pow2, c_pow2), (tposs, tpos), (tcurs, tcur), (posA, posall), (posO, posown)):
            ld(b_, a_)

        kv_chunks = [("ka", C_KA, 0), ("ka", C_KA + 512, 4), ("va", C_VA, 0), ("va", C_VA + 512, 512),
                     ("kb", C_KB, 0), ("kb", C_KB + 512, 4), ("vb", C_VB, 0), ("vb", C_VB + 512, 512)]
        stbufs = []
        kisS = P.view("kisS", None)
        load_g(G, g_mix, D)
        for g in range(NG if (stage >= 2 and 'nophaseA' not in dbg) else 0):
            s0 = g * 512
            rope_tables(posA, posA[:, g * 4:(g + 1) * 4], 4)
            for i in range(4):
                X = rot(XT, "XT")
                ld(X, xall[s0 + i * 128:s0 + (i + 1) * 128, :])
                norm(X, X[:], G, g_mix, D, X[:], X)
                transposes(X, X[:], D, HTG[i], lambda j, nj, i=i: HTG[i][:, j:j + nj, :])
            for (kind, c0, aux) in kv_chunks:
                slot = load_w(w_in, 0, KC, c0, 512)
                if kind in ("ka", "kb"):
                    st = rot(KST, "cp")
                else:
                    st = rot(VST, "cp")
                for i in range(4):
                    ps = rot(pa, "pa")
                    mm(ps[:, :], ps, lambda k, i=i: HTG[i][:, k, :], lambda k, slot=slot: slot[:, k, :], KC, [HTG[i], slot])
                    if kind in ("ka", "kb"):
                        tm = rot(TM, "TM")
                        rope(ps, ps[:, :], 4, 128, 16, cosA[:, i, :], sinA[:, i, :], cosA, sinA, tm, tm[:, :])
                        transposes(tm, tm[:, :], 512, st, lambda j, nj, st=st, i=i: st[:, j:j + nj, i * 128:(i + 1) * 128])
                    else:
                        copy(st[:, i, :], ps[:, :], [ps], [st])
                if kind in ("ka", "kb"):
                    dst = KaT if kind == "ka" else KbT
                    P.dma("pool", dst[aux:aux + 4, :, s0:s0 + 512].rearrange("h d s -> d h s"), st[:, :, :], reads=[st], sembuf=st)
                    if kind == "kb":
                        P.op("dve", lambda e, st=st: e.tensor_reduce(out=sm[:, 32:40].rearrange("p (h b) -> p h b", h=4),
                                                                     in_=st[:, :, :].bitcast(F32).rearrange("p h (b t) -> p h b t", b=2), axis=AX.X, op=ALU.add),
                             reads=[st], writes=[sm])
                        P.op("dve", lambda e, aux=aux, g=g: e.tensor_scalar(out=kmT[:, aux:aux + 4, 2 * g:2 * g + 2], in0=sm[:, 32:40].rearrange("p (h b) -> p h b", h=4),
                                                                            scalar1=1.0 / 256, scalar2=None, op0=ALU.mult), reads=[sm], writes=[kmT])
                else:
                    dst = Va if kind == "va" else Vb
                    P.dma("pool", dst[s0:s0 + 512, aux:aux + 512].rearrange("(i p) c -> p i c", p=128), st[:, :, :], reads=[st], sembuf=st)
                if st not in stbufs:
                    stbufs.append(st)
            slot = load_w(w_in, 0, KC, C_KI, 64)
            kis = rot(KIc, "KI")
            for i in range(4):
                ps = rot(pa, "pa")
                mm(ps[:, 0:64], ps, lambda k, i=i: HTG[i][:, k, :], lambda k, slot=slot: slot[:, k, 0:64], KC, [HTG[i], slot])
                tm = rot(TM, "TM")
                rope(ps, ps[:, 0:64], 1, 64, 8, cosI[:, i, :], sinI[:, i, :], cosI, sinI, tm, tm[:, 0:64])
                psq = rot(pt, "pt")
                P.op("pe", lambda e, tm=tm, psq=psq: e.transpose(out=psq[0:64, 0:128], in_=tm[:, 0:64], identity=ident[:]), reads=[tm, ident], writes=[psq])
                copy(kis[0:64, i * 128:(i + 1) * 128], psq[0:64, 0:128], [psq], [kis])
            P.dma("pool", KiT[:, s0:s0 + 512], kis[0:64, :], reads=[kis], sembuf=kisS)
            if kisS not in stbufs:
                stbufs.append(kisS)
        if stage >= 2:
            P.wait_all_dma("sp", stbufs)

        P.alias(HTG, [MHT])
        load_g(G, g_mem_kv, D)
        for mt in range(2):
            X = rot(XT, "XT")
            ld(X, mem[mt * 128:(mt + 1) * 128, :])
            norm(X, X[:], G, g_mem_kv, D, X[:], X)
            transposes(X, X[:], D, MHT, lambda j, nj, mt=mt: MHT[:, j:j + nj, mt * 128:(mt + 1) * 128])
        for c in range(2):
            slot = load_w(w_mem_kv, 0, KC, c * 512, 512)
            for mt in range(2):
                ps = rot(pa, "pa")
                mm(ps[:, :], ps, lambda k, mt=mt: MHT[:, k, mt * 128:(mt + 1) * 128], lambda k, slot=slot: slot[:, k, :], KC, [MHT, slot])
                if c == 0:
                    tm = rot(TM, "TM")
                    copy(tm[:, :], ps[:, :], [ps], [tm])
                    transposes(tm, tm[:, :], 512, KMT, lambda j, nj, mt=mt: KMT[:, j:j + nj, mt * 128:(mt + 1) * 128])
                else:
                    copy(VM[:, mt, :], ps[:, :], [ps], [VM])
        P.alias([MHT] + KST + VST, [score, MG, AT, biasM] + KBf + VBf)

        sc_att = 1.0 / math.sqrt(128.0)

        def attention(QT, nheads, nchunks, kload, bias_fn, OT):
            for h in range(nheads):
                for s0 in range(0, nchunks, ABATCH):
                    nb = min(ABATCH, nchunks - s0)
                    L = rot(pL, "pL")
                    vaps = []
                    for jj in range(nb):
                        j = s0 + jj
                        kap, vap, kb_, vb_ = kload(h, j)
                        vaps.append((vap, vb_))
                        extra = bias_fn(h, j)
                        P.op("pe", lambda e, kap=kap, L=L, h=h, extra=extra, jj=jj: e.matmul(L[:, jj * TB:(jj + 1) * TB], lhsT=kap, rhs=QT[:, h, :], start=True, stop=(len(extra) == 0)),
                             reads=[kb_, QT], writes=[L])
                        for xi, (lh, rh, rb) in enumerate(extra):
                            P.op("pe", lambda e, lh=lh, rh=rh, L=L, xi=xi, extra=extra, jj=jj: e.matmul(L[:, jj * TB:(jj + 1) * TB], lhsT=lh, rhs=rh, start=False, stop=(xi == len(extra) - 1)),
                                 reads=rb, writes=[L])
                    p_ = rot(PT, "PT")
                    P.op("act", lambda e, p_=p_, L=L, nb=nb: e.activation(out=p_[:, 0:nb * TB], in_=L[:, 0:nb * TB], func=AF.Exp, scale=sc_att), reads=[L], writes=[p_])
                    for jj in range(nb):
                        j = s0 + jj
                        vap, vb_ = vaps[jj]
                        P.op("pe", lambda e, vap=vap, p_=p_, j=j, jj=jj: e.matmul(pO[:, 0:TB], lhsT=vap, rhs=p_[:, jj * TB:(jj + 1) * TB], start=(j == 0), stop=(j == nchunks - 1)),
                             reads=[vb_, p_], writes=[pO])
                        P.op("pe", lambda e, p_=p_, j=j, jj=jj: e.matmul(pR[:, 0:TB], lhsT=ones[:, :], rhs=p_[:, jj * TB:(jj + 1) * TB], start=(j == 0), stop=(j == nchunks - 1)),
                             reads=[ones, p_], writes=[pR])
                REC = rot(TM, "TM")
                P.op("dve", lambda e, REC=REC: e.reciprocal(out=REC[:, 0:TB], in_=pR[:, 0:TB]), reads=[pR], writes=[REC])
                P.op("dve", lambda e, h=h, REC=REC: e.tensor_tensor(out=OT[:, h, :], in0=pO[:, 0:TB], in1=REC[:, 0:TB], op=ALU.mult), reads=[pO, REC], writes=[OT])

        for r in range(NR):
            EXT = 512 * (r + 1)
            NCH = EXT // 512
            X = rot(XT, "XT")
            ld(X, xown[r * 128:(r + 1) * 128, :])
            if stage >= 2 and 'norounds2' not in dbg:
                load_g(G, g_mix, D)
                norm(X, X[:], G, g_mix, D, X[:], X)
                transposes(X, X[:], D, HT, lambda j, nj: HT[:, j:j + nj, :])
                if r % 4 == 0:
                    rope_tables(posO, posO[:, r:r + 4], 4)
                rc = r % 4
                load_g(Gq, g_cq, 512)
                slot = load_w(w_in, 0, KC, C_CQ, 512)
                ps = rot(pa, "pa")
                mm(ps[:, :], ps, lambda k: HT[:, k, :], lambda k, slot=slot: slot[:, k, :], KC, [HT, slot])
                tm = rot(TM, "TM")
                copy(tm[:, :], ps[:, :], [ps], [tm])
                norm(tm, tm[:, :], Gq, g_cq, 512, tm[:, :], tm)
                transposes(tm, tm[:, :], 512, CQT, lambda j, nj: CQT[:, j:j + nj, :])
                for c in range(2):
                    slot = load_w(w_uq, 0, 4, c * 512, 512)
                    ps = rot(pa, "pa")
                    mm(ps[:, :], ps, lambda k: CQT[:, k, :], lambda k, slot=slot: slot[:, k, :], 4, [CQT, slot])
                    tm = rot(TM, "TM")
                    rope(ps, ps[:, :], 4, 128, 16, cosA[:, rc, :], sinA[:, rc, :], cosA, sinA, tm, tm[:, :])
                    transposes(tm, tm[:, :], 512, QAT, lambda j, nj, c=c: QAT[:, 4 * c + j:4 * c + j + nj, :])
            if stage >= 2.2:
                slot = load_w(w_in, 0, KC, C_WI, 16)
                ps = rot(pa, "pa")
                mm(ps[:, 0:16], ps, lambda k: HT[:, k, :], lambda k, slot=slot: slot[:, k, 0:16], KC, [HT, slot])
                P.op("dve", lambda e, ps=ps: e.tensor_scalar(out=WI[:, :], in0=ps[:, 0:16], scalar1=1.0 / 32.0, scalar2=None, op0=ALU.mult), reads=[ps], writes=[WI])
                P.op("dve", lambda e: e.tensor_scalar(out=SG[:, :], in0=WI[:, :], scalar1=0.0, scalar2=2.0, op0=ALU.is_ge, op1=ALU.mult), reads=[WI], writes=[SG])
                P.op("dve", lambda e: e.tensor_scalar(out=SG[:, :], in0=SG[:, :], scalar1=-1.0, scalar2=None, op0=ALU.add), reads=[SG], writes=[SG])
                P.op("dve", lambda e: e.tensor_scalar(out=LO[:, :], in0=SG[:, :], scalar1=-1.0, scalar2=0.5e30, op0=ALU.add, op1=ALU.mult), reads=[SG], writes=[LO])
                P.op("dve", lambda e: e.tensor_scalar(out=HI[:, :], in0=SG[:, :], scalar1=1.0, scalar2=0.5e30, op0=ALU.add, op1=ALU.mult), reads=[SG], writes=[HI])
                for c in range(2):
                    slot = load_w(w_iq, 0, 4, c * 512, 512)
                    ps = rot(pa, "pa")
                    mm(ps[:, :], ps, lambda k: CQT[:, k, :], lambda k, slot=slot: slot[:, k, :], 4, [CQT, slot])
                    tm = rot(TM, "TM")
                    rope(ps, ps[:, :], 8, 64, 8, cosI[:, rc, :], sinI[:, rc, :], cosI, sinI, tm, tm[:, :])
                    P.op("dve", lambda e, tm=tm, c=c: e.tensor_tensor(out=tm[:, :].rearrange("p (h d) -> p h d", h=8), in0=tm[:, :].rearrange("p (h d) -> p h d", h=8),
                                                                   in1=WI[:, 8 * c:8 * c + 8].unsqueeze(2).broadcast_to([128, 8, 64]), op=ALU.mult), reads=[tm, WI], writes=[tm])
                    transposes(tm, tm[:, :], 512, QIT, lambda j, nj, c=c: QIT[:, 4 * c + j:4 * c + j + nj, :])
                P.op("dve", lambda e, r=r: e.tensor_scalar(out=sm[:, 20:21], in0=tposs[:, r:r + 1], scalar1=-float(512 * r), scalar2=None, op0=ALU.add), reads=[tposs], writes=[sm])
                P.op("dve", lambda e: e.tensor_scalar(out=CB16[:, :], in0=iotaw[:, :], scalar1=sm[:, 20:21], scalar2=MB, op0=ALU.is_gt, op1=ALU.mult), reads=[iotaw, sm], writes=[CB16])
                for c in range(NCH):
                    kc_ = rot(KIc, "KI")
                    P.dma("sp", kc_[0:64, :], KiT[:, c * 512:(c + 1) * 512], writes=[kc_])
                    P.dma("sp", kc_[64:128, :], KiT[:, c * 512:(c + 1) * 512], writes=[kc_])
                    for h in range(16):
                        pr, hf = h // 2, h % 2
                        L = rot(pL, "pL")
                        P.op("pe", lambda e, L=L, pr=pr, hf=hf, kc_=kc_: e.matmul(L[:, :], lhsT=QIT[64 * hf:64 * hf + 64, pr, :], rhs=kc_[64 * hf:64 * hf + 64, :], start=True, stop=True),
                             reads=[QIT, kc_], writes=[L])
                        rb = rot(RB, "RB")
                        P.op("dve", lambda e, L=L, rb=rb, h=h: e.tensor_scalar(out=rb[:, :], in0=L[:, :], scalar1=LO[:, h:h + 1], scalar2=HI[:, h:h + 1], op0=ALU.max, op1=ALU.min),
                             reads=[L, LO, HI], writes=[rb])
                        P.op("pe", lambda e, rb=rb, h=h: e.matmul(pO[:, :], lhsT=identr[:, :], rhs=rb[:, :], start=(h == 0), stop=(h == 15)), reads=[identr, rb], writes=[pO])
                    if c == NCH - 1:
                        CB = rot(TM, "TM")
                        P.op("dve", lambda e, CB=CB: e.tensor_scalar(out=CB[:, :], in0=iotaw[:, :], scalar1=sm[:, 20:21], scalar2=NEG, op0=ALU.is_gt, op1=ALU.mult), reads=[iotaw, sm], writes=[CB])
                        P.op("dve", lambda e, c=c, CB=CB: e.tensor_tensor(out=scoreW[:, c * 512:(c + 1) * 512], in0=pO[:, :], in1=CB[:, :], op=ALU.add), reads=[pO, CB], writes=[score])
                    else:
                        P.op("act", lambda e, c=c: e.activation(out=scoreW[:, c * 512:(c + 1) * 512], in_=pO[:, :], func=AF.Copy), reads=[pO], writes=[score])
                NIT = 36 if r == 0 else 26
                if r == 0:
                    P.op("dve", lambda e: e.memset(sm[:, 21:22], -1.0e4), writes=[sm])
                else:
                    P.op("dve", lambda e: e.tensor_reduce(out=sm[:, 21:22], in_=score[:, 0:512], axis=AX.X, op=ALU.min), reads=[score], writes=[sm])
                P.op("dve", lambda e, EXT=EXT: e.tensor_reduce(out=sm[:, 22:23], in_=score[:, 0:EXT], axis=AX.X, op=ALU.max), reads=[score], writes=[sm])
                P.op("dve", lambda e: e.tensor_tensor(out=sm[:, 23:24], in0=sm[:, 22:23], in1=sm[:, 21:22], op=ALU.subtract), reads=[sm], writes=[sm])
                P.op("dve", lambda e: e.tensor_scalar(out=Wk[:, :], in0=pow2[:, :], scalar1=sm[:, 23:24], scalar2=None, op0=ALU.mult), reads=[pow2, sm], writes=[Wk])
                P.op("dve", lambda e: e.tensor_tensor(out=sm[:, 24:25], in0=sm[:, 21:22], in1=Wk[:, 0:1], op=ALU.add), reads=[sm, Wk], writes=[sm])
                jk = biasM
                for k in range(NIT):
                    P.op("dve", lambda e, EXT=EXT: e.tensor_scalar(out=jk[:, 0:EXT], in0=score[:, 0:EXT], scalar1=sm[:, 24:25], scalar2=None, op0=ALU.is_ge, op1=ALU.add,
                                                                   accum_out=sm[:, 25:26]), reads=[score, sm], writes=[jk, sm])
                    P.op("dve", lambda e, k=k: e.scalar_tensor_tensor(out=sm[:, 26:27], in0=sm[:, 25:26], scalar=256.0, in1=Wk[:, k:k + 1], op0=ALU.is_ge, op1=ALU.mult),
                         reads=[sm, Wk], writes=[sm])
                    P.op("dve", lambda e, k=k: e.scalar_tensor_tensor(out=sm[:, 24:25], in0=sm[:, 26:27], scalar=Wk[:, k + 1:k + 2], in1=sm[:, 24:25], op0=ALU.subtract, op1=ALU.add),
                         reads=[sm, Wk], writes=[sm])
                P.op("dve", lambda e, NIT=NIT: e.tensor_tensor(out=sm[:, 27:28], in0=sm[:, 24:25], in1=Wk[:, NIT:NIT + 1], op=ALU.subtract), reads=[sm, Wk], writes=[sm])
                P.op("dve", lambda e, EXT=EXT: e.tensor_scalar(out=biasM[:, 0:EXT], in0=score[:, 0:EXT], scalar1=sm[:, 27:28], scalar2=MB, op0=ALU.is_lt, op1=ALU.mult),
                     reads=[score, sm], writes=[biasM])

            if stage >= 2.4:
                def kload_a(h, j, EXT=EXT):
                    jj = j % 8
                    if jj == 0:
                        n = min(1024, EXT - j * 128)
                        kb_ = rot(KBf, "cp"); vb_ = rot(VBf, "cp")
                        kload_a.cur = (kb_, vb_)
                        P.dma("sp", kb_[:, 0:n], KaT[h, :, j * 128:j * 128 + n], writes=[kb_])
                        P.dma("sp", vb_[:, 0:n // 128, :], Va[j * 128:j * 128 + n, h * 128:(h + 1) * 128].rearrange("(j p) d -> p j d", p=128), writes=[vb_])
                    kb_, vb_ = kload_a.cur
                    return kb_[:, jj * 128:(jj + 1) * 128], vb_[:, jj, :], kb_, vb_

                attention(QAT, 8, EXT // 128, kload_a,
                          lambda h, j: [(biasM[:, j * 128:(j + 1) * 128], identb[:, :], [biasM, identb])], OAT)

            if stage >= 2.6:
                for c in range(2):
                    slot = load_w(w_in, 0, KC, C_QB + c * 512, 512)
                    ps = rot(pa, "pa")
                    mm(ps[:, :], ps, lambda k: HT[:, k, :], lambda k, slot=slot: slot[:, k, :], KC, [HT, slot])
                    tm = rot(TM, "TM")
                    rope(ps, ps[:, :], 4, 128, 16, cosA[:, rc, :], sinA[:, rc, :], cosA, sinA, tm, tm[:, :])
                    transposes(tm, tm[:, :], 512, QBT, lambda j, nj, c=c: QBT[:, 4 * c + j:4 * c + j + nj, :])
                for h in range(8):
                    P.op("pe", lambda e, h=h: e.matmul(pR[:, h * 32:h * 32 + NBLK], lhsT=QBT[:, h, :], rhs=kmT[:, h, 0:NBLK], start=True, stop=True), reads=[QBT, kmT], writes=[pR])
                P.op("dve", lambda e, r=r: e.tensor_scalar(out=NM[:, :], in0=iotan[:, :], scalar1=tcurs[:, r:r + 1], scalar2=NEG, op0=ALU.is_ge, op1=ALU.mult), reads=[iotan, tcurs], writes=[NM])
                P.op("dve", lambda e, r=r: e.tensor_scalar(out=EQ[:, :], in0=iotan[:, :], scalar1=tcurs[:, r:r + 1], scalar2=None, op0=ALU.not_equal), reads=[iotan, tcurs], writes=[EQ])
                P.op("dve", lambda e: e.memset(GS[:, :, :], NEG), writes=[GS])
                P.op("dve", lambda e: e.tensor_tensor(out=GS[:, :, 0:NBLK], in0=pR[:, 0:256].rearrange("p (h n) -> p h n", h=8)[:, :, 0:NBLK],
                                                      in1=NM[:, 0:NBLK].unsqueeze(1).broadcast_to([128, 8, NBLK]), op=ALU.add), reads=[pR, NM], writes=[GS])
                for h in range(8):
                    P.op("dve", lambda e, h=h: e.max(out=M8[:, h, :], in_=GS[:, h, :]), reads=[GS], writes=[M8])
                P.op("dve", lambda e: e.tensor_scalar(out=TH[:, :], in0=M8[:, :, 2], scalar1=-1.0e29, scalar2=None, op0=ALU.max), reads=[M8], writes=[TH])
                P.op("dve", lambda e: e.tensor_tensor(out=GB[:, :, :], in0=GS[:, :, :], in1=TH[:, :].unsqueeze(2).broadcast_to([128, 8, 32]), op=ALU.is_lt), reads=[GS, TH], writes=[GB])
                P.op("dve", lambda e: e.scalar_tensor_tensor(out=GB[:, :, :], in0=GB[:, :, :], scalar=MB, in1=EQ[:, :].unsqueeze(1).broadcast_to([128, 8, 32]), op0=ALU.mult, op1=ALU.mult),
                     reads=[GB, EQ], writes=[GB])
                for h in range(8):
                    psq = rot(pt, "pt")
                    P.op("pe", lambda e, h=h, psq=psq: e.transpose(out=psq[0:32, 0:128], in_=GB[:, h, :], identity=ident[:]), reads=[GB, ident], writes=[psq])
                    copy(GBT[:, h, :], psq[0:32, 0:128], [psq], [GBT])

                def kload_b(h, j, EXT=EXT):
                    jj = j % 8
                    if jj == 0:
                        n = min(1024, EXT - j * 128)
                        kb_ = rot(KBf, "cp"); vb_ = rot(VBf, "cp")
                        kload_b.cur = (kb_, vb_)
                        P.dma("sp", kb_[:, 0:n], KbT[h, :, j * 128:j * 128 + n], writes=[kb_])
                        P.dma("sp", vb_[:, 0:n // 128, :], Vb[j * 128:j * 128 + n, h * 128:(h + 1) * 128].rearrange("(j p) d -> p j d", p=128), writes=[vb_])
                    kb_, vb_ = kload_b.cur
                    return kb_[:, jj * 128:(jj + 1) * 128], vb_[:, jj, :], kb_, vb_

                def bias_b(h, j, r=r):
                    n = j // 2
                    ex = [(identb[0:32, n:n + 1].broadcast_to([32, 128]), GBT[:, h, :], [identb, GBT])]
                    if j >= 4 * r:
                        w = j - 4 * r
                        ex.append((CB16[:, w * 128:(w + 1) * 128], identb[:, :], [CB16, identb]))
                    return ex

                attention(QBT, 8, EXT // 128, kload_b, bias_b, OBT)

            if stage < 3:
                ld(X1, xown[r * 128:(r + 1) * 128, :])
            if stage >= 3:
                for c in range(4):
                    for bi, (OT_, wo, cg) in enumerate(((OAT, w_dsa_o, C_GA), (OBT, w_moba_o, C_GB))):
                        slot = load_w(wo, 0, 8, c * 512, 512)
                        ps = rot(pa, "pa")
                        mm(ps[:, :], ps, lambda k, OT_=OT_: OT_[:, k, :], lambda k, slot=slot: slot[:, k, :], 8, [OT_, slot])
                        y = rot(TM, "TM")
                        P.op("act", lambda e, y=y, ps=ps: e.activation(out=y[:, :], in_=ps[:, :], func=AF.Copy), reads=[ps], writes=[y])
                        slot2 = load_w(w_in, 0, KC, cg + c * 512, 512)
                        ps2 = rot(pa, "pa")
                        mm(ps2[:, :], ps2, lambda k: HT[:, k, :], lambda k, slot2=slot2: slot2[:, k, :], KC, [HT, slot2])
                        sg = rot(TM, "TM")
                        P.op("act", lambda e, sg=sg, ps2=ps2: e.activation(out=sg[:, :], in_=ps2[:, :], func=AF.Sigmoid), reads=[ps2], writes=[sg])
                        if bi == 0:
                            P.op("dve", lambda e, y=y, sg=sg, c=c: e.tensor_tensor(out=MGW[:, c * 512:(c + 1) * 512], in0=y[:, :], in1=sg[:, :], op=ALU.mult), reads=[y, sg], writes=[MG])
                        else:
                            P.op("dve", lambda e, y=y, sg=sg: e.tensor_tensor(out=y[:, :], in0=y[:, :], in1=sg[:, :], op=ALU.mult), reads=[y, sg], writes=[y])
                            P.op("dve", lambda e, y=y, c=c: e.tensor_tensor(out=MGW[:, c * 512:(c + 1) * 512], in0=MG[:, c * 512:(c + 1) * 512], in1=y[:, :], op=ALU.add), reads=[y, MG], writes=[MG])
                transposes(MG, MG[:, :], D, HT, lambda j, nj: HT[:, j:j + nj, :])
                ld(X1, xown[r * 128:(r + 1) * 128, :])
                for c in range(4):
                    slot = load_w(w_out, 0, KC, c * 512, 512)
                    ps = rot(pa, "pa")
                    mm(ps[:, :], ps, lambda k: HT[:, k, :], lambda k, slot=slot: slot[:, k, :], KC, [HT, slot])
                    P.op("dve", lambda e, ps=ps, c=c: e.tensor_tensor(out=X1[:, c * 512:(c + 1) * 512], in0=ps[:, :], in1=X1[:, c * 512:(c + 1) * 512], op=ALU.add), reads=[ps, X1], writes=[X1])

            load_g(G, g_mem_q, D)
            norm(X1, X1[:, :], G, g_mem_q, D, MGW, MG)
            transposes(MG, MG[:, :], D, HT, lambda j, nj: HT[:, j:j + nj, :])
            slot = load_w(w_mem_q, 0, KC, 0, 512)
            ps = rot(pa, "pa")
            mm(ps[:, :], ps, lambda k: HT[:, k, :], lambda k, slot=slot: slot[:, k, :], KC, [HT, slot])
            tm = rot(TM, "TM")
            copy(tm[:, :], ps[:, :], [ps], [tm])
            transposes(tm, tm[:, :], 512, QMT, lambda j, nj: QMT[:, j:j + nj, :])
            attention(QMT, 4, 2, lambda h, j: (KMT[:, h, j * 128:(j + 1) * 128], VM[:, j, h * 128:(h + 1) * 128], KMT, VM), lambda h, j: [], OMT)
            for c in range(4):
                slot = load_w(w_mem_o, 0, 4, c * 512, 512)
                ps = rot(pa, "pa")
                mm(ps[:, :], ps, lambda k: OMT[:, k, :], lambda k, slot=slot: slot[:, k, :], 4, [OMT, slot])
                P.op("dve", lambda e, ps=ps, c=c: e.tensor_tensor(out=X1[:, c * 512:(c + 1) * 512], in0=ps[:, :], in1=X1[:, c * 512:(c + 1) * 512], op=ALU.add), reads=[ps, X1], writes=[X1])

            pair = NR >= 2
            if pair and r % 2 == 0:
                P.dma("pool", x2s, X1[:, :], reads=[X1], writes=[x2sV], sembuf=x2sS)
                continue
            tiles = [(r, X1, MG, MGW)]
            if pair:
                P.alias([biasM], [X2e])
                P.dma("sp", X2e[:, :], x2s, reads=[x2sV], writes=[X2e])
                tiles = [(r - 1, X2e, MG, MGW), (r, X1, MG2, MG2W)]
            nt = len(tiles)
            P.alias(KBf + VBf, [HT2])
            load_g(G, g_ff, D)
            for m, (row, Xb, NB, NBW) in enumerate(tiles):
                norm(Xb, Xb[:, :], G, g_ff, D, NBW, NB)
                transposes(NB, NB[:, :], D, HT2, lambda j, nj, m=m: HT2[:, j:j + nj, m * 128:(m + 1) * 128])
            for qd in range(4):
                for j in range(4):
                    slot = load_w(w_ff1, 0, KC, qd * 2048 + j * 512, 512)
                    for fs in range(4):
                        ps = rot(pa, "pa")
                        mm(ps[:, 0:nt * 128], ps, lambda k, slot=slot, fs=fs: slot[:, k, fs * 128:(fs + 1) * 128], lambda k: HT2[:, k, 0:nt * 128], KC, [HT2, slot])
                        tm = rot(TM, "TM")
                        P.op("act", lambda e, tm=tm, ps=ps: e.activation(out=tm[:, 0:nt * 128], in_=ps[:, 0:nt * 128], func=AF.Relu), reads=[ps], writes=[tm])
                        P.op("dve", lambda e, tm=tm, j=j, fs=fs: e.tensor_tensor(out=AT2[:, j * 4 + fs, 0:nt * 128], in0=tm[:, 0:nt * 128], in1=tm[:, 0:nt * 128], op=ALU.mult), reads=[tm], writes=[AT2])
                for c in range(4):
                    slot = wslot()
                    P.dma("sp", slot[:, :, :], w_ff2[qd * 2048:(qd + 1) * 2048, c * 512:(c + 1) * 512].rearrange("(k p) n -> p k n", p=128), writes=[slot])
                    for m, (row, Xb, NB, NBW) in enumerate(tiles):
                        ps = rot(pa, "pa")
                        mm(ps[:, :], ps, lambda k, m=m: AT2[:, k, m * 128:(m + 1) * 128], lambda k, slot=slot: slot[:, k, :], KC, [AT2, slot])
                        P.op("dve", lambda e, ps=ps, c=c, Xb=Xb: e.tensor_tensor(out=Xb[:, c * 512:(c + 1) * 512], in0=ps[:, :], in1=Xb[:, c * 512:(c + 1) * 512], op=ALU.add), reads=[ps, Xb], writes=[Xb])

            load_g(G, g_final, D)
            for m, (row, Xb, NB, NBW) in enumerate(tiles):
                norm(Xb, Xb[:, :], G, g_final, D, None, None, final_row=row)
            P.alias([HT2], KBf + VBf)
            if pair:
                P.alias([X2e], [biasM])
        P.wait_all_dma("pool", TM)
        P.emit()
    return nc


def host_inputs(inputs, S):
    NR = S // 512
    f32 = np.float32
    consts = {
        "c_ident": np.eye(128, dtype=f32),
        "c_identr": np.eye(128, dtype=f32),
        "c_identb": np.eye(128, dtype=f32).astype(ml_dtypes.bfloat16),
        "c_ones": np.ones((128, 128), f32),
        "c_iotaw": np.tile(np.arange(512, dtype=f32)[None, :], (128, 1)),
        "c_iotan": np.tile(np.arange(32, dtype=f32)[None, :], (128, 1)),
        "c_invf": np.tile((np.float32(500000.0) ** (-np.arange(16, dtype=f32) * f32(2.0 / 32)))[None, :], (128, 1)).astype(f32),
        "c_invfi": np.tile((np.float32(500000.0) ** (-np.arange(8, dtype=f32) * f32(2.0 / 16)))[None, :], (128, 1)).astype(f32),
        "c_pow2": np.tile((0.5 ** np.arange(1, 49, dtype=np.float64)).astype(f32)[None, :], (128, 1)),
    }
    wnames = ["w_in", "w_uq", "w_iq", "w_dsa_o", "w_moba_o", "w_out", "w_mem_q", "w_mem_kv", "w_mem_o", "w_ff1", "w_ff2"]
    gnames = ["g_mix", "g_cq", "g_mem_q", "g_mem_kv", "g_ff"]
    shared = dict(consts)
    for n in wnames:
        shared[n] = np.ascontiguousarray(np.asarray(inputs[n], f32)[0])
    for n in gnames:
        shared[n] = np.ascontiguousarray(np.asarray(inputs[n], f32)[0][None, :])
    shared["g_final"] = np.ascontiguousarray(np.asarray(inputs["g_final"], f32)[None, :])
    x = np.asarray(inputs["x"], f32)
    pos = np.asarray(inputs["positions"], np.int32)
    memv = np.asarray(inputs["mem"], f32)
    maps, owners = [], []
    for c in range(8):
        b, q = c // 4, c % 4
        blks = [4 * r + (q if r % 2 == 0 else 3 - q) for r in range(NR)]
        rows = np.concatenate([np.arange(bk * 128, (bk + 1) * 128) for bk in blks])
        m = dict(shared)
        m["xall"] = np.ascontiguousarray(x[b])
        m["posall"] = np.ascontiguousarray(pos[b].reshape(S // 128, 128).T)
        m["xown"] = np.ascontiguousarray(x[b][rows])
        po = np.zeros((128, max(4, NR)), np.int32)
        po[:, :NR] = pos[b][rows].reshape(NR, 128).T
        m["posown"] = po
        m["tpos"] = np.ascontiguousarray(rows.reshape(NR, 128).T.astype(f32))
        m["tcur"] = np.ascontiguousarray((rows // 256).reshape(NR, 128).T.astype(f32))
        m["mem"] = np.ascontiguousarray(memv[b])
        maps.append(m)
        owners.append((b, rows))
    return maps, owners


_NC_CACHE = {}


def kernel(**inputs):
    S = int(np.asarray(inputs["x"]).shape[1])
    if S not in _NC_CACHE:
        _NC_CACHE[S] = build(S)
    nc = _NC_CACHE[S]
    maps, owners = host_inputs(inputs, S)
    res = run_bass_kernel_spmd(nc, maps, core_ids=list(range(8)))
    outp = np.zeros((2, S, D), np.float32)
    for c, (b, rows) in enumerate(owners):
        outp[b, rows] = np.asarray(res.results[c]["out"], np.float32)
    return outp
```

```python
import math
import numpy as np
import ml_dtypes
import concourse.bass as bass
import concourse.mybir as mybir
from concourse.bass_utils import run_bass_kernel_spmd
from contextlib import ExitStack

F32 = mybir.dt.float32
F32R = mybir.dt.float32r
BF16 = mybir.dt.bfloat16
I32 = mybir.dt.int32
AF = mybir.ActivationFunctionType
ALU = mybir.AluOpType
AX = mybir.AxisListType
ENG = ["pe", "act", "dve", "pool", "sp"]

D = 2048
KC = 16
TB = 128
NEG = -1.0e30
MB = -30000.0
ABATCH = 4
EPS = 1e-6
PI = math.pi


class Buf:
    def __init__(self, name, t=None):
        self.name = name
        self.t = t
        self.writer = None
        self.readers = {}
        self.dma_sem = None
        self.dma_cnt = 0

    def __getitem__(self, k):
        return self.t[k]


class Op:
    __slots__ = ("eng", "fn", "deps", "dmadeps", "signal", "count", "dma", "dmasem")

    def __init__(self, eng, fn):
        self.eng = eng
        self.fn = fn
        self.deps = set()
        self.dmadeps = {}
        self.signal = False
        self.count = 0
        self.dma = False
        self.dmasem = None


class Prog:
    def __init__(self, nc):
        self.nc = nc
        self.ops = {e: [] for e in ENG}
        self.es = ExitStack()
        self.semh = {}

    def sbuf(self, name, shape, dt=F32):
        return Buf(name, self.es.enter_context(self.nc.sbuf_tensor(name, list(shape), dt)))

    def psum(self, name, shape, dt=F32):
        return Buf(name, self.es.enter_context(self.nc.psum_tensor(name, list(shape), dt)))

    def view(self, name, ap):
        return Buf(name, ap)

    def _new_sem(self, name):
        h = self.es.enter_context(self.nc.semaphore(name))
        self.semh[name] = h
        return name

    def _dep(self, op, w):
        if w[0] == "dma":
            _, sem, val = w
            if op.dmadeps.get(sem, 0) < val:
                op.dmadeps[sem] = val
        else:
            if op.eng == "pe" and w[0] == "pe":
                return
            op.deps.add(w)

    def op(self, eng, fn, reads=(), writes=()):
        op = Op(eng, fn)
        me = (eng, len(self.ops[eng]))
        for b in reads:
            if b.writer is not None:
                self._dep(op, b.writer)
        for b in writes:
            if b.writer is not None:
                self._dep(op, b.writer)
            for r in b.readers.values():
                self._dep(op, r)
        for b in reads:
            b.readers[eng] = me
        for b in writes:
            b.writer = me
            b.readers = {}
        self.ops[eng].append(op)
        return op

    def dma(self, eng, out_ap, in_ap, reads=(), writes=(), sembuf=None):
        sb = sembuf if sembuf is not None else (writes[0] if writes else reads[0])
        if sb.dma_sem is None:
            sb.dma_sem = self._new_sem("d_" + sb.name)
        op = Op(eng, None)
        for b in reads:
            if b.writer is not None:
                self._dep(op, b.writer)
        for b in writes:
            if b.writer is not None and not (b.writer[0] == "dma" and not b.readers):
                self._dep(op, b.writer)
            for r in b.readers.values():
                self._dep(op, r)
        sb.dma_cnt += 16
        tok = ("dma", sb.dma_sem, sb.dma_cnt)
        for b in reads:
            b.readers[("dma", sb.dma_sem)] = tok
        for b in writes:
            b.writer = tok
            b.readers = {}
        op.dma = True
        op.dmasem = sb.dma_sem
        op.fn = lambda e, o=out_ap, i=in_ap: e.dma_start(out=o, in_=i)
        self.ops[eng].append(op)
        return op

    def wait_all_dma(self, eng, bufs):
        op = Op(eng, lambda e: e.nop())
        for b in bufs:
            if b.dma_sem is not None:
                op.dmadeps[b.dma_sem] = b.dma_cnt
        self.ops[eng].append(op)
        return op

    def alias(self, old, new):
        merged = {}
        for b in old:
            toks = list(b.readers.items())
            if b.writer is not None:
                w = b.writer
                toks.append(((("dma", w[1]) if w[0] == "dma" else w[0]), w))
            for k, t in toks:
                cur = merged.get(k)
                if cur is None or (t[0] == "dma" and t[2] > cur[2]) or (t[0] != "dma" and t[1] > cur[1]):
                    merged[k] = t
        for b in new:
            b.writer = None
            b.readers = dict(merged)

    def emit(self):
        nc = self.nc
        for e in ENG:
            for op in self.ops[e]:
                for (de, di) in op.deps:
                    self.ops[de][di].signal = True
        esem = {}
        for e in ENG:
            c = 0
            for op in self.ops[e]:
                if op.signal and not op.dma:
                    c += 1
                    op.count = c
            if c > 0:
                esem[e] = self._new_sem("s_" + e)
        engmap = {"pe": "tensor", "act": "scalar", "dve": "vector", "pool": "gpsimd", "sp": "sync"}
        prog = self

        def make(e):
            def body(eng):
                waited = {}
                for op in prog.ops[e]:
                    need = {}
                    for (de, di) in op.deps:
                        d = prog.ops[de][di]
                        s = esem[de]
                        if need.get(s, 0) < d.count:
                            need[s] = d.count
                    for s, v in op.dmadeps.items():
                        if need.get(s, 0) < v:
                            need[s] = v
                    for s, v in need.items():
                        if waited.get(s, 0) < v:
                            eng.wait_ge(prog.semh[s], v)
                            waited[s] = v
                    ins = op.fn(eng)
                    if op.dma:
                        ins.then_inc(prog.semh[op.dmasem], 16)
                    elif op.signal:
                        ins.then_inc(prog.semh[esem[e]], 1)
            return body

        with nc.Block() as block:
            for e in ENG:
                if self.ops[e]:
                    getattr(block, engmap[e])(make(e))


C_CQ, C_KA, C_VA, C_KI, C_WI, C_QB, C_KB, C_VB, C_GA, C_GB = 0, 512, 1536, 2560, 2624, 2640, 3664, 4688, 5712, 7760


def build(S, stage=99, dbg=()):
    NR = S // 512
    NG = S // 512
    NBLK = S // 256
    nc = bass.Bass("TRN2", target_bir_lowering=False)
    nc.dge_precook = False

    def din(name, shape, dt=F32):
        return nc.dram_tensor(name, list(shape), dt, kind="ExternalInput").ap()

    xall = din("xall", [S, D]); posall = din("posall", [128, S // 128], I32)
    xown = din("xown", [NR * TB, D]); posown = din("posown", [128, max(4, NR)], I32)
    tpos = din("tpos", [128, NR]); tcur = din("tcur", [128, NR])
    mem = din("mem", [256, D])
    w_in = din("w_in", [D, 9808], F32R); w_uq = din("w_uq", [512, 1024], F32R); w_iq = din("w_iq", [512, 1024], F32R)
    w_dsa_o = din("w_dsa_o", [1024, D], F32R); w_moba_o = din("w_moba_o", [1024, D], F32R)
    w_out = din("w_out", [D, D], F32R); w_mem_q = din("w_mem_q", [D, 512], F32R)
    w_mem_kv = din("w_mem_kv", [D, 1024], F32R); w_mem_o = din("w_mem_o", [512, D], F32R)
    w_ff1 = din("w_ff1", [D, 8192], F32R); w_ff2 = din("w_ff2", [8192, D], F32R)
    g_mix = din("g_mix", [1, D]); g_cq = din("g_cq", [1, 512]); g_mem_q = din("g_mem_q", [1, D])
    g_mem_kv = din("g_mem_kv", [1, D]); g_ff = din("g_ff", [1, D]); g_final = din("g_final", [1, D])
    c_ident = din("c_ident", [128, 128]); c_identr = din("c_identr", [128, 128], F32R); c_identb = din("c_identb", [128, 128], BF16)
    c_ones = din("c_ones", [128, 128], F32R); c_iotaw = din("c_iotaw", [128, 512]); c_iotan = din("c_iotan", [128, 32])
    c_invf = din("c_invf", [128, 16]); c_invfi = din("c_invfi", [128, 8]); c_pow2 = din("c_pow2", [128, 48])
    out = nc.dram_tensor("out", [NR * TB, D], F32, kind="ExternalOutput").ap()
    KaT = nc.dram_tensor("KaT", [8, 128, S], F32R, kind="Internal").ap()
    KbT = nc.dram_tensor("KbT", [8, 128, S], F32R, kind="Internal").ap()
    Va = nc.dram_tensor("Va", [S, 1024], F32R, kind="Internal").ap()
    Vb = nc.dram_tensor("Vb", [S, 1024], F32R, kind="Internal").ap()
    KiT = nc.dram_tensor("KiT", [64, S], F32R, kind="Internal").ap()
    x2s = nc.dram_tensor("x2s", [128, D], F32, kind="Internal").ap()

    P = Prog(nc)
    with P.es:
        Wt = [P.sbuf("W%d" % i, [128, KC, 512], F32R) for i in range(2)]
        wi = [0]

        def wslot():
            wi[0] ^= 1
            return Wt[wi[0]]

        XT = [P.sbuf("XT0", [128, D])]
        G = P.sbuf("G", [128, D]); Gq = G
        HTt = P.sbuf("HT", [128, KC, TB], F32R)
        HT = P.view("HTv", HTt.t)
        A1 = P.sbuf("A1", [128, 8192])
        A2 = P.sbuf("A2", [128, 4096])
        A3 = P.sbuf("A3", [128, 4096])
        ident = P.sbuf("ident", [128, 128]); identb = P.sbuf("identb", [128, 128], BF16)
        ones = P.sbuf("ones", [128, 128], F32R); iotaw = P.sbuf("iotaw", [128, 512]); iotan = P.sbuf("iotan", [128, 32])
        invf = P.sbuf("invf", [128, 16]); invfi = P.sbuf("invfi", [128, 8]); pow2 = P.sbuf("pow2", [128, 48])
        kmT = P.sbuf("kmT", [128, 8, 32], F32R)
        KMT = P.sbuf("KMT", [128, 4, 256], F32R); VM = P.sbuf("VM", [128, 2, 512], F32R)
        QAT = P.sbuf("QAT", [128, 8, TB], F32R); QIT = P.sbuf("QIT", [128, 8, TB], F32R)
        QBT = QIT; CQT = P.sbuf("CQT", [128, 4, TB], F32R)
        OAT = P.sbuf("OAT", [128, 8, TB], F32R); OBT = P.sbuf("OBT", [128, 8, TB], F32R)
        QMT = CQT; OMT = QAT
        PT = [P.sbuf("PT%d" % i, [128, ABATCH * TB], F32R) for i in range(2)]
        RB = [P.sbuf("RB%d" % i, [128, 512], F32R) for i in range(2)]
        KIc = [P.sbuf("KIc%d" % i, [128, 512], F32R) for i in range(1)]
        identr = P.sbuf("identr", [128, 128], F32R); LO = P.sbuf("LO", [128, 16]); HI = P.sbuf("HI", [128, 16])
        CB16 = P.sbuf("CB16", [128, 512], BF16)
        TM = [P.sbuf("TM%d" % i, [128, 512]) for i in range(3)]
        T64 = [P.sbuf("T64%d" % i, [128, 64]) for i in range(4)]
        GBT = P.sbuf("GBT", [32, 8, TB], BF16)
        GS = P.sbuf("GS", [128, 8, 32]); GB = GS; NM = P.sbuf("NM", [128, 32]); EQ = P.sbuf("EQ", [128, 32])
        M8 = P.sbuf("M8", [128, 8, 8]); TH = P.sbuf("TH", [128, 8])
        sm = P.sbuf("sm", [128, 64])
        Wk = P.sbuf("Wk", [128, 48])
        posf = P.sbuf("posf", [128, 8]); posA = P.sbuf("posA", [128, S // 128], I32); posO = P.sbuf("posO", [128, max(4, NR)], I32)
        ang = P.sbuf("ang", [128, 4, 16]); kint = P.sbuf("kint", [128, 4, 16], I32); kf = P.sbuf("kf", [128, 4, 16]); mk = P.sbuf("mk", [128, 4, 16])
        cosA = P.sbuf("cosA", [128, 4, 16]); sinA = P.sbuf("sinA", [128, 4, 16])
        cosI = P.sbuf("cosI", [128, 4, 8]); sinI = P.sbuf("sinI", [128, 4, 8])
        WI = P.sbuf("WI", [128, 16]); SG = P.sbuf("SG", [128, 16])
        tposs = P.sbuf("tposs", [128, NR]); tcurs = P.sbuf("tcurs", [128, NR])
        pa = [P.psum("pa%d" % i, [128, 512]) for i in range(2)]
        pt = [P.psum("pt%d" % i, [128, 512]) for i in range(2)]
        pL = [P.psum("pL%d" % i, [128, 512]) for i in range(2)]
        pO = P.psum("pO", [128, 512]); pR = P.psum("pR", [128, 512])
        cnt = {"KB": 0, "VB": 0, "pa": 0, "pt": 0, "pL": 0, "TM": 0, "PT": 0, "RB": 0, "KI": 0, "XT": 0, "cp": 0}

        def rot(lst, key):
            cnt[key] += 1
            return lst[cnt[key] % len(lst)]

        a1 = A1.t[:]
        a2 = A2.t[:]
        a3 = A3.t[:]
        HTG = [P.view("HTG%d" % i, a1[:, i * 2048:(i + 1) * 2048].bitcast(F32R).rearrange("p (k t) -> p k t", k=KC)) for i in range(4)]
        score = P.view("score", a1)
        scoreW = a1.bitcast(F32R)
        MG = P.view("MG", a1[:, 0:2048]); MGW = a1[:, 0:2048].bitcast(F32R); X1 = XT[0]
        AT = P.view("AT", a1[:, 4096:6144].bitcast(F32R).rearrange("p (k t) -> p k t", k=KC))
        AT2 = P.view("AT2", a1[:, 4096:8192].bitcast(F32R).rearrange("p (k t) -> p k t", k=KC))
        MG2 = P.view("MG2", a1[:, 2048:4096]); MG2W = a1[:, 2048:4096].bitcast(F32R)
        MHT = P.view("MHT", a1[:, 0:4096].bitcast(F32R).rearrange("p (k t) -> p k t", k=KC))
        KST = [P.view("KST%d" % i, a2[:, i * 2048:(i + 1) * 2048].bitcast(F32R).rearrange("p (h t) -> p h t", h=4)) for i in range(2)]
        VST = [P.view("VST%d" % i, a3[:, i * 2048:(i + 1) * 2048].bitcast(F32R).rearrange("p (i c) -> p i c", i=4)) for i in range(2)]
        biasM = P.view("biasM", a2[:, 0:4096].bitcast(BF16))
        X2e = P.view("X2e", a2[:, 0:2048])
        HT2 = P.view("HT2", a3[:, 0:4096].bitcast(F32R).rearrange("p (k t) -> p k t", k=KC))
        x2sV = P.view("x2sV", None); x2sS = P.view("x2sS", None)
        KBf = [P.view("KBf%d" % i, a3[:, i * 1024:(i + 1) * 1024].bitcast(F32R)) for i in range(2)]
        VBf = [P.view("VBf%d" % i, a3[:, 2048 + i * 1024:2048 + (i + 1) * 1024].bitcast(F32R).rearrange("p (j d) -> p j d", j=8)) for i in range(2)]

        def ld(dst, src_ap, dst_ap=None, eng="sp"):
            P.dma(eng, dst_ap if dst_ap is not None else dst[:], src_ap, writes=[dst])

        def copy(dst_ap, src_ap, reads, writes):
            cnt["cp"] += 1
            if cnt["cp"] % 2:
                P.op("act", lambda e: e.activation(out=dst_ap, in_=src_ap, func=AF.Copy), reads=reads, writes=writes)
            else:
                P.op("dve", lambda e: e.tensor_copy(out=dst_ap, in_=src_ap), reads=reads, writes=writes)

        def load_w(w_ap, k0, kc, c0, n):
            slot = wslot()
            src = w_ap[k0:k0 + kc * 128, c0:c0 + n].rearrange("(k p) n -> p k n", p=128)
            P.dma("sp", slot[:, 0:kc, 0:n], src, writes=[slot])
            return slot

        def load_g(dst, g_ap, n):
            P.dma("sp", dst[:, 0:n], g_ap[0, :].partition_broadcast(128), writes=[dst])

        def norm(Xb, x_ap, Gb, g_ap, F, out_ap, outb, final_row=None):
            scr = rot(TM, "TM")
            if F > 512:
                npc = F // 512
                for i in range(npc):
                    P.op("act", lambda e, i=i: e.activation(out=scr[:, 0:512], in_=x_ap[:, i * 512:(i + 1) * 512], func=AF.Square,
                                                            accum_out=sm[:, 8 + i:9 + i]), reads=[Xb], writes=[scr, sm])
                P.op("dve", lambda e: e.tensor_reduce(out=sm[:, 0:1], in_=sm[:, 8:8 + npc], axis=AX.X, op=ALU.add), reads=[sm], writes=[sm])
            else:
                P.op("act", lambda e: e.activation(out=scr[:, 0:F], in_=x_ap, func=AF.Square, accum_out=sm[:, 0:1]), reads=[Xb], writes=[scr, sm])
            P.op("dve", lambda e: e.tensor_scalar(out=sm[:, 1:2], in0=sm[:, 0:1], scalar1=1.0 / F, scalar2=EPS, op0=ALU.mult, op1=ALU.add), reads=[sm], writes=[sm])
            P.op("act", lambda e: e.activation(out=sm[:, 2:3], in_=sm[:, 1:2], func=AF.Sqrt), reads=[sm], writes=[sm])
            P.op("dve", lambda e: e.reciprocal(out=sm[:, 3:4], in_=sm[:, 2:3]), reads=[sm], writes=[sm])
            if final_row is not None:
                for i in range(4):
                    o = rot(TM, "TM")
                    P.op("dve", lambda e, o=o, i=i: e.scalar_tensor_tensor(out=o[:, :], in0=x_ap[:, i * 512:(i + 1) * 512], scalar=sm[:, 3:4], in1=Gb[:, i * 512:(i + 1) * 512],
                                                                          op0=ALU.mult, op1=ALU.mult), reads=[Xb, sm, Gb], writes=[o])
                    P.dma("pool", out[final_row * 128:(final_row + 1) * 128, i * 512:(i + 1) * 512], o[:, :], reads=[o], sembuf=o)
                return
            P.op("dve", lambda e: e.scalar_tensor_tensor(out=out_ap, in0=x_ap, scalar=sm[:, 3:4], in1=Gb[:, 0:F], op0=ALU.mult, op1=ALU.mult),
                 reads=[Xb, sm, Gb], writes=[outb])

        def transposes(srcb, src_ap, ncols, dstb, dst_fn, rows=128):
            nb = ncols // rows
            j = 0
            while j < nb:
                nj = min(4, nb - j)
                ps = rot(pt, "pt")
                for jj in range(nj):
                    P.op("pe", lambda e, jj=jj, j=j, ps=ps: e.transpose(out=ps[0:rows, jj * 128:(jj + 1) * 128],
                                                                        in_=src_ap[:, (j + jj) * rows:(j + jj + 1) * rows], identity=ident[:]),
                         reads=[srcb, ident], writes=[ps])
                copy(dst_fn(j, nj), ps[0:rows, 0:nj * 128].rearrange("p (j t) -> p j t", j=nj), [ps], [dstb])
                j += nj

        def mm(ps_ap, psb, lhs_fn, rhs_fn, kc, reads):
            for k in range(kc):
                P.op("pe", lambda e, k=k: e.matmul(ps_ap, lhsT=lhs_fn(k), rhs=rhs_fn(k), start=(k == 0), stop=(k == kc - 1)),
                     reads=reads, writes=[psb])

        def rope_tables(posb, pos_ap, n):
            if 'norope' in dbg:
                return
            if 'notables' in dbg:
                for tb_ in (cosA, sinA, cosI, sinI):
                    P.op("dve", lambda e, tb_=tb_: e.memset(tb_[:], 0.5), writes=[tb_])
                return
            P.op("dve", lambda e: e.tensor_copy(out=posf[:, 0:n], in_=pos_ap), reads=[posb], writes=[posf])
            for (inv, half, cs, sn) in ((invf, 16, cosA, sinA), (invfi, 8, cosI, sinI)):
                a = ang[:, 0:n, 0:half]; ki = kint[:, 0:n, 0:half]; kk = kf[:, 0:n, 0:half]; m_ = mk[:, 0:n, 0:half]
                for i_ in range(n):
                    P.op("dve", lambda e, i_=i_, inv=inv, half=half: e.tensor_scalar(out=ang[:, i_, 0:half], in0=inv[:, 0:half], scalar1=posf[:, i_:i_ + 1], scalar2=None, op0=ALU.mult),
                         reads=[posf, inv], writes=[ang])
                for (shift, dstt) in ((0.0, sn), (PI / 2, cs)):
                    P.op("dve", lambda e, a=a, ki=ki, shift=shift: e.tensor_scalar(out=ki, in0=a, scalar1=shift, scalar2=1.0 / (2 * PI), op0=ALU.add, op1=ALU.mult),
                         reads=[ang], writes=[kint])
                    P.op("dve", lambda e, ki=ki, kk=kk: e.tensor_copy(out=kk, in_=ki), reads=[kint], writes=[kf])
                    P.op("dve", lambda e, a=a, kk=kk: e.scalar_tensor_tensor(out=kk, in0=kk, scalar=-2 * PI, in1=a, op0=ALU.mult, op1=ALU.add),
                         reads=[kf, ang], writes=[kf])
                    if shift != 0.0:
                        P.op("dve", lambda e, kk=kk, shift=shift: e.tensor_scalar(out=kk, in0=kk, scalar1=shift, scalar2=None, op0=ALU.add), reads=[kf], writes=[kf])
                    P.op("dve", lambda e, kk=kk, m_=m_: e.tensor_scalar(out=m_, in0=kk, scalar1=PI, scalar2=-2 * PI, op0=ALU.is_gt, op1=ALU.mult), reads=[kf], writes=[mk])
                    P.op("dve", lambda e, kk=kk, m_=m_: e.tensor_tensor(out=kk, in0=kk, in1=m_, op=ALU.add), reads=[kf, mk], writes=[kf])
                    P.op("dve", lambda e, kk=kk, m_=m_: e.tensor_scalar(out=m_, in0=kk, scalar1=-PI, scalar2=2 * PI, op0=ALU.is_lt, op1=ALU.mult), reads=[kf], writes=[mk])
                    P.op("dve", lambda e, kk=kk, m_=m_: e.tensor_tensor(out=kk, in0=kk, in1=m_, op=ALU.add), reads=[kf, mk], writes=[kf])
                    P.op("dve", lambda e, kk=kk: e.tensor_scalar(out=kk, in0=kk, scalar1=-3.14159, scalar2=3.14159, op0=ALU.max, op1=ALU.min), reads=[kf], writes=[kf])
                    P.op("act", lambda e, kk=kk, dstt=dstt, half=half: e.activation(out=dstt[:, 0:n, 0:half], in_=kk, func=AF.Sin), reads=[kf], writes=[dstt])

        def rope(psb, ps_ap, nh, hd, half, cs_ap, sn_ap, csb, snb, dstb, dst_ap):
            P.op("act", lambda e: e.activation(out=dst_ap, in_=ps_ap, func=AF.Copy), reads=[psb], writes=[dstb])
            if 'norope' in dbg or 'noapply' in dbg:
                return
            p3 = ps_ap.rearrange("p (h d) -> p h d", h=nh)
            d3 = dst_ap.rearrange("p (h d) -> p h d", h=nh)
            x1 = p3[:, :, 0:half]; x2 = p3[:, :, half:2 * half]
            C = cs_ap.unsqueeze(1).broadcast_to([128, nh, half]); Sn = sn_ap.unsqueeze(1).broadcast_to([128, nh, half])
            t = [T64[i][:, 0:nh * half].rearrange("p (h d) -> p h d", h=nh) for i in range(4)]
            P.op("dve", lambda e: e.tensor_tensor(out=t[0], in0=x1, in1=C, op=ALU.mult), reads=[psb, csb, dstb], writes=[T64[0]])
            P.op("dve", lambda e: e.tensor_tensor(out=t[1], in0=x2, in1=Sn, op=ALU.mult), reads=[psb, snb, dstb], writes=[T64[1]])
            P.op("dve", lambda e: e.tensor_tensor(out=t[2], in0=x2, in1=C, op=ALU.mult), reads=[psb, csb, dstb], writes=[T64[2]])
            P.op("dve", lambda e: e.tensor_tensor(out=t[3], in0=x1, in1=Sn, op=ALU.mult), reads=[psb, snb, dstb], writes=[T64[3]])
            if 'apply4' in dbg:
                return
            P.op("dve", lambda e: e.tensor_tensor(out=d3[:, :, 0:half], in0=t[0], in1=t[1], op=ALU.subtract), reads=[T64[0], T64[1]], writes=[dstb])
            P.op("dve", lambda e: e.tensor_tensor(out=d3[:, :, half:2 * half], in0=t[2], in1=t[3], op=ALU.add), reads=[T64[2], T64[3]], writes=[dstb])

        for (b_, a_) in ((ident, c_ident), (identr, c_identr), (identb, c_identb), (ones, c_ones), (iotaw, c_iotaw), (iotan, c_iotan), (invf, c_invf),
                         (invfi, c_invfi), (pow2, c_pow2), (tposs, tpos), (tcurs, tcur), (posA, posall), (posO, posown)):
            ld(b_, a_)

        kv_chunks = [("ka", C_KA, 0), ("ka", C_KA + 512, 4), ("va", C_VA, 0), ("va", C_VA + 512, 512),
                     ("kb", C_KB, 0), ("kb", C_KB + 512, 4), ("vb", C_VB, 0), ("vb", C_VB + 512, 512)]
        stbufs = []
        kisS = P.view("kisS", None)
        load_g(G, g_mix, D)
        for g in range(NG if (stage >= 2 and 'nophaseA' not in dbg) else 0):
            s0 = g * 512
            rope_tables(posA, posA[:, g * 4:(g + 1) * 4], 4)
            for i in range(4):
                X = rot(XT, "XT")
                ld(X, xall[s0 + i * 128:s0 + (i + 1) * 128, :])
                norm(X, X[:], G, g_mix, D, X[:], X)
                transposes(X, X[:], D, HTG[i], lambda j, nj, i=i: HTG[i][:, j:j + nj, :])
            for (kind, c0, aux) in kv_chunks:
                slot = load_w(w_in, 0, KC, c0, 512)
                if kind in ("ka", "kb"):
                    st = rot(KST, "cp")
                else:
                    st = rot(VST, "cp")
                for i in range(4):
                    ps = rot(pa, "pa")
                    mm(ps[:, :], ps, lambda k, i=i: HTG[i][:, k, :], lambda k, slot=slot: slot[:, k, :], KC, [HTG[i], slot])
                    if kind in ("ka", "kb"):
                        tm = rot(TM, "TM")
                        rope(ps, ps[:, :], 4, 128, 16, cosA[:, i, :], sinA[:, i, :], cosA, sinA, tm, tm[:, :])
                        transposes(tm, tm[:, :], 512, st, lambda j, nj, st=st, i=i: st[:, j:j + nj, i * 128:(i + 1) * 128])
                    else:
                        copy(st[:, i, :], ps[:, :], [ps], [st])
                if kind in ("ka", "kb"):
                    dst = KaT if kind == "ka" else KbT
                    P.dma("pool", dst[aux:aux + 4, :, s0:s0 + 512].rearrange("h d s -> d h s"), st[:, :, :], reads=[st], sembuf=st)
                    if kind == "kb":
                        P.op("dve", lambda e, st=st: e.tensor_reduce(out=sm[:, 32:40].rearrange("p (h b) -> p h b", h=4),
                                                                     in_=st[:, :, :].bitcast(F32).rearrange("p h (b t) -> p h b t", b=2), axis=AX.X, op=ALU.add),
                             reads=[st], writes=[sm])
                        P.op("dve", lambda e, aux=aux, g=g: e.tensor_scalar(out=kmT[:, aux:aux + 4, 2 * g:2 * g + 2], in0=sm[:, 32:40].rearrange("p (h b) -> p h b", h=4),
                                                                            scalar1=1.0 / 256, scalar2=None, op0=ALU.mult), reads=[sm], writes=[kmT])
                else:
                    dst = Va if kind == "va" else Vb
                    P.dma("pool", dst[s0:s0 + 512, aux:aux + 512].rearrange("(i p) c -> p i c", p=128), st[:, :, :], reads=[st], sembuf=st)
                if st not in stbufs:
                    stbufs.append(st)
            slot = load_w(w_in, 0, KC, C_KI, 64)
            kis = rot(KIc, "KI")
            for i in range(4):
                ps = rot(pa, "pa")
                mm(ps[:, 0:64], ps, lambda k, i=i: HTG[i][:, k, :], lambda k, slot=slot: slot[:, k, 0:64], KC, [HTG[i], slot])
                tm = rot(TM, "TM")
                rope(ps, ps[:, 0:64], 1, 64, 8, cosI[:, i, :], sinI[:, i, :], cosI, sinI, tm, tm[:, 0:64])
                psq = rot(pt, "pt")
                P.op("pe", lambda e, tm=tm, psq=psq: e.transpose(out=psq[0:64, 0:128], in_=tm[:, 0:64], identity=ident[:]), reads=[tm, ident], writes=[psq])
                copy(kis[0:64, i * 128:(i + 1) * 128], psq[0:64, 0:128], [psq], [kis])
            P.dma("pool", KiT[:, s0:s0 + 512], kis[0:64, :], reads=[kis], sembuf=kisS)
            if kisS not in stbufs:
                stbufs.append(kisS)
        if stage >= 2:
            P.wait_all_dma("sp", stbufs)

        P.alias(HTG, [MHT])
        load_g(G, g_mem_kv, D)
        for mt in range(2):
            X = rot(XT, "XT")
            ld(X, mem[mt * 128:(mt + 1) * 128, :])
            norm(X, X[:], G, g_mem_kv, D, X[:], X)
            transposes(X, X[:], D, MHT, lambda j, nj, mt=mt: MHT[:, j:j + nj, mt * 128:(mt + 1) * 128])
        for c in range(2):
            slot = load_w(w_mem_kv, 0, KC, c * 512, 512)
            for mt in range(2):
                ps = rot(pa, "pa")
                mm(ps[:, :], ps, lambda k, mt=mt: MHT[:, k, mt * 128:(mt + 1) * 128], lambda k, slot=slot: slot[:, k, :], KC, [MHT, slot])
                if c == 0:
                    tm = rot(TM, "TM")
                    copy(tm[:, :], ps[:, :], [ps], [tm])
                    transposes(tm, tm[:, :], 512, KMT, lambda j, nj, mt=mt: KMT[:, j:j + nj, mt * 128:(mt + 1) * 128])
                else:
                    copy(VM[:, mt, :], ps[:, :], [ps], [VM])
        P.alias([MHT] + KST + VST, [score, MG, AT, biasM] + KBf + VBf)

        sc_att = 1.0 / math.sqrt(128.0)

        def attention(QT, nheads, nchunks, kload, bias_fn, OT):
            items = [(h, s0) for h in range(nheads) for s0 in range(0, nchunks, ABATCH)]
            state = {}

            def emit_qk(h, s0):
                nb = min(ABATCH, nchunks - s0)
                L = rot(pL, "pL")
                vaps = []
                for jj in range(nb):
                    j = s0 + jj
                    kap, vap, kb_, vb_ = kload(h, j)
                    vaps.append((vap, vb_))
                    extra = bias_fn(h, j)
                    P.op("pe", lambda e, kap=kap, L=L, h=h, extra=extra, jj=jj: e.matmul(L[:, jj * TB:(jj + 1) * TB], lhsT=kap, rhs=QT[:, h, :], start=True, stop=(len(extra) == 0)),
                         reads=[kb_, QT], writes=[L])
                    for xi, (lh, rh, rb) in enumerate(extra):
                        P.op("pe", lambda e, lh=lh, rh=rh, L=L, xi=xi, extra=extra, jj=jj: e.matmul(L[:, jj * TB:(jj + 1) * TB], lhsT=lh, rhs=rh, start=False, stop=(xi == len(extra) - 1)),
                             reads=rb, writes=[L])
                state[(h, s0)] = (L, vaps, nb)

            def emit_pv(h, s0):
                L, vaps, nb = state.pop((h, s0))
                p_ = rot(PT, "PT")
                P.op("act", lambda e, p_=p_, L=L, nb=nb: e.activation(out=p_[:, 0:nb * TB], in_=L[:, 0:nb * TB], func=AF.Exp, scale=sc_att), reads=[L], writes=[p_])
                for jj in range(nb):
                    j = s0 + jj
                    vap, vb_ = vaps[jj]
                    P.op("pe", lambda e, vap=vap, p_=p_, j=j, jj=jj: e.matmul(pO[:, 0:TB], lhsT=vap, rhs=p_[:, jj * TB:(jj + 1) * TB], start=(j == 0), stop=(j == nchunks - 1)),
                         reads=[vb_, p_], writes=[pO])
                    P.op("pe", lambda e, p_=p_, j=j, jj=jj: e.matmul(pR[:, 0:TB], lhsT=ones[:, :], rhs=p_[:, jj * TB:(jj + 1) * TB], start=(j == 0), stop=(j == nchunks - 1)),
                         reads=[ones, p_], writes=[pR])
                if s0 + ABATCH >= nchunks:
                    REC = rot(TM, "TM")
                    P.op("dve", lambda e, REC=REC: e.reciprocal(out=REC[:, 0:TB], in_=pR[:, 0:TB]), reads=[pR], writes=[REC])
                    P.op("dve", lambda e, h=h, REC=REC: e.tensor_tensor(out=OT[:, h, :], in0=pO[:, 0:TB], in1=REC[:, 0:TB], op=ALU.mult), reads=[pO, REC], writes=[OT])

            for idx in range(len(items) + 1):
                if idx < len(items):
                    emit_qk(*items[idx])
                if idx >= 1:
                    emit_pv(*items[idx - 1])

        for r in range(NR):
            EXT = 512 * (r + 1)
            NCH = EXT // 512
            X = rot(XT, "XT")
            ld(X, xown[r * 128:(r + 1) * 128, :])
            if stage >= 2 and 'norounds2' not in dbg:
                load_g(G, g_mix, D)
                norm(X, X[:], G, g_mix, D, X[:], X)
                transposes(X, X[:], D, HT, lambda j, nj: HT[:, j:j + nj, :])
                if r % 4 == 0:
                    rope_tables(posO, posO[:, r:r + 4], 4)
                rc = r % 4
                load_g(Gq, g_cq, 512)
                slot = load_w(w_in, 0, KC, C_CQ, 512)
                ps = rot(pa, "pa")
                mm(ps[:, :], ps, lambda k: HT[:, k, :], lambda k, slot=slot: slot[:, k, :], KC, [HT, slot])
                tm = rot(TM, "TM")
                copy(tm[:, :], ps[:, :], [ps], [tm])
                norm(tm, tm[:, :], Gq, g_cq, 512, tm[:, :], tm)
                transposes(tm, tm[:, :], 512, CQT, lambda j, nj: CQT[:, j:j + nj, :])
                for c in range(2):
                    slot = load_w(w_uq, 0, 4, c * 512, 512)
                    ps = rot(pa, "pa")
                    mm(ps[:, :], ps, lambda k: CQT[:, k, :], lambda k, slot=slot: slot[:, k, :], 4, [CQT, slot])
                    tm = rot(TM, "TM")
                    rope(ps, ps[:, :], 4, 128, 16, cosA[:, rc, :], sinA[:, rc, :], cosA, sinA, tm, tm[:, :])
                    transposes(tm, tm[:, :], 512, QAT, lambda j, nj, c=c: QAT[:, 4 * c + j:4 * c + j + nj, :])
            if stage >= 2.2:
                slot = load_w(w_in, 0, KC, C_WI, 16)
                ps = rot(pa, "pa")
                mm(ps[:, 0:16], ps, lambda k: HT[:, k, :], lambda k, slot=slot: slot[:, k, 0:16], KC, [HT, slot])
                P.op("dve", lambda e, ps=ps: e.tensor_scalar(out=WI[:, :], in0=ps[:, 0:16], scalar1=1.0 / 32.0, scalar2=None, op0=ALU.mult), reads=[ps], writes=[WI])
                P.op("dve", lambda e: e.tensor_scalar(out=SG[:, :], in0=WI[:, :], scalar1=0.0, scalar2=2.0, op0=ALU.is_ge, op1=ALU.mult), reads=[WI], writes=[SG])
                P.op("dve", lambda e: e.tensor_scalar(out=SG[:, :], in0=SG[:, :], scalar1=-1.0, scalar2=None, op0=ALU.add), reads=[SG], writes=[SG])
                P.op("dve", lambda e: e.tensor_scalar(out=LO[:, :], in0=SG[:, :], scalar1=-1.0, scalar2=0.5e30, op0=ALU.add, op1=ALU.mult), reads=[SG], writes=[LO])
                P.op("dve", lambda e: e.tensor_scalar(out=HI[:, :], in0=SG[:, :], scalar1=1.0, scalar2=0.5e30, op0=ALU.add, op1=ALU.mult), reads=[SG], writes=[HI])
                for c in range(2):
                    slot = load_w(w_iq, 0, 4, c * 512, 512)
                    ps = rot(pa, "pa")
                    mm(ps[:, :], ps, lambda k: CQT[:, k, :], lambda k, slot=slot: slot[:, k, :], 4, [CQT, slot])
                    tm = rot(TM, "TM")
                    rope(ps, ps[:, :], 8, 64, 8, cosI[:, rc, :], sinI[:, rc, :], cosI, sinI, tm, tm[:, :])
                    P.op("dve", lambda e, tm=tm, c=c: e.tensor_tensor(out=tm[:, :].rearrange("p (h d) -> p h d", h=8), in0=tm[:, :].rearrange("p (h d) -> p h d", h=8),
                                                                   in1=WI[:, 8 * c:8 * c + 8].unsqueeze(2).broadcast_to([128, 8, 64]), op=ALU.mult), reads=[tm, WI], writes=[tm])
                    transposes(tm, tm[:, :], 512, QIT, lambda j, nj, c=c: QIT[:, 4 * c + j:4 * c + j + nj, :])
                P.op("dve", lambda e, r=r: e.tensor_scalar(out=sm[:, 20:21], in0=tposs[:, r:r + 1], scalar1=-float(512 * r), scalar2=None, op0=ALU.add), reads=[tposs], writes=[sm])
                P.op("dve", lambda e: e.tensor_scalar(out=CB16[:, :], in0=iotaw[:, :], scalar1=sm[:, 20:21], scalar2=MB, op0=ALU.is_gt, op1=ALU.mult), reads=[iotaw, sm], writes=[CB16])
                for c in range(NCH):
                    kc_ = rot(KIc, "KI")
                    P.dma("sp", kc_[0:64, :], KiT[:, c * 512:(c + 1) * 512], writes=[kc_])
                    P.dma("sp", kc_[64:128, :], KiT[:, c * 512:(c + 1) * 512], writes=[kc_])
                    prev = None
                    for h in range(17):
                        if h < 16:
                            pr, hf = h // 2, h % 2
                            L = rot(pL, "pL")
                            P.op("pe", lambda e, L=L, pr=pr, hf=hf, kc_=kc_: e.matmul(L[:, :], lhsT=QIT[64 * hf:64 * hf + 64, pr, :], rhs=kc_[64 * hf:64 * hf + 64, :], start=True, stop=True),
                                 reads=[QIT, kc_], writes=[L])
                            rb = rot(RB, "RB")
                            P.op("dve", lambda e, L=L, rb=rb, h=h: e.tensor_scalar(out=rb[:, :], in0=L[:, :], scalar1=LO[:, h:h + 1], scalar2=HI[:, h:h + 1], op0=ALU.max, op1=ALU.min),
                                 reads=[L, LO, HI], writes=[rb])
                        if prev is not None:
                            ph, prb = prev
                            P.op("pe", lambda e, prb=prb, ph=ph: e.matmul(pO[:, :], lhsT=identr[:, :], rhs=prb[:, :], start=(ph == 0), stop=(ph == 15)), reads=[identr, prb], writes=[pO])
                        prev = (h, rb) if h < 16 else None
                    if c == NCH - 1:
                        CB = rot(TM, "TM")
                        P.op("dve", lambda e, CB=CB: e.tensor_scalar(out=CB[:, :], in0=iotaw[:, :], scalar1=sm[:, 20:21], scalar2=NEG, op0=ALU.is_gt, op1=ALU.mult), reads=[iotaw, sm], writes=[CB])
                        P.op("dve", lambda e, c=c, CB=CB: e.tensor_tensor(out=scoreW[:, c * 512:(c + 1) * 512], in0=pO[:, :], in1=CB[:, :], op=ALU.add), reads=[pO, CB], writes=[score])
                    else:
                        P.op("act", lambda e, c=c: e.activation(out=scoreW[:, c * 512:(c + 1) * 512], in_=pO[:, :], func=AF.Copy), reads=[pO], writes=[score])
                NIT = 36 if r == 0 else 26
                if r == 0:
                    P.op("dve", lambda e: e.memset(sm[:, 21:22], -1.0e4), writes=[sm])
                else:
                    P.op("dve", lambda e: e.tensor_reduce(out=sm[:, 21:22], in_=score[:, 0:512], axis=AX.X, op=ALU.min), reads=[score], writes=[sm])
                P.op("dve", lambda e, EXT=EXT: e.tensor_reduce(out=sm[:, 22:23], in_=score[:, 0:EXT], axis=AX.X, op=ALU.max), reads=[score], writes=[sm])
                P.op("dve", lambda e: e.tensor_tensor(out=sm[:, 23:24], in0=sm[:, 22:23], in1=sm[:, 21:22], op=ALU.subtract), reads=[sm], writes=[sm])
                P.op("dve", lambda e: e.tensor_scalar(out=Wk[:, :], in0=pow2[:, :], scalar1=sm[:, 23:24], scalar2=None, op0=ALU.mult), reads=[pow2, sm], writes=[Wk])
                P.op("dve", lambda e: e.tensor_tensor(out=sm[:, 24:25], in0=sm[:, 21:22], in1=Wk[:, 0:1], op=ALU.add), reads=[sm, Wk], writes=[sm])
                jk = biasM
                for k in range(NIT):
                    P.op("dve", lambda e, EXT=EXT: e.tensor_scalar(out=jk[:, 0:EXT], in0=score[:, 0:EXT], scalar1=sm[:, 24:25], scalar2=None, op0=ALU.is_ge, op1=ALU.add,
                                                                   accum_out=sm[:, 25:26]), reads=[score, sm], writes=[jk, sm])
                    P.op("dve", lambda e, k=k: e.scalar_tensor_tensor(out=sm[:, 26:27], in0=sm[:, 25:26], scalar=256.0, in1=Wk[:, k:k + 1], op0=ALU.is_ge, op1=ALU.mult),
                         reads=[sm, Wk], writes=[sm])
                    P.op("dve", lambda e, k=k: e.scalar_tensor_tensor(out=sm[:, 24:25], in0=sm[:, 26:27], scalar=Wk[:, k + 1:k + 2], in1=sm[:, 24:25], op0=ALU.subtract, op1=ALU.add),
                         reads=[sm, Wk], writes=[sm])
                P.op("dve", lambda e, NIT=NIT: e.tensor_tensor(out=sm[:, 27:28], in0=sm[:, 24:25], in1=Wk[:, NIT:NIT + 1], op=ALU.subtract), reads=[sm, Wk], writes=[sm])
                P.op("dve", lambda e, EXT=EXT: e.tensor_scalar(out=biasM[:, 0:EXT], in0=score[:, 0:EXT], scalar1=sm[:, 27:28], scalar2=MB, op0=ALU.is_lt, op1=ALU.mult),
                     reads=[score, sm], writes=[biasM])

            if stage >= 2.4:
                def kload_a(h, j, EXT=EXT):
                    jj = j % 8
                    if jj == 0:
                        n = min(1024, EXT - j * 128)
                        kb_ = rot(KBf, "KB"); vb_ = rot(VBf, "VB")
                        kload_a.cur = (kb_, vb_)
                        P.dma("sp", kb_[:, 0:n], KaT[h, :, j * 128:j * 128 + n], writes=[kb_])
                        P.dma("sp", vb_[:, 0:n // 128, :], Va[j * 128:j * 128 + n, h * 128:(h + 1) * 128].rearrange("(j p) d -> p j d", p=128), writes=[vb_])
                    kb_, vb_ = kload_a.cur
                    return kb_[:, jj * 128:(jj + 1) * 128], vb_[:, jj, :], kb_, vb_

                attention(QAT, 8, EXT // 128, kload_a,
                          lambda h, j: [(biasM[:, j * 128:(j + 1) * 128], identb[:, :], [biasM, identb])], OAT)

            if stage >= 2.6:
                for c in range(2):
                    slot = load_w(w_in, 0, KC, C_QB + c * 512, 512)
                    ps = rot(pa, "pa")
                    mm(ps[:, :], ps, lambda k: HT[:, k, :], lambda k, slot=slot: slot[:, k, :], KC, [HT, slot])
                    tm = rot(TM, "TM")
                    rope(ps, ps[:, :], 4, 128, 16, cosA[:, rc, :], sinA[:, rc, :], cosA, sinA, tm, tm[:, :])
                    transposes(tm, tm[:, :], 512, QBT, lambda j, nj, c=c: QBT[:, 4 * c + j:4 * c + j + nj, :])
                for h in range(8):
                    P.op("pe", lambda e, h=h: e.matmul(pR[:, h * 32:h * 32 + NBLK], lhsT=QBT[:, h, :], rhs=kmT[:, h, 0:NBLK], start=True, stop=True), reads=[QBT, kmT], writes=[pR])
                P.op("dve", lambda e, r=r: e.tensor_scalar(out=NM[:, :], in0=iotan[:, :], scalar1=tcurs[:, r:r + 1], scalar2=NEG, op0=ALU.is_ge, op1=ALU.mult), reads=[iotan, tcurs], writes=[NM])
                P.op("dve", lambda e, r=r: e.tensor_scalar(out=EQ[:, :], in0=iotan[:, :], scalar1=tcurs[:, r:r + 1], scalar2=None, op0=ALU.not_equal), reads=[iotan, tcurs], writes=[EQ])
                P.op("dve", lambda e: e.memset(GS[:, :, :], NEG), writes=[GS])
                P.op("dve", lambda e: e.tensor_tensor(out=GS[:, :, 0:NBLK], in0=pR[:, 0:256].rearrange("p (h n) -> p h n", h=8)[:, :, 0:NBLK],
                                                      in1=NM[:, 0:NBLK].unsqueeze(1).broadcast_to([128, 8, NBLK]), op=ALU.add), reads=[pR, NM], writes=[GS])
                for h in range(8):
                    P.op("dve", lambda e, h=h: e.max(out=M8[:, h, :], in_=GS[:, h, :]), reads=[GS], writes=[M8])
                P.op("dve", lambda e: e.tensor_scalar(out=TH[:, :], in0=M8[:, :, 2], scalar1=-1.0e29, scalar2=None, op0=ALU.max), reads=[M8], writes=[TH])
                P.op("dve", lambda e: e.tensor_tensor(out=GB[:, :, :], in0=GS[:, :, :], in1=TH[:, :].unsqueeze(2).broadcast_to([128, 8, 32]), op=ALU.is_lt), reads=[GS, TH], writes=[GB])
                P.op("dve", lambda e: e.scalar_tensor_tensor(out=GB[:, :, :], in0=GB[:, :, :], scalar=MB, in1=EQ[:, :].unsqueeze(1).broadcast_to([128, 8, 32]), op0=ALU.mult, op1=ALU.mult),
                     reads=[GB, EQ], writes=[GB])
                for h in range(8):
                    psq = rot(pt, "pt")
                    P.op("pe", lambda e, h=h, psq=psq: e.transpose(out=psq[0:32, 0:128], in_=GB[:, h, :], identity=ident[:]), reads=[GB, ident], writes=[psq])
                    copy(GBT[:, h, :], psq[0:32, 0:128], [psq], [GBT])

                def kload_b(h, j, EXT=EXT):
                    jj = j % 8
                    if jj == 0:
                        n = min(1024, EXT - j * 128)
                        kb_ = rot(KBf, "KB"); vb_ = rot(VBf, "VB")
                        kload_b.cur = (kb_, vb_)
                        P.dma("sp", kb_[:, 0:n], KbT[h, :, j * 128:j * 128 + n], writes=[kb_])
                        P.dma("sp", vb_[:, 0:n // 128, :], Vb[j * 128:j * 128 + n, h * 128:(h + 1) * 128].rearrange("(j p) d -> p j d", p=128), writes=[vb_])
                    kb_, vb_ = kload_b.cur
                    return kb_[:, jj * 128:(jj + 1) * 128], vb_[:, jj, :], kb_, vb_

                def bias_b(h, j, r=r):
                    n = j // 2
                    ex = [(identb[0:32, n:n + 1].broadcast_to([32, 128]), GBT[:, h, :], [identb, GBT])]
                    if j >= 4 * r:
                        w = j - 4 * r
                        ex.append((CB16[:, w * 128:(w + 1) * 128], identb[:, :], [CB16, identb]))
                    return ex

                attention(QBT, 8, EXT // 128, kload_b, bias_b, OBT)

            if stage < 3:
                ld(X1, xown[r * 128:(r + 1) * 128, :])
            if stage >= 3:
                for c in range(4):
                    for bi, (OT_, wo, cg) in enumerate(((OAT, w_dsa_o, C_GA), (OBT, w_moba_o, C_GB))):
                        slot = load_w(wo, 0, 8, c * 512, 512)
                        ps = rot(pa, "pa")
                        mm(ps[:, :], ps, lambda k, OT_=OT_: OT_[:, k, :], lambda k, slot=slot: slot[:, k, :], 8, [OT_, slot])
                        y = rot(TM, "TM")
                        P.op("act", lambda e, y=y, ps=ps: e.activation(out=y[:, :], in_=ps[:, :], func=AF.Copy), reads=[ps], writes=[y])
                        slot2 = load_w(w_in, 0, KC, cg + c * 512, 512)
                        ps2 = rot(pa, "pa")
                        mm(ps2[:, :], ps2, lambda k: HT[:, k, :], lambda k, slot2=slot2: slot2[:, k, :], KC, [HT, slot2])
                        sg = rot(TM, "TM")
                        P.op("act", lambda e, sg=sg, ps2=ps2: e.activation(out=sg[:, :], in_=ps2[:, :], func=AF.Sigmoid), reads=[ps2], writes=[sg])
                        if bi == 0:
                            P.op("dve", lambda e, y=y, sg=sg, c=c: e.tensor_tensor(out=MGW[:, c * 512:(c + 1) * 512], in0=y[:, :], in1=sg[:, :], op=ALU.mult), reads=[y, sg], writes=[MG])
                        else:
                            P.op("dve", lambda e, y=y, sg=sg: e.tensor_tensor(out=y[:, :], in0=y[:, :], in1=sg[:, :], op=ALU.mult), reads=[y, sg], writes=[y])
                            P.op("dve", lambda e, y=y, c=c: e.tensor_tensor(out=MGW[:, c * 512:(c + 1) * 512], in0=MG[:, c * 512:(c + 1) * 512], in1=y[:, :], op=ALU.add), reads=[y, MG], writes=[MG])
                transposes(MG, MG[:, :], D, HT, lambda j, nj: HT[:, j:j + nj, :])
                ld(X1, xown[r * 128:(r + 1) * 128, :])
                for c in range(4):
                    slot = load_w(w_out, 0, KC, c * 512, 512)
                    ps = rot(pa, "pa")
                    mm(ps[:, :], ps, lambda k: HT[:, k, :], lambda k, slot=slot: slot[:, k, :], KC, [HT, slot])
                    P.op("dve", lambda e, ps=ps, c=c: e.tensor_tensor(out=X1[:, c * 512:(c + 1) * 512], in0=ps[:, :], in1=X1[:, c * 512:(c + 1) * 512], op=ALU.add), reads=[ps, X1], writes=[X1])

            load_g(G, g_mem_q, D)
            norm(X1, X1[:, :], G, g_mem_q, D, MGW, MG)
            transposes(MG, MG[:, :], D, HT, lambda j, nj: HT[:, j:j + nj, :])
            slot = load_w(w_mem_q, 0, KC, 0, 512)
            ps = rot(pa, "pa")
            mm(ps[:, :], ps, lambda k: HT[:, k, :], lambda k, slot=slot: slot[:, k, :], KC, [HT, slot])
            tm = rot(TM, "TM")
            copy(tm[:, :], ps[:, :], [ps], [tm])
            transposes(tm, tm[:, :], 512, QMT, lambda j, nj: QMT[:, j:j + nj, :])
            attention(QMT, 4, 2, lambda h, j: (KMT[:, h, j * 128:(j + 1) * 128], VM[:, j, h * 128:(h + 1) * 128], KMT, VM), lambda h, j: [], OMT)
            for c in range(4):
                slot = load_w(w_mem_o, 0, 4, c * 512, 512)
                ps = rot(pa, "pa")
                mm(ps[:, :], ps, lambda k: OMT[:, k, :], lambda k, slot=slot: slot[:, k, :], 4, [OMT, slot])
                P.op("dve", lambda e, ps=ps, c=c: e.tensor_tensor(out=X1[:, c * 512:(c + 1) * 512], in0=ps[:, :], in1=X1[:, c * 512:(c + 1) * 512], op=ALU.add), reads=[ps, X1], writes=[X1])

            pair = NR >= 2
            if pair and r % 2 == 0:
                P.dma("pool", x2s, X1[:, :], reads=[X1], writes=[x2sV], sembuf=x2sS)
                continue
            tiles = [(r, X1, MG, MGW)]
            if pair:
                P.alias([biasM], [X2e])
                P.dma("sp", X2e[:, :], x2s, reads=[x2sV], writes=[X2e])
                tiles = [(r - 1, X2e, MG, MGW), (r, X1, MG2, MG2W)]
            nt = len(tiles)
            P.alias(KBf + VBf, [HT2])
            load_g(G, g_ff, D)
            for m, (row, Xb, NB, NBW) in enumerate(tiles):
                norm(Xb, Xb[:, :], G, g_ff, D, NBW, NB)
                transposes(NB, NB[:, :], D, HT2, lambda j, nj, m=m: HT2[:, j:j + nj, m * 128:(m + 1) * 128])
            for qd in range(4):
                for j in range(4):
                    slot = load_w(w_ff1, 0, KC, qd * 2048 + j * 512, 512)
                    for fs in range(4):
                        ps = rot(pa, "pa")
                        mm(ps[:, 0:nt * 128], ps, lambda k, slot=slot, fs=fs: slot[:, k, fs * 128:(fs + 1) * 128], lambda k: HT2[:, k, 0:nt * 128], KC, [HT2, slot])
                        tm = rot(TM, "TM")
                        P.op("act", lambda e, tm=tm, ps=ps: e.activation(out=tm[:, 0:nt * 128], in_=ps[:, 0:nt * 128], func=AF.Relu), reads=[ps], writes=[tm])
                        P.op("dve", lambda e, tm=tm, j=j, fs=fs: e.tensor_tensor(out=AT2[:, j * 4 + fs, 0:nt * 128], in0=tm[:, 0:nt * 128], in1=tm[:, 0:nt * 128], op=ALU.mult), reads=[tm], writes=[AT2])
                for c in range(4):
                    slot = wslot()
                    P.dma("sp", slot[:, :, :], w_ff2[qd * 2048:(qd + 1) * 2048, c * 512:(c + 1) * 512].rearrange("(k p) n -> p k n", p=128), writes=[slot])
                    for m, (row, Xb, NB, NBW) in enumerate(tiles):
                        ps = rot(pa, "pa")
                        mm(ps[:, :], ps, lambda k, m=m: AT2[:, k, m * 128:(m + 1) * 128], lambda k, slot=slot: slot[:, k, :], KC, [AT2, slot])
                        P.op("dve", lambda e, ps=ps, c=c, Xb=Xb: e.tensor_tensor(out=Xb[:, c * 512:(c + 1) * 512], in0=ps[:, :], in1=Xb[:, c * 512:(c + 1) * 512], op=ALU.add), reads=[ps, Xb], writes=[Xb])

            load_g(G, g_final, D)
            for m, (row, Xb, NB, NBW) in enumerate(tiles):
                norm(Xb, Xb[:, :], G, g_final, D, None, None, final_row=row)
            P.alias([HT2], KBf + VBf)
            if pair:
                P.alias([X2e], [biasM])
        P.wait_all_dma("pool", TM)
        P.emit()
    return nc


def host_inputs(inputs, S):
    NR = S // 512
    f32 = np.float32
    consts = {
        "c_ident": np.eye(128, dtype=f32),
        "c_identr": np.eye(128, dtype=f32),
        "c_identb": np.eye(128, dtype=f32).astype(ml_dtypes.bfloat16),
        "c_ones": np.ones((128, 128), f32),
        "c_iotaw": np.tile(np.arange(512, dtype=f32)[None, :], (128, 1)),
        "c_iotan": np.tile(np.arange(32, dtype=f32)[None, :], (128, 1)),
        "c_invf": np.tile((np.float32(500000.0) ** (-np.arange(16, dtype=f32) * f32(2.0 / 32)))[None, :], (128, 1)).astype(f32),
        "c_invfi": np.tile((np.float32(500000.0) ** (-np.arange(8, dtype=f32) * f32(2.0 / 16)))[None, :], (128, 1)).astype(f32),
        "c_pow2": np.tile((0.5 ** np.arange(1, 49, dtype=np.float64)).astype(f32)[None, :], (128, 1)),
    }
    wnames = ["w_in", "w_uq", "w_iq", "w_dsa_o", "w_moba_o", "w_out", "w_mem_q", "w_mem_kv", "w_mem_o", "w_ff1", "w_ff2"]
    gnames = ["g_mix", "g_cq", "g_mem_q", "g_mem_kv", "g_ff"]
    shared = dict(consts)
    for n in wnames:
        shared[n] = np.ascontiguousarray(np.asarray(inputs[n], f32)[0])
    for n in gnames:
        shared[n] = np.ascontiguousarray(np.asarray(inputs[n], f32)[0][None, :])
    shared["g_final"] = np.ascontiguousarray(np.asarray(inputs["g_final"], f32)[None, :])
    x = np.asarray(inputs["x"], f32)
    pos = np.asarray(inputs["positions"], np.int32)
    memv = np.asarray(inputs["mem"], f32)
    maps, owners = [], []
    for c in range(8):
        b, q = c // 4, c % 4
        blks = [4 * r + (q if r % 2 == 0 else 3 - q) for r in range(NR)]
        rows = np.concatenate([np.arange(bk * 128, (bk + 1) * 128) for bk in blks])
        m = dict(shared)
        m["xall"] = np.ascontiguousarray(x[b])
        m["posall"] = np.ascontiguousarray(pos[b].reshape(S // 128, 128).T)
        m["xown"] = np.ascontiguousarray(x[b][rows])
        po = np.zeros((128, max(4, NR)), np.int32)
        po[:, :NR] = pos[b][rows].reshape(NR, 128).T
        m["posown"] = po
        m["tpos"] = np.ascontiguousarray(rows.reshape(NR, 128).T.astype(f32))
        m["tcur"] = np.ascontiguousarray((rows // 256).reshape(NR, 128).T.astype(f32))
        m["mem"] = np.ascontiguousarray(memv[b])
        maps.append(m)
        owners.append((b, rows))
    return maps, owners


_NC_CACHE = {}


def kernel(**inputs):
    S = int(np.asarray(inputs["x"]).shape[1])
    if S not in _NC_CACHE:
        _NC_CACHE[S] = build(S)
    nc = _NC_CACHE[S]
    maps, owners = host_inputs(inputs, S)
    res = run_bass_kernel_spmd(nc, maps, core_ids=list(range(8)))
    outp = np.zeros((2, S, D), np.float32)
    for c, (b, rows) in enumerate(owners):
        outp[b, rows] = np.asarray(res.results[c]["out"], np.float32)
    return outp
```

```python
import math
import numpy as np
import ml_dtypes
import concourse.bass as bass
import concourse.mybir as mybir
from concourse.bass_utils import run_bass_kernel_spmd
from contextlib import ExitStack

F32 = mybir.dt.float32
F32R = mybir.dt.float32r
BF16 = mybir.dt.bfloat16
I32 = mybir.dt.int32
AF = mybir.ActivationFunctionType
ALU = mybir.AluOpType
AX = mybir.AxisListType
ENG = ["pe", "act", "dve", "pool", "sp"]

D = 2048
KC = 16
TB = 128
NEG = -1.0e30
MB = -30000.0
ABATCH = 4
EPS = 1e-6
PI = math.pi


class Buf:
    def __init__(self, name, t=None):
        self.name = name
        self.t = t
        self.writer = None
        self.readers = {}
        self.dma_sem = None
        self.dma_cnt = 0

    def __getitem__(self, k):
        return self.t[k]


class Op:
    __slots__ = ("eng", "fn", "deps", "dmadeps", "signal", "count", "dma", "dmasem")

    def __init__(self, eng, fn):
        self.eng = eng
        self.fn = fn
        self.deps = set()
        self.dmadeps = {}
        self.signal = False
        self.count = 0
        self.dma = False
        self.dmasem = None


class Prog:
    def __init__(self, nc):
        self.nc = nc
        self.ops = {e: [] for e in ENG}
        self.es = ExitStack()
        self.semh = {}

    def sbuf(self, name, shape, dt=F32):
        return Buf(name, self.es.enter_context(self.nc.sbuf_tensor(name, list(shape), dt)))

    def psum(self, name, shape, dt=F32):
        return Buf(name, self.es.enter_context(self.nc.psum_tensor(name, list(shape), dt)))

    def view(self, name, ap):
        return Buf(name, ap)

    def _new_sem(self, name):
        h = self.es.enter_context(self.nc.semaphore(name))
        self.semh[name] = h
        return name

    def _dep(self, op, w):
        if w[0] == "dma":
            _, sem, val = w
            if op.dmadeps.get(sem, 0) < val:
                op.dmadeps[sem] = val
        else:
            if op.eng == "pe" and w[0] == "pe":
                return
            op.deps.add(w)

    def op(self, eng, fn, reads=(), writes=()):
        op = Op(eng, fn)
        me = (eng, len(self.ops[eng]))
        for b in reads:
            if b.writer is not None:
                self._dep(op, b.writer)
        for b in writes:
            if b.writer is not None:
                self._dep(op, b.writer)
            for r in b.readers.values():
                self._dep(op, r)
        for b in reads:
            b.readers[eng] = me
        for b in writes:
            b.writer = me
            b.readers = {}
        self.ops[eng].append(op)
        return op

    def dma(self, eng, out_ap, in_ap, reads=(), writes=(), sembuf=None):
        sb = sembuf if sembuf is not None else (writes[0] if writes else reads[0])
        if sb.dma_sem is None:
            sb.dma_sem = self._new_sem("d_" + sb.name)
        op = Op(eng, None)
        for b in reads:
            if b.writer is not None:
                self._dep(op, b.writer)
        for b in writes:
            if b.writer is not None and not (b.writer[0] == "dma" and not b.readers):
                self._dep(op, b.writer)
            for r in b.readers.values():
                self._dep(op, r)
        sb.dma_cnt += 16
        tok = ("dma", sb.dma_sem, sb.dma_cnt)
        for b in reads:
            b.readers[("dma", sb.dma_sem)] = tok
        for b in writes:
            b.writer = tok
            b.readers = {}
        op.dma = True
        op.dmasem = sb.dma_sem
        op.fn = lambda e, o=out_ap, i=in_ap: e.dma_start(out=o, in_=i)
        self.ops[eng].append(op)
        return op

    def wait_all_dma(self, eng, bufs):
        op = Op(eng, lambda e: e.nop())
        for b in bufs:
            if b.dma_sem is not None:
                op.dmadeps[b.dma_sem] = b.dma_cnt
        self.ops[eng].append(op)
        return op

    def alias(self, old, new):
        merged = {}
        for b in old:
            toks = list(b.readers.items())
            if b.writer is not None:
                w = b.writer
                toks.append(((("dma", w[1]) if w[0] == "dma" else w[0]), w))
            for k, t in toks:
                cur = merged.get(k)
                if cur is None or (t[0] == "dma" and t[2] > cur[2]) or (t[0] != "dma" and t[1] > cur[1]):
                    merged[k] = t
        for b in new:
            b.writer = None
            b.readers = dict(merged)

    def emit(self):
        nc = self.nc
        for e in ENG:
            for op in self.ops[e]:
                for (de, di) in op.deps:
                    self.ops[de][di].signal = True
        esem = {}
        for e in ENG:
            c = 0
            for op in self.ops[e]:
                if op.signal and not op.dma:
                    c += 1
                    op.count = c
            if c > 0:
                esem[e] = self._new_sem("s_" + e)
        engmap = {"pe": "tensor", "act": "scalar", "dve": "vector", "pool": "gpsimd", "sp": "sync"}
        prog = self

        def make(e):
            def body(eng):
                waited = {}
                for op in prog.ops[e]:
                    need = {}
                    for (de, di) in op.deps:
                        d = prog.ops[de][di]
                        s = esem[de]
                        if need.get(s, 0) < d.count:
                            need[s] = d.count
                    for s, v in op.dmadeps.items():
                        if need.get(s, 0) < v:
                            need[s] = v
                    for s, v in need.items():
                        if waited.get(s, 0) < v:
                            eng.wait_ge(prog.semh[s], v)
                            waited[s] = v
                    ins = op.fn(eng)
                    if op.dma:
                        ins.then_inc(prog.semh[op.dmasem], 16)
                    elif op.signal:
                        ins.then_inc(prog.semh[esem[e]], 1)
            return body

        with nc.Block() as block:
            for e in ENG:
                if self.ops[e]:
                    getattr(block, engmap[e])(make(e))


C_CQ, C_KA, C_VA, C_KI, C_WI, C_QB, C_KB, C_VB, C_GA, C_GB = 0, 512, 1536, 2560, 2624, 2640, 3664, 4688, 5712, 7760


def build(S, stage=99, dbg=()):
    NR = S // 512
    NG = S // 512
    NBLK = S // 256
    nc = bass.Bass("TRN2", target_bir_lowering=False)
    nc.dge_precook = False

    def din(name, shape, dt=F32):
        return nc.dram_tensor(name, list(shape), dt, kind="ExternalInput").ap()

    xall = din("xall", [S, D]); posall = din("posall", [128, S // 128], I32)
    xown = din("xown", [NR * TB, D]); posown = din("posown", [128, max(4, NR)], I32)
    tpos = din("tpos", [128, NR]); tcur = din("tcur", [128, NR])
    mem = din("mem", [256, D])
    w_in = din("w_in", [D, 9808], F32R); w_uq = din("w_uq", [512, 1024], F32R); w_iq = din("w_iq", [512, 1024], F32R)
    w_dsa_o = din("w_dsa_o", [1024, D], F32R); w_moba_o = din("w_moba_o", [1024, D], F32R)
    w_out = din("w_out", [D, D], F32R); w_mem_q = din("w_mem_q", [D, 512], F32R)
    w_mem_kv = din("w_mem_kv", [D, 1024], F32R); w_mem_o = din("w_mem_o", [512, D], F32R)
    w_ff1 = din("w_ff1", [D, 8192], F32R); w_ff2 = din("w_ff2", [8192, D], F32R)
    g_mix = din("g_mix", [1, D]); g_cq = din("g_cq", [1, 512]); g_mem_q = din("g_mem_q", [1, D])
    g_mem_kv = din("g_mem_kv", [1, D]); g_ff = din("g_ff", [1, D]); g_final = din("g_final", [1, D])
    c_ident = din("c_ident", [128, 128]); c_identr = din("c_identr", [128, 128], F32R); c_identb = din("c_identb", [128, 128], BF16)
    c_ones = din("c_ones", [128, 128], F32R); c_iotaw = din("c_iotaw", [128, 512]); c_iotan = din("c_iotan", [128, 32])
    c_invf = din("c_invf", [128, 16]); c_invfi = din("c_invfi", [128, 8]); c_pow2 = din("c_pow2", [128, 48])
    out = nc.dram_tensor("out", [NR * TB, D], F32, kind="ExternalOutput").ap()
    KaT = nc.dram_tensor("KaT", [8, 128, S], F32R, kind="Internal").ap()
    KbT = nc.dram_tensor("KbT", [8, 128, S], F32R, kind="Internal").ap()
    Va = nc.dram_tensor("Va", [S, 1024], F32R, kind="Internal").ap()
    Vb = nc.dram_tensor("Vb", [S, 1024], F32R, kind="Internal").ap()
    KiT = nc.dram_tensor("KiT", [64, S], F32R, kind="Internal").ap()
    x2s = nc.dram_tensor("x2s", [128, D], F32, kind="Internal").ap()

    P = Prog(nc)
    with P.es:
        Wt = [P.sbuf("W%d" % i, [128, KC, 512], F32R) for i in range(2)]
        wi = [0]

        def wslot():
            wi[0] ^= 1
            return Wt[wi[0]]

        XT = [P.sbuf("XT0", [128, D])]
        G = P.sbuf("G", [128, D]); Gq = G
        HTt = P.sbuf("HT", [128, KC, TB], F32R)
        HT = P.view("HTv", HTt.t)
        A1 = P.sbuf("A1", [128, 8192])
        A2 = P.sbuf("A2", [128, 4096])
        A3 = P.sbuf("A3", [128, 4096])
        ident = P.sbuf("ident", [128, 128]); identb = P.sbuf("identb", [128, 128], BF16)
        ones = P.sbuf("ones", [128, 128], F32R); iotaw = P.sbuf("iotaw", [128, 512]); iotan = P.sbuf("iotan", [128, 32])
        invf = P.sbuf("invf", [128, 16]); invfi = P.sbuf("invfi", [128, 8]); pow2 = P.sbuf("pow2", [128, 48])
        kmT = P.sbuf("kmT", [128, 8, 32], F32R)
        KMT = P.sbuf("KMT", [128, 4, 256], F32R); VM = P.sbuf("VM", [128, 2, 512], F32R)
        QAT = P.sbuf("QAT", [128, 8, TB], F32R); QIT = P.sbuf("QIT", [128, 8, TB], F32R)
        QBT = QIT; CQT = P.sbuf("CQT", [128, 4, TB], F32R)
        OAT = P.sbuf("OAT", [128, 8, TB], F32R); OBT = P.sbuf("OBT", [128, 8, TB], F32R)
        QMT = CQT; OMT = QAT
        PT = [P.sbuf("PT%d" % i, [128, ABATCH * TB], F32R) for i in range(2)]
        RB = [P.sbuf("RB%d" % i, [128, 512], F32R) for i in range(2)]
        KIc = [P.sbuf("KIc%d" % i, [128, 512], F32R) for i in range(1)]
        identr = P.sbuf("identr", [128, 128], F32R); LO = P.sbuf("LO", [128, 16]); HI = P.sbuf("HI", [128, 16])
        CB16 = P.sbuf("CB16", [128, 512], BF16)
        TM = [P.sbuf("TM%d" % i, [128, 512]) for i in range(3)]
        T64 = [P.sbuf("T64%d" % i, [128, 64]) for i in range(4)]
        GBT = P.sbuf("GBT", [32, 8, TB], BF16)
        GS = P.sbuf("GS", [128, 8, 32]); GB = GS; NM = P.sbuf("NM", [128, 32]); EQ = P.sbuf("EQ", [128, 32])
        M8 = P.sbuf("M8", [128, 8, 8]); TH = P.sbuf("TH", [128, 8])
        sm = P.sbuf("sm", [128, 64])
        Wk = P.sbuf("Wk", [128, 48])
        posf = P.sbuf("posf", [128, 8]); posA = P.sbuf("posA", [128, S // 128], I32); posO = P.sbuf("posO", [128, max(4, NR)], I32)
        ang = P.sbuf("ang", [128, 4, 16]); kint = P.sbuf("kint", [128, 4, 16], I32); kf = P.sbuf("kf", [128, 4, 16]); mk = P.sbuf("mk", [128, 4, 16])
        cosA = P.sbuf("cosA", [128, 4, 16]); sinA = P.sbuf("sinA", [128, 4, 16])
        cosI = P.sbuf("cosI", [128, 4, 8]); sinI = P.sbuf("sinI", [128, 4, 8])
        WI = P.sbuf("WI", [128, 16]); SG = P.sbuf("SG", [128, 16])
        tposs = P.sbuf("tposs", [128, NR]); tcurs = P.sbuf("tcurs", [128, NR])
        pa = [P.psum("pa%d" % i, [128, 512]) for i in range(2)]
        pt = [P.psum("pt%d" % i, [128, 512]) for i in range(2)]
        pL = [P.psum("pL%d" % i, [128, 512]) for i in range(2)]
        pO = P.psum("pO", [128, 512]); pR = P.psum("pR", [128, 512])
        cnt = {"KB": 0, "VB": 0, "pa": 0, "pt": 0, "pL": 0, "TM": 0, "PT": 0, "RB": 0, "KI": 0, "XT": 0, "cp": 0}

        def rot(lst, key):
            cnt[key] += 1
            return lst[cnt[key] % len(lst)]

        a1 = A1.t[:]
        a2 = A2.t[:]
        a3 = A3.t[:]
        HTG = [P.view("HTG%d" % i, a1[:, i * 2048:(i + 1) * 2048].bitcast(F32R).rearrange("p (k t) -> p k t", k=KC)) for i in range(4)]
        score = P.view("score", a1)
        scoreW = a1.bitcast(F32R)
        MG = P.view("MG", a1[:, 0:2048]); MGW = a1[:, 0:2048].bitcast(F32R); X1 = XT[0]
        AT = P.view("AT", a1[:, 4096:6144].bitcast(F32R).rearrange("p (k t) -> p k t", k=KC))
        AT2 = P.view("AT2", a1[:, 4096:8192].bitcast(F32R).rearrange("p (k t) -> p k t", k=KC))
        MG2 = P.view("MG2", a1[:, 2048:4096]); MG2W = a1[:, 2048:4096].bitcast(F32R)
        MHT = P.view("MHT", a1[:, 0:4096].bitcast(F32R).rearrange("p (k t) -> p k t", k=KC))
        KST = [P.view("KST%d" % i, a2[:, i * 2048:(i + 1) * 2048].bitcast(F32R).rearrange("p (h t) -> p h t", h=4)) for i in range(2)]
        VST = [P.view("VST%d" % i, a3[:, i * 2048:(i + 1) * 2048].bitcast(F32R).rearrange("p (i c) -> p i c", i=4)) for i in range(2)]
        biasM = P.view("biasM", a2[:, 0:4096].bitcast(BF16))
        X2e = P.view("X2e", a2[:, 0:2048])
        HT2 = P.view("HT2", a3[:, 0:4096].bitcast(F32R).rearrange("p (k t) -> p k t", k=KC))
        x2sV = P.view("x2sV", None); x2sS = P.view("x2sS", None)
        KBf = [P.view("KBf%d" % i, a3[:, i * 1024:(i + 1) * 1024].bitcast(F32R)) for i in range(2)]
        VBf = [P.view("VBf%d" % i, a3[:, 2048 + i * 1024:2048 + (i + 1) * 1024].bitcast(F32R).rearrange("p (j d) -> p j d", j=8)) for i in range(2)]

        def ld(dst, src_ap, dst_ap=None, eng="sp"):
            P.dma(eng, dst_ap if dst_ap is not None else dst[:], src_ap, writes=[dst])

        def copy(dst_ap, src_ap, reads, writes):
            cnt["cp"] += 1
            if cnt["cp"] % 2:
                P.op("act", lambda e: e.activation(out=dst_ap, in_=src_ap, func=AF.Copy), reads=reads, writes=writes)
            else:
                P.op("dve", lambda e: e.tensor_copy(out=dst_ap, in_=src_ap), reads=reads, writes=writes)

        def load_w(w_ap, k0, kc, c0, n):
            slot = wslot()
            src = w_ap[k0:k0 + kc * 128, c0:c0 + n].rearrange("(k p) n -> p k n", p=128)
            P.dma("sp", slot[:, 0:kc, 0:n], src, writes=[slot])
            return slot

        def load_g(dst, g_ap, n):
            P.dma("sp", dst[:, 0:n], g_ap[0, :].partition_broadcast(128), writes=[dst])

        def norm(Xb, x_ap, Gb, g_ap, F, out_ap, outb, final_row=None):
            scr = rot(TM, "TM")
            if F > 512:
                npc = F // 512
                for i in range(npc):
                    P.op("act", lambda e, i=i: e.activation(out=scr[:, 0:512], in_=x_ap[:, i * 512:(i + 1) * 512], func=AF.Square,
                                                            accum_out=sm[:, 8 + i:9 + i]), reads=[Xb], writes=[scr, sm])
                P.op("dve", lambda e: e.tensor_reduce(out=sm[:, 0:1], in_=sm[:, 8:8 + npc], axis=AX.X, op=ALU.add), reads=[sm], writes=[sm])
            else:
                P.op("act", lambda e: e.activation(out=scr[:, 0:F], in_=x_ap, func=AF.Square, accum_out=sm[:, 0:1]), reads=[Xb], writes=[scr, sm])
            P.op("dve", lambda e: e.tensor_scalar(out=sm[:, 1:2], in0=sm[:, 0:1], scalar1=1.0 / F, scalar2=EPS, op0=ALU.mult, op1=ALU.add), reads=[sm], writes=[sm])
            P.op("act", lambda e: e.activation(out=sm[:, 2:3], in_=sm[:, 1:2], func=AF.Sqrt), reads=[sm], writes=[sm])
            P.op("dve", lambda e: e.reciprocal(out=sm[:, 3:4], in_=sm[:, 2:3]), reads=[sm], writes=[sm])
            if final_row is not None:
                for i in range(4):
                    o = rot(TM, "TM")
                    P.op("dve", lambda e, o=o, i=i: e.scalar_tensor_tensor(out=o[:, :], in0=x_ap[:, i * 512:(i + 1) * 512], scalar=sm[:, 3:4], in1=Gb[:, i * 512:(i + 1) * 512],
                                                                          op0=ALU.mult, op1=ALU.mult), reads=[Xb, sm, Gb], writes=[o])
                    P.dma("pool", out[final_row * 128:(final_row + 1) * 128, i * 512:(i + 1) * 512], o[:, :], reads=[o], sembuf=o)
                return
            P.op("dve", lambda e: e.scalar_tensor_tensor(out=out_ap, in0=x_ap, scalar=sm[:, 3:4], in1=Gb[:, 0:F], op0=ALU.mult, op1=ALU.mult),
                 reads=[Xb, sm, Gb], writes=[outb])

        def transposes(srcb, src_ap, ncols, dstb, dst_fn, rows=128):
            nb = ncols // rows
            j = 0
            while j < nb:
                nj = min(4, nb - j)
                ps = rot(pt, "pt")
                for jj in range(nj):
                    P.op("pe", lambda e, jj=jj, j=j, ps=ps: e.transpose(out=ps[0:rows, jj * 128:(jj + 1) * 128],
                                                                        in_=src_ap[:, (j + jj) * rows:(j + jj + 1) * rows], identity=ident[:]),
                         reads=[srcb, ident], writes=[ps])
                copy(dst_fn(j, nj), ps[0:rows, 0:nj * 128].rearrange("p (j t) -> p j t", j=nj), [ps], [dstb])
                j += nj

        def mm(ps_ap, psb, lhs_fn, rhs_fn, kc, reads):
            for k in range(kc):
                P.op("pe", lambda e, k=k: e.matmul(ps_ap, lhsT=lhs_fn(k), rhs=rhs_fn(k), start=(k == 0), stop=(k == kc - 1)),
                     reads=reads, writes=[psb])

        def rope_tables(posb, pos_ap, n):
            if 'norope' in dbg:
                return
            if 'notables' in dbg:
                for tb_ in (cosA, sinA, cosI, sinI):
                    P.op("dve", lambda e, tb_=tb_: e.memset(tb_[:], 0.5), writes=[tb_])
                return
            P.op("dve", lambda e: e.tensor_copy(out=posf[:, 0:n], in_=pos_ap), reads=[posb], writes=[posf])
            for (inv, half, cs, sn) in ((invf, 16, cosA, sinA), (invfi, 8, cosI, sinI)):
                a = ang[:, 0:n, 0:half]; ki = kint[:, 0:n, 0:half]; kk = kf[:, 0:n, 0:half]; m_ = mk[:, 0:n, 0:half]
                for i_ in range(n):
                    P.op("dve", lambda e, i_=i_, inv=inv, half=half: e.tensor_scalar(out=ang[:, i_, 0:half], in0=inv[:, 0:half], scalar1=posf[:, i_:i_ + 1], scalar2=None, op0=ALU.mult),
                         reads=[posf, inv], writes=[ang])
                for (shift, dstt) in ((0.0, sn), (PI / 2, cs)):
                    P.op("dve", lambda e, a=a, ki=ki, shift=shift: e.tensor_scalar(out=ki, in0=a, scalar1=shift, scalar2=1.0 / (2 * PI), op0=ALU.add, op1=ALU.mult),
                         reads=[ang], writes=[kint])
                    P.op("dve", lambda e, ki=ki, kk=kk: e.tensor_copy(out=kk, in_=ki), reads=[kint], writes=[kf])
                    P.op("dve", lambda e, a=a, kk=kk: e.scalar_tensor_tensor(out=kk, in0=kk, scalar=-2 * PI, in1=a, op0=ALU.mult, op1=ALU.add),
                         reads=[kf, ang], writes=[kf])
                    if shift != 0.0:
                        P.op("dve", lambda e, kk=kk, shift=shift: e.tensor_scalar(out=kk, in0=kk, scalar1=shift, scalar2=None, op0=ALU.add), reads=[kf], writes=[kf])
                    P.op("dve", lambda e, kk=kk, m_=m_: e.tensor_scalar(out=m_, in0=kk, scalar1=PI, scalar2=-2 * PI, op0=ALU.is_gt, op1=ALU.mult), reads=[kf], writes=[mk])
                    P.op("dve", lambda e, kk=kk, m_=m_: e.tensor_tensor(out=kk, in0=kk, in1=m_, op=ALU.add), reads=[kf, mk], writes=[kf])
                    P.op("dve", lambda e, kk=kk, m_=m_: e.tensor_scalar(out=m_, in0=kk, scalar1=-PI, scalar2=2 * PI, op0=ALU.is_lt, op1=ALU.mult), reads=[kf], writes=[mk])
                    P.op("dve", lambda e, kk=kk, m_=m_: e.tensor_tensor(out=kk, in0=kk, in1=m_, op=ALU.add), reads=[kf, mk], writes=[kf])
                    P.op("dve", lambda e, kk=kk: e.tensor_scalar(out=kk, in0=kk, scalar1=-3.14159, scalar2=3.14159, op0=ALU.max, op1=ALU.min), reads=[kf], writes=[kf])
                    P.op("act", lambda e, kk=kk, dstt=dstt, half=half: e.activation(out=dstt[:, 0:n, 0:half], in_=kk, func=AF.Sin), reads=[kf], writes=[dstt])

        def rope(psb, ps_ap, nh, hd, half, cs_ap, sn_ap, csb, snb, dstb, dst_ap):
            P.op("act", lambda e: e.activation(out=dst_ap, in_=ps_ap, func=AF.Copy), reads=[psb], writes=[dstb])
            if 'norope' in dbg or 'noapply' in dbg:
                return
            p3 = ps_ap.rearrange("p (h d) -> p h d", h=nh)
            d3 = dst_ap.rearrange("p (h d) -> p h d", h=nh)
            x1 = p3[:, :, 0:half]; x2 = p3[:, :, half:2 * half]
            C = cs_ap.unsqueeze(1).broadcast_to([128, nh, half]); Sn = sn_ap.unsqueeze(1).broadcast_to([128, nh, half])
            t = [T64[i][:, 0:nh * half].rearrange("p (h d) -> p h d", h=nh) for i in range(4)]
            P.op("dve", lambda e: e.tensor_tensor(out=t[0], in0=x1, in1=C, op=ALU.mult), reads=[psb, csb, dstb], writes=[T64[0]])
            P.op("dve", lambda e: e.tensor_tensor(out=t[1], in0=x2, in1=Sn, op=ALU.mult), reads=[psb, snb, dstb], writes=[T64[1]])
            P.op("dve", lambda e: e.tensor_tensor(out=t[2], in0=x2, in1=C, op=ALU.mult), reads=[psb, csb, dstb], writes=[T64[2]])
            P.op("dve", lambda e: e.tensor_tensor(out=t[3], in0=x1, in1=Sn, op=ALU.mult), reads=[psb, snb, dstb], writes=[T64[3]])
            if 'apply4' in dbg:
                return
            P.op("dve", lambda e: e.tensor_tensor(out=d3[:, :, 0:half], in0=t[0], in1=t[1], op=ALU.subtract), reads=[T64[0], T64[1]], writes=[dstb])
            P.op("dve", lambda e: e.tensor_tensor(out=d3[:, :, half:2 * half], in0=t[2], in1=t[3], op=ALU.add), reads=[T64[2], T64[3]], writes=[dstb])

        for (b_, a_) in ((ident, c_ident), (identr, c_identr), (identb, c_identb), (ones, c_ones), (iotaw, c_iotaw), (iotan, c_iotan), (invf, c_invf),
                         (invfi, c_invfi), (pow2, c_pow2), (tposs, tpos), (tcurs, tcur), (posA, posall), (posO, posown)):
            ld(b_, a_)

        kv_chunks = [("ka", C_KA, 0), ("ka", C_KA + 512, 4), ("va", C_VA, 0), ("va", C_VA + 512, 512),
                     ("kb", C_KB, 0), ("kb", C_KB + 512, 4), ("vb", C_VB, 0), ("vb", C_VB + 512, 512)]
        stbufs = []
        kisS = P.view("kisS", None)
        load_g(G, g_mix, D)
        for g in range(NG if (stage >= 2 and 'nophaseA' not in dbg) else 0):
            s0 = g * 512
            rope_tables(posA, posA[:, g * 4:(g + 1) * 4], 4)
            for i in range(4):
                X = rot(XT, "XT")
                ld(X, xall[s0 + i * 128:s0 + (i + 1) * 128, :])
                norm(X, X[:], G, g_mix, D, X[:], X)
                transposes(X, X[:], D, HTG[i], lambda j, nj, i=i: HTG[i][:, j:j + nj, :])
            for (kind, c0, aux) in kv_chunks:
                slot = load_w(w_in, 0, KC, c0, 512)
                if kind in ("ka", "kb"):
                    st = rot(KST, "cp")
                else:
                    st = rot(VST, "cp")
                for i in range(4):
                    ps = rot(pa, "pa")
                    mm(ps[:, :], ps, lambda k, i=i: HTG[i][:, k, :], lambda k, slot=slot: slot[:, k, :], KC, [HTG[i], slot])
                    if kind in ("ka", "kb"):
                        tm = rot(TM, "TM")
                        rope(ps, ps[:, :], 4, 128, 16, cosA[:, i, :], sinA[:, i, :], cosA, sinA, tm, tm[:, :])
                        transposes(tm, tm[:, :], 512, st, lambda j, nj, st=st, i=i: st[:, j:j + nj, i * 128:(i + 1) * 128])
                    else:
                        copy(st[:, i, :], ps[:, :], [ps], [st])
                if kind in ("ka", "kb"):
                    dst = KaT if kind == "ka" else KbT
                    P.dma("pool", dst[aux:aux + 4, :, s0:s0 + 512].rearrange("h d s -> d h s"), st[:, :, :], reads=[st], sembuf=st)
                    if kind == "kb":
                        P.op("dve", lambda e, st=st: e.tensor_reduce(out=sm[:, 32:40].rearrange("p (h b) -> p h b", h=4),
                                                                     in_=st[:, :, :].bitcast(F32).rearrange("p h (b t) -> p h b t", b=2), axis=AX.X, op=ALU.add),
                             reads=[st], writes=[sm])
                        P.op("dve", lambda e, aux=aux, g=g: e.tensor_scalar(out=kmT[:, aux:aux + 4, 2 * g:2 * g + 2], in0=sm[:, 32:40].rearrange("p (h b) -> p h b", h=4),
                                                                            scalar1=1.0 / 256, scalar2=None, op0=ALU.mult), reads=[sm], writes=[kmT])
                else:
                    dst = Va if kind == "va" else Vb
                    P.dma("pool", dst[s0:s0 + 512, aux:aux + 512].rearrange("(i p) c -> p i c", p=128), st[:, :, :], reads=[st], sembuf=st)
                if st not in stbufs:
                    stbufs.append(st)
            slot = load_w(w_in, 0, KC, C_KI, 64)
            kis = rot(KIc, "KI")
            for i in range(4):
                ps = rot(pa, "pa")
                mm(ps[:, 0:64], ps, lambda k, i=i: HTG[i][:, k, :], lambda k, slot=slot: slot[:, k, 0:64], KC, [HTG[i], slot])
                tm = rot(TM, "TM")
                rope(ps, ps[:, 0:64], 1, 64, 8, cosI[:, i, :], sinI[:, i, :], cosI, sinI, tm, tm[:, 0:64])
                psq = rot(pt, "pt")
                P.op("pe", lambda e, tm=tm, psq=psq: e.transpose(out=psq[0:64, 0:128], in_=tm[:, 0:64], identity=ident[:]), reads=[tm, ident], writes=[psq])
                copy(kis[0:64, i * 128:(i + 1) * 128], psq[0:64, 0:128], [psq], [kis])
            P.dma("pool", KiT[:, s0:s0 + 512], kis[0:64, :], reads=[kis], sembuf=kisS)
            if kisS not in stbufs:
                stbufs.append(kisS)
        if stage >= 2:
            P.wait_all_dma("sp", stbufs)

        P.alias(HTG, [MHT])
        load_g(G, g_mem_kv, D)
        for mt in range(2):
            X = rot(XT, "XT")
            ld(X, mem[mt * 128:(mt + 1) * 128, :])
            norm(X, X[:], G, g_mem_kv, D, X[:], X)
            transposes(X, X[:], D, MHT, lambda j, nj, mt=mt: MHT[:, j:j + nj, mt * 128:(mt + 1) * 128])
        for c in range(2):
            slot = load_w(w_mem_kv, 0, KC, c * 512, 512)
            for mt in range(2):
                ps = rot(pa, "pa")
                mm(ps[:, :], ps, lambda k, mt=mt: MHT[:, k, mt * 128:(mt + 1) * 128], lambda k, slot=slot: slot[:, k, :], KC, [MHT, slot])
                if c == 0:
                    tm = rot(TM, "TM")
                    copy(tm[:, :], ps[:, :], [ps], [tm])
                    transposes(tm, tm[:, :], 512, KMT, lambda j, nj, mt=mt: KMT[:, j:j + nj, mt * 128:(mt + 1) * 128])
                else:
                    copy(VM[:, mt, :], ps[:, :], [ps], [VM])
        P.alias([MHT] + KST + VST, [score, MG, AT, biasM] + KBf + VBf)

        sc_att = 1.0 / math.sqrt(128.0)

        def attention(QT, nheads, nchunks, kload, bias_fn, OT):
            items = [(h, s0) for h in range(nheads) for s0 in range(0, nchunks, ABATCH)]
            state = {}

            def emit_qk(h, s0):
                nb = min(ABATCH, nchunks - s0)
                L = rot(pL, "pL")
                vaps = []
                for jj in range(nb):
                    j = s0 + jj
                    kap, vap, kb_, vb_ = kload(h, j)
                    vaps.append((vap, vb_))
                    extra = bias_fn(h, j)
                    P.op("pe", lambda e, kap=kap, L=L, h=h, extra=extra, jj=jj: e.matmul(L[:, jj * TB:(jj + 1) * TB], lhsT=kap, rhs=QT[:, h, :], start=True, stop=(len(extra) == 0)),
                         reads=[kb_, QT], writes=[L])
                    for xi, (lh, rh, rb) in enumerate(extra):
                        P.op("pe", lambda e, lh=lh, rh=rh, L=L, xi=xi, extra=extra, jj=jj: e.matmul(L[:, jj * TB:(jj + 1) * TB], lhsT=lh, rhs=rh, start=False, stop=(xi == len(extra) - 1)),
                             reads=rb, writes=[L])
                state[(h, s0)] = (L, vaps, nb)

            def emit_pv(h, s0):
                L, vaps, nb = state.pop((h, s0))
                p_ = rot(PT, "PT")
                P.op("act", lambda e, p_=p_, L=L, nb=nb: e.activation(out=p_[:, 0:nb * TB], in_=L[:, 0:nb * TB], func=AF.Exp, scale=sc_att), reads=[L], writes=[p_])
                for jj in range(nb):
                    j = s0 + jj
                    vap, vb_ = vaps[jj]
                    P.op("pe", lambda e, vap=vap, p_=p_, j=j, jj=jj: e.matmul(pO[:, 0:TB], lhsT=vap, rhs=p_[:, jj * TB:(jj + 1) * TB], start=(j == 0), stop=(j == nchunks - 1)),
                         reads=[vb_, p_], writes=[pO])
                    P.op("pe", lambda e, p_=p_, j=j, jj=jj: e.matmul(pR[:, 0:TB], lhsT=ones[:, :], rhs=p_[:, jj * TB:(jj + 1) * TB], start=(j == 0), stop=(j == nchunks - 1)),
                         reads=[ones, p_], writes=[pR])
                if s0 + ABATCH >= nchunks:
                    REC = rot(TM, "TM")
                    P.op("dve", lambda e, REC=REC: e.reciprocal(out=REC[:, 0:TB], in_=pR[:, 0:TB]), reads=[pR], writes=[REC])
                    P.op("dve", lambda e, h=h, REC=REC: e.tensor_tensor(out=OT[:, h, :], in0=pO[:, 0:TB], in1=REC[:, 0:TB], op=ALU.mult), reads=[pO, REC], writes=[OT])

            for idx in range(len(items) + 1):
                if idx < len(items):
                    emit_qk(*items[idx])
                if idx >= 1:
                    emit_pv(*items[idx - 1])

        for r in range(NR):
            EXT = 512 * (r + 1)
            NCH = EXT // 512
            X = rot(XT, "XT")
            ld(X, xown[r * 128:(r + 1) * 128, :])
            if stage >= 2 and 'norounds2' not in dbg:
                load_g(G, g_mix, D)
                norm(X, X[:], G, g_mix, D, X[:], X)
                transposes(X, X[:], D, HT, lambda j, nj: HT[:, j:j + nj, :])
                if r % 4 == 0:
                    rope_tables(posO, posO[:, r:r + 4], 4)
                rc = r % 4
                load_g(Gq, g_cq, 512)
                slot = load_w(w_in, 0, KC, C_CQ, 512)
                ps = rot(pa, "pa")
                mm(ps[:, :], ps, lambda k: HT[:, k, :], lambda k, slot=slot: slot[:, k, :], KC, [HT, slot])
                tm = rot(TM, "TM")
                copy(tm[:, :], ps[:, :], [ps], [tm])
                norm(tm, tm[:, :], Gq, g_cq, 512, tm[:, :], tm)
                transposes(tm, tm[:, :], 512, CQT, lambda j, nj: CQT[:, j:j + nj, :])
                for c in range(2):
                    slot = load_w(w_uq, 0, 4, c * 512, 512)
                    ps = rot(pa, "pa")
                    mm(ps[:, :], ps, lambda k: CQT[:, k, :], lambda k, slot=slot: slot[:, k, :], 4, [CQT, slot])
                    tm = rot(TM, "TM")
                    rope(ps, ps[:, :], 4, 128, 16, cosA[:, rc, :], sinA[:, rc, :], cosA, sinA, tm, tm[:, :])
                    transposes(tm, tm[:, :], 512, QAT, lambda j, nj, c=c: QAT[:, 4 * c + j:4 * c + j + nj, :])
            if stage >= 2.2:
                slot = load_w(w_in, 0, KC, C_WI, 16)
                ps = rot(pa, "pa")
                mm(ps[:, 0:16], ps, lambda k: HT[:, k, :], lambda k, slot=slot: slot[:, k, 0:16], KC, [HT, slot])
                P.op("dve", lambda e, ps=ps: e.tensor_scalar(out=WI[:, :], in0=ps[:, 0:16], scalar1=1.0 / 32.0, scalar2=None, op0=ALU.mult), reads=[ps], writes=[WI])
                P.op("dve", lambda e: e.tensor_scalar(out=SG[:, :], in0=WI[:, :], scalar1=0.0, scalar2=2.0, op0=ALU.is_ge, op1=ALU.mult), reads=[WI], writes=[SG])
                P.op("dve", lambda e: e.tensor_scalar(out=SG[:, :], in0=SG[:, :], scalar1=-1.0, scalar2=None, op0=ALU.add), reads=[SG], writes=[SG])
                P.op("dve", lambda e: e.tensor_scalar(out=LO[:, :], in0=SG[:, :], scalar1=-1.0, scalar2=0.5e30, op0=ALU.add, op1=ALU.mult), reads=[SG], writes=[LO])
                P.op("dve", lambda e: e.tensor_scalar(out=HI[:, :], in0=SG[:, :], scalar1=1.0, scalar2=0.5e30, op0=ALU.add, op1=ALU.mult), reads=[SG], writes=[HI])
                for c in range(2):
                    slot = load_w(w_iq, 0, 4, c * 512, 512)
                    ps = rot(pa, "pa")
                    mm(ps[:, :], ps, lambda k: CQT[:, k, :], lambda k, slot=slot: slot[:, k, :], 4, [CQT, slot])
                    tm = rot(TM, "TM")
                    rope(ps, ps[:, :], 8, 64, 8, cosI[:, rc, :], sinI[:, rc, :], cosI, sinI, tm, tm[:, :])
                    P.op("dve", lambda e, tm=tm, c=c: e.tensor_tensor(out=tm[:, :].rearrange("p (h d) -> p h d", h=8), in0=tm[:, :].rearrange("p (h d) -> p h d", h=8),
                                                                   in1=WI[:, 8 * c:8 * c + 8].unsqueeze(2).broadcast_to([128, 8, 64]), op=ALU.mult), reads=[tm, WI], writes=[tm])
                    transposes(tm, tm[:, :], 512, QIT, lambda j, nj, c=c: QIT[:, 4 * c + j:4 * c + j + nj, :])
                P.op("dve", lambda e, r=r: e.tensor_scalar(out=sm[:, 20:21], in0=tposs[:, r:r + 1], scalar1=-float(512 * r), scalar2=None, op0=ALU.add), reads=[tposs], writes=[sm])
                P.op("dve", lambda e: e.tensor_scalar(out=CB16[:, :], in0=iotaw[:, :], scalar1=sm[:, 20:21], scalar2=MB, op0=ALU.is_gt, op1=ALU.mult), reads=[iotaw, sm], writes=[CB16])
                for c in range(NCH):
                    kc_ = rot(KIc, "KI")
                    P.dma("sp", kc_[0:64, :], KiT[:, c * 512:(c + 1) * 512], writes=[kc_])
                    P.dma("sp", kc_[64:128, :], KiT[:, c * 512:(c + 1) * 512], writes=[kc_])
                    prev = None
                    for h in range(17):
                        if h < 16:
                            pr, hf = h // 2, h % 2
                            L = rot(pL, "pL")
                            P.op("pe", lambda e, L=L, pr=pr, hf=hf, kc_=kc_: e.matmul(L[:, :], lhsT=QIT[64 * hf:64 * hf + 64, pr, :], rhs=kc_[64 * hf:64 * hf + 64, :], start=True, stop=True),
                                 reads=[QIT, kc_], writes=[L])
                            rb = rot(RB, "RB")
                            P.op("dve", lambda e, L=L, rb=rb, h=h: e.tensor_scalar(out=rb[:, :], in0=L[:, :], scalar1=LO[:, h:h + 1], scalar2=HI[:, h:h + 1], op0=ALU.max, op1=ALU.min),
                                 reads=[L, LO, HI], writes=[rb])
                        if prev is not None:
                            ph, prb = prev
                            P.op("pe", lambda e, prb=prb, ph=ph: e.matmul(pO[:, :], lhsT=identr[:, :], rhs=prb[:, :], start=(ph == 0), stop=(ph == 15)), reads=[identr, prb], writes=[pO])
                        prev = (h, rb) if h < 16 else None
                    if c == NCH - 1:
                        CB = rot(TM, "TM")
                        P.op("dve", lambda e, CB=CB: e.tensor_scalar(out=CB[:, :], in0=iotaw[:, :], scalar1=sm[:, 20:21], scalar2=NEG, op0=ALU.is_gt, op1=ALU.mult), reads=[iotaw, sm], writes=[CB])
                        P.op("dve", lambda e, c=c, CB=CB: e.tensor_tensor(out=scoreW[:, c * 512:(c + 1) * 512], in0=pO[:, :], in1=CB[:, :], op=ALU.add), reads=[pO, CB], writes=[score])
                    else:
                        P.op("act", lambda e, c=c: e.activation(out=scoreW[:, c * 512:(c + 1) * 512], in_=pO[:, :], func=AF.Copy), reads=[pO], writes=[score])
                NIT = 36 if r == 0 else 22
                if r == 0:
                    P.op("dve", lambda e: e.memset(sm[:, 21:22], -1.0e4), writes=[sm])
                else:
                    P.op("dve", lambda e: e.tensor_reduce(out=sm[:, 21:22], in_=score[:, 0:512], axis=AX.X, op=ALU.min), reads=[score], writes=[sm])
                P.op("dve", lambda e, EXT=EXT: e.tensor_reduce(out=sm[:, 22:23], in_=score[:, 0:EXT], axis=AX.X, op=ALU.max), reads=[score], writes=[sm])
                P.op("dve", lambda e: e.tensor_tensor(out=sm[:, 23:24], in0=sm[:, 22:23], in1=sm[:, 21:22], op=ALU.subtract), reads=[sm], writes=[sm])
                P.op("dve", lambda e: e.tensor_scalar(out=Wk[:, :], in0=pow2[:, :], scalar1=sm[:, 23:24], scalar2=None, op0=ALU.mult), reads=[pow2, sm], writes=[Wk])
                P.op("dve", lambda e: e.tensor_tensor(out=sm[:, 24:25], in0=sm[:, 21:22], in1=Wk[:, 0:1], op=ALU.add), reads=[sm, Wk], writes=[sm])
                jk = biasM
                for k in range(NIT):
                    P.op("dve", lambda e, EXT=EXT: e.tensor_scalar(out=jk[:, 0:EXT], in0=score[:, 0:EXT], scalar1=sm[:, 24:25], scalar2=None, op0=ALU.is_ge, op1=ALU.add,
                                                                   accum_out=sm[:, 25:26]), reads=[score, sm], writes=[jk, sm])
                    P.op("dve", lambda e, k=k: e.scalar_tensor_tensor(out=sm[:, 26:27], in0=sm[:, 25:26], scalar=256.0, in1=Wk[:, k:k + 1], op0=ALU.is_ge, op1=ALU.mult),
                         reads=[sm, Wk], writes=[sm])
                    P.op("dve", lambda e, k=k: e.scalar_tensor_tensor(out=sm[:, 24:25], in0=sm[:, 26:27], scalar=Wk[:, k + 1:k + 2], in1=sm[:, 24:25], op0=ALU.subtract, op1=ALU.add),
                         reads=[sm, Wk], writes=[sm])
                P.op("dve", lambda e, NIT=NIT: e.tensor_tensor(out=sm[:, 27:28], in0=sm[:, 24:25], in1=Wk[:, NIT:NIT + 1], op=ALU.subtract), reads=[sm, Wk], writes=[sm])
                P.op("dve", lambda e, EXT=EXT: e.tensor_scalar(out=biasM[:, 0:EXT], in0=score[:, 0:EXT], scalar1=sm[:, 27:28], scalar2=MB, op0=ALU.is_lt, op1=ALU.mult),
                     reads=[score, sm], writes=[biasM])

            if stage >= 2.4:
                def kload_a(h, j, EXT=EXT):
                    jj = j % 8
                    if jj == 0:
                        n = min(1024, EXT - j * 128)
                        kb_ = rot(KBf, "KB"); vb_ = rot(VBf, "VB")
                        kload_a.cur = (kb_, vb_)
                        P.dma("sp", kb_[:, 0:n], KaT[h, :, j * 128:j * 128 + n], writes=[kb_])
                        P.dma("sp", vb_[:, 0:n // 128, :], Va[j * 128:j * 128 + n, h * 128:(h + 1) * 128].rearrange("(j p) d -> p j d", p=128), writes=[vb_])
                    kb_, vb_ = kload_a.cur
                    return kb_[:, jj * 128:(jj + 1) * 128], vb_[:, jj, :], kb_, vb_

                attention(QAT, 8, EXT // 128, kload_a,
                          lambda h, j: [(biasM[:, j * 128:(j + 1) * 128], identb[:, :], [biasM, identb])], OAT)

            if stage >= 2.6:
                for c in range(2):
                    slot = load_w(w_in, 0, KC, C_QB + c * 512, 512)
                    ps = rot(pa, "pa")
                    mm(ps[:, :], ps, lambda k: HT[:, k, :], lambda k, slot=slot: slot[:, k, :], KC, [HT, slot])
                    tm = rot(TM, "TM")
                    rope(ps, ps[:, :], 4, 128, 16, cosA[:, rc, :], sinA[:, rc, :], cosA, sinA, tm, tm[:, :])
                    transposes(tm, tm[:, :], 512, QBT, lambda j, nj, c=c: QBT[:, 4 * c + j:4 * c + j + nj, :])
                for h in range(8):
                    P.op("pe", lambda e, h=h: e.matmul(pR[:, h * 32:h * 32 + NBLK], lhsT=QBT[:, h, :], rhs=kmT[:, h, 0:NBLK], start=True, stop=True), reads=[QBT, kmT], writes=[pR])
                P.op("dve", lambda e, r=r: e.tensor_scalar(out=NM[:, :], in0=iotan[:, :], scalar1=tcurs[:, r:r + 1], scalar2=NEG, op0=ALU.is_ge, op1=ALU.mult), reads=[iotan, tcurs], writes=[NM])
                P.op("dve", lambda e, r=r: e.tensor_scalar(out=EQ[:, :], in0=iotan[:, :], scalar1=tcurs[:, r:r + 1], scalar2=None, op0=ALU.not_equal), reads=[iotan, tcurs], writes=[EQ])
                P.op("dve", lambda e: e.memset(GS[:, :, :], NEG), writes=[GS])
                P.op("dve", lambda e: e.tensor_tensor(out=GS[:, :, 0:NBLK], in0=pR[:, 0:256].rearrange("p (h n) -> p h n", h=8)[:, :, 0:NBLK],
                                                      in1=NM[:, 0:NBLK].unsqueeze(1).broadcast_to([128, 8, NBLK]), op=ALU.add), reads=[pR, NM], writes=[GS])
                for h in range(8):
                    P.op("dve", lambda e, h=h: e.max(out=M8[:, h, :], in_=GS[:, h, :]), reads=[GS], writes=[M8])
                P.op("dve", lambda e: e.tensor_scalar(out=TH[:, :], in0=M8[:, :, 2], scalar1=-1.0e29, scalar2=None, op0=ALU.max), reads=[M8], writes=[TH])
                P.op("dve", lambda e: e.tensor_tensor(out=GB[:, :, :], in0=GS[:, :, :], in1=TH[:, :].unsqueeze(2).broadcast_to([128, 8, 32]), op=ALU.is_lt), reads=[GS, TH], writes=[GB])
                P.op("dve", lambda e: e.scalar_tensor_tensor(out=GB[:, :, :], in0=GB[:, :, :], scalar=MB, in1=EQ[:, :].unsqueeze(1).broadcast_to([128, 8, 32]), op0=ALU.mult, op1=ALU.mult),
                     reads=[GB, EQ], writes=[GB])
                for h in range(8):
                    psq = rot(pt, "pt")
                    P.op("pe", lambda e, h=h, psq=psq: e.transpose(out=psq[0:32, 0:128], in_=GB[:, h, :], identity=ident[:]), reads=[GB, ident], writes=[psq])
                    copy(GBT[:, h, :], psq[0:32, 0:128], [psq], [GBT])

                def kload_b(h, j, EXT=EXT):
                    jj = j % 8
                    if jj == 0:
                        n = min(1024, EXT - j * 128)
                        kb_ = rot(KBf, "KB"); vb_ = rot(VBf, "VB")
                        kload_b.cur = (kb_, vb_)
                        P.dma("sp", kb_[:, 0:n], KbT[h, :, j * 128:j * 128 + n], writes=[kb_])
                        P.dma("sp", vb_[:, 0:n // 128, :], Vb[j * 128:j * 128 + n, h * 128:(h + 1) * 128].rearrange("(j p) d -> p j d", p=128), writes=[vb_])
                    kb_, vb_ = kload_b.cur
                    return kb_[:, jj * 128:(jj + 1) * 128], vb_[:, jj, :], kb_, vb_

                def bias_b(h, j, r=r):
                    n = j // 2
                    ex = [(identb[0:32, n:n + 1].broadcast_to([32, 128]), GBT[:, h, :], [identb, GBT])]
                    if j >= 4 * r:
                        w = j - 4 * r
                        ex.append((CB16[:, w * 128:(w + 1) * 128], identb[:, :], [CB16, identb]))
                    return ex

                attention(QBT, 8, EXT // 128, kload_b, bias_b, OBT)

            if stage < 3:
                ld(X1, xown[r * 128:(r + 1) * 128, :])
            if stage >= 3:
                for c in range(4):
                    for bi, (OT_, wo, cg) in enumerate(((OAT, w_dsa_o, C_GA), (OBT, w_moba_o, C_GB))):
                        slot = load_w(wo, 0, 8, c * 512, 512)
                        ps = rot(pa, "pa")
                        mm(ps[:, :], ps, lambda k, OT_=OT_: OT_[:, k, :], lambda k, slot=slot: slot[:, k, :], 8, [OT_, slot])
                        y = rot(TM, "TM")
                        P.op("act", lambda e, y=y, ps=ps: e.activation(out=y[:, :], in_=ps[:, :], func=AF.Copy), reads=[ps], writes=[y])
                        slot2 = load_w(w_in, 0, KC, cg + c * 512, 512)
                        ps2 = rot(pa, "pa")
                        mm(ps2[:, :], ps2, lambda k: HT[:, k, :], lambda k, slot2=slot2: slot2[:, k, :], KC, [HT, slot2])
                        sg = rot(TM, "TM")
                        P.op("act", lambda e, sg=sg, ps2=ps2: e.activation(out=sg[:, :], in_=ps2[:, :], func=AF.Sigmoid), reads=[ps2], writes=[sg])
                        if bi == 0:
                            P.op("dve", lambda e, y=y, sg=sg, c=c: e.tensor_tensor(out=MGW[:, c * 512:(c + 1) * 512], in0=y[:, :], in1=sg[:, :], op=ALU.mult), reads=[y, sg], writes=[MG])
                        else:
                            P.op("dve", lambda e, y=y, sg=sg: e.tensor_tensor(out=y[:, :], in0=y[:, :], in1=sg[:, :], op=ALU.mult), reads=[y, sg], writes=[y])
                            P.op("dve", lambda e, y=y, c=c: e.tensor_tensor(out=MGW[:, c * 512:(c + 1) * 512], in0=MG[:, c * 512:(c + 1) * 512], in1=y[:, :], op=ALU.add), reads=[y, MG], writes=[MG])
                transposes(MG, MG[:, :], D, HT, lambda j, nj: HT[:, j:j + nj, :])
                ld(X1, xown[r * 128:(r + 1) * 128, :])
                for c in range(4):
                    slot = load_w(w_out, 0, KC, c * 512, 512)
                    ps = rot(pa, "pa")
                    mm(ps[:, :], ps, lambda k: HT[:, k, :], lambda k, slot=slot: slot[:, k, :], KC, [HT, slot])
                    P.op("dve", lambda e, ps=ps, c=c: e.tensor_tensor(out=X1[:, c * 512:(c + 1) * 512], in0=ps[:, :], in1=X1[:, c * 512:(c + 1) * 512], op=ALU.add), reads=[ps, X1], writes=[X1])

            load_g(G, g_mem_q, D)
            norm(X1, X1[:, :], G, g_mem_q, D, MGW, MG)
            transposes(MG, MG[:, :], D, HT, lambda j, nj: HT[:, j:j + nj, :])
            slot = load_w(w_mem_q, 0, KC, 0, 512)
            ps = rot(pa, "pa")
            mm(ps[:, :], ps, lambda k: HT[:, k, :], lambda k, slot=slot: slot[:, k, :], KC, [HT, slot])
            tm = rot(TM, "TM")
            copy(tm[:, :], ps[:, :], [ps], [tm])
            transposes(tm, tm[:, :], 512, QMT, lambda j, nj: QMT[:, j:j + nj, :])
            attention(QMT, 4, 2, lambda h, j: (KMT[:, h, j * 128:(j + 1) * 128], VM[:, j, h * 128:(h + 1) * 128], KMT, VM), lambda h, j: [], OMT)
            for c in range(4):
                slot = load_w(w_mem_o, 0, 4, c * 512, 512)
                ps = rot(pa, "pa")
                mm(ps[:, :], ps, lambda k: OMT[:, k, :], lambda k, slot=slot: slot[:, k, :], 4, [OMT, slot])
                P.op("dve", lambda e, ps=ps, c=c: e.tensor_tensor(out=X1[:, c * 512:(c + 1) * 512], in0=ps[:, :], in1=X1[:, c * 512:(c + 1) * 512], op=ALU.add), reads=[ps, X1], writes=[X1])

            pair = NR >= 2
            if pair and r % 2 == 0:
                P.dma("pool", x2s, X1[:, :], reads=[X1], writes=[x2sV], sembuf=x2sS)
                continue
            tiles = [(r, X1, MG, MGW)]
            if pair:
                P.alias([biasM], [X2e])
                P.dma("sp", X2e[:, :], x2s, reads=[x2sV], writes=[X2e])
                tiles = [(r - 1, X2e, MG, MGW), (r, X1, MG2, MG2W)]
            nt = len(tiles)
            P.alias(KBf + VBf, [HT2])
            load_g(G, g_ff, D)
            for m, (row, Xb, NB, NBW) in enumerate(tiles):
                norm(Xb, Xb[:, :], G, g_ff, D, NBW, NB)
                transposes(NB, NB[:, :], D, HT2, lambda j, nj, m=m: HT2[:, j:j + nj, m * 128:(m + 1) * 128])
            for qd in range(4):
                for j in range(4):
                    slot = load_w(w_ff1, 0, KC, qd * 2048 + j * 512, 512)
                    for fs in range(4):
                        ps = rot(pa, "pa")
                        mm(ps[:, 0:nt * 128], ps, lambda k, slot=slot, fs=fs: slot[:, k, fs * 128:(fs + 1) * 128], lambda k: HT2[:, k, 0:nt * 128], KC, [HT2, slot])
                        tm = rot(TM, "TM")
                        P.op("act", lambda e, tm=tm, ps=ps: e.activation(out=tm[:, 0:nt * 128], in_=ps[:, 0:nt * 128], func=AF.Relu), reads=[ps], writes=[tm])
                        P.op("dve", lambda e, tm=tm, j=j, fs=fs: e.tensor_tensor(out=AT2[:, j * 4 + fs, 0:nt * 128], in0=tm[:, 0:nt * 128], in1=tm[:, 0:nt * 128], op=ALU.mult), reads=[tm], writes=[AT2])
                for c in range(4):
                    slot = wslot()
                    P.dma("sp", slot[:, :, :], w_ff2[qd * 2048:(qd + 1) * 2048, c * 512:(c + 1) * 512].rearrange("(k p) n -> p k n", p=128), writes=[slot])
                    for m, (row, Xb, NB, NBW) in enumerate(tiles):
                        ps = rot(pa, "pa")
                        mm(ps[:, :], ps, lambda k, m=m: AT2[:, k, m * 128:(m + 1) * 128], lambda k, slot=slot: slot[:, k, :], KC, [AT2, slot])
                        P.op("dve", lambda e, ps=ps, c=c, Xb=Xb: e.tensor_tensor(out=Xb[:, c * 512:(c + 1) * 512], in0=ps[:, :], in1=Xb[:, c * 512:(c + 1) * 512], op=ALU.add), reads=[ps, Xb], writes=[Xb])

            load_g(G, g_final, D)
            for m, (row, Xb, NB, NBW) in enumerate(tiles):
                norm(Xb, Xb[:, :], G, g_final, D, None, None, final_row=row)
            P.alias([HT2], KBf + VBf)
            if pair:
                P.alias([X2e], [biasM])
        P.wait_all_dma("pool", TM)
        P.emit()
    return nc


def host_inputs(inputs, S):
    NR = S // 512
    f32 = np.float32
    consts = {
        "c_ident": np.eye(128, dtype=f32),
        "c_identr": np.eye(128, dtype=f32),
        "c_identb": np.eye(128, dtype=f32).astype(ml_dtypes.bfloat16),
        "c_ones": np.ones((128, 128), f32),
        "c_iotaw": np.tile(np.arange(512, dtype=f32)[None, :], (128, 1)),
        "c_iotan": np.tile(np.arange(32, dtype=f32)[None, :], (128, 1)),
        "c_invf": np.tile((np.float32(500000.0) ** (-np.arange(16, dtype=f32) * f32(2.0 / 32)))[None, :], (128, 1)).astype(f32),
        "c_invfi": np.tile((np.float32(500000.0) ** (-np.arange(8, dtype=f32) * f32(2.0 / 16)))[None, :], (128, 1)).astype(f32),
        "c_pow2": np.tile((0.5 ** np.arange(1, 49, dtype=np.float64)).astype(f32)[None, :], (128, 1)),
    }
    wnames = ["w_in", "w_uq", "w_iq", "w_dsa_o", "w_moba_o", "w_out", "w_mem_q", "w_mem_kv", "w_mem_o", "w_ff1", "w_ff2"]
    gnames = ["g_mix", "g_cq", "g_mem_q", "g_mem_kv", "g_ff"]
    shared = dict(consts)
    for n in wnames:
        shared[n] = np.ascontiguousarray(np.asarray(inputs[n], f32)[0])
    for n in gnames:
        shared[n] = np.ascontiguousarray(np.asarray(inputs[n], f32)[0][None, :])
    shared["g_final"] = np.ascontiguousarray(np.asarray(inputs["g_final"], f32)[None, :])
    x = np.asarray(inputs["x"], f32)
    pos = np.asarray(inputs["positions"], np.int32)
    memv = np.asarray(inputs["mem"], f32)
    maps, owners = [], []
    for c in range(8):
        b, q = c // 4, c % 4
        blks = [4 * r + (q if r % 2 == 0 else 3 - q) for r in range(NR)]
        rows = np.concatenate([np.arange(bk * 128, (bk + 1) * 128) for bk in blks])
        m = dict(shared)
        m["xall"] = np.ascontiguousarray(x[b])
        m["posall"] = np.ascontiguousarray(pos[b].reshape(S // 128, 128).T)
        m["xown"] = np.ascontiguousarray(x[b][rows])
        po = np.zeros((128, max(4, NR)), np.int32)
        po[:, :NR] = pos[b][rows].reshape(NR, 128).T
        m["posown"] = po
        m["tpos"] = np.ascontiguousarray(rows.reshape(NR, 128).T.astype(f32))
        m["tcur"] = np.ascontiguousarray((rows // 256).reshape(NR, 128).T.astype(f32))
        m["mem"] = np.ascontiguousarray(memv[b])
        maps.append(m)
        owners.append((b, rows))
    return maps, owners


_NC_CACHE = {}


def kernel(**inputs):
    S = int(np.asarray(inputs["x"]).shape[1])
    if S not in _NC_CACHE:
        _NC_CACHE[S] = build(S)
    nc = _NC_CACHE[S]
    maps, owners = host_inputs(inputs, S)
    res = run_bass_kernel_spmd(nc, maps, core_ids=list(range(8)))
    outp = np.zeros((2, S, D), np.float32)
    for c, (b, rows) in enumerate(owners):
        outp[b, rows] = np.asarray(res.results[c]["out"], np.float32)
    return outp
```

```python
import math
import numpy as np
import ml_dtypes
import concourse.bass as bass
import concourse.mybir as mybir
from concourse.bass_utils import run_bass_kernel_spmd
from contextlib import ExitStack

F32 = mybir.dt.float32
F32R = mybir.dt.float32r
BF16 = mybir.dt.bfloat16
I32 = mybir.dt.int32
AF = mybir.ActivationFunctionType
ALU = mybir.AluOpType
AX = mybir.AxisListType
ENG = ["pe", "act", "dve", "pool", "sp"]

D = 2048
KC = 16
TB = 128
NEG = -1.0e30
MB = -30000.0
ABATCH = 4
EPS = 1e-6
PI = math.pi


class Buf:
    def __init__(self, name, t=None):
        self.name = name
        self.t = t
        self.writer = None
        self.readers = {}
        self.dma_sem = None
        self.dma_cnt = 0

    def __getitem__(self, k):
        return self.t[k]


class Op:
    __slots__ = ("eng", "fn", "deps", "dmadeps", "signal", "count", "dma", "dmasem")

    def __init__(self, eng, fn):
        self.eng = eng
        self.fn = fn
        self.deps = set()
        self.dmadeps = {}
        self.signal = False
        self.count = 0
        self.dma = False
        self.dmasem = None


class Prog:
    def __init__(self, nc):
        self.nc = nc
        self.ops = {e: [] for e in ENG}
        self.es = ExitStack()
        self.semh = {}

    def sbuf(self, name, shape, dt=F32):
        return Buf(name, self.es.enter_context(self.nc.sbuf_tensor(name, list(shape), dt)))

    def psum(self, name, shape, dt=F32):
        return Buf(name, self.es.enter_context(self.nc.psum_tensor(name, list(shape), dt)))

    def view(self, name, ap):
        return Buf(name, ap)

    def _new_sem(self, name):
        h = self.es.enter_context(self.nc.semaphore(name))
        self.semh[name] = h
        return name

    def _dep(self, op, w):
        if w[0] == "dma":
            _, sem, val = w
            if op.dmadeps.get(sem, 0) < val:
                op.dmadeps[sem] = val
        else:
            if op.eng == "pe" and w[0] == "pe":
                return
            op.deps.add(w)

    def op(self, eng, fn, reads=(), writes=()):
        op = Op(eng, fn)
        me = (eng, len(self.ops[eng]))
        for b in reads:
            if b.writer is not None:
                self._dep(op, b.writer)
        for b in writes:
            if b.writer is not None:
                self._dep(op, b.writer)
            for r in b.readers.values():
                self._dep(op, r)
        for b in reads:
            b.readers[eng] = me
        for b in writes:
            b.writer = me
            b.readers = {}
        self.ops[eng].append(op)
        return op

    def dma(self, eng, out_ap, in_ap, reads=(), writes=(), sembuf=None):
        sb = sembuf if sembuf is not None else (writes[0] if writes else reads[0])
        if sb.dma_sem is None:
            sb.dma_sem = self._new_sem("d_" + sb.name)
        op = Op(eng, None)
        for b in reads:
            if b.writer is not None:
                self._dep(op, b.writer)
        for b in writes:
            if b.writer is not None and not (b.writer[0] == "dma" and not b.readers):
                self._dep(op, b.writer)
            for r in b.readers.values():
                self._dep(op, r)
        sb.dma_cnt += 16
        tok = ("dma", sb.dma_sem, sb.dma_cnt)
        for b in reads:
            b.readers[("dma", sb.dma_sem)] = tok
        for b in writes:
            b.writer = tok
            b.readers = {}
        op.dma = True
        op.dmasem = sb.dma_sem
        op.fn = lambda e, o=out_ap, i=in_ap: e.dma_start(out=o, in_=i)
        self.ops[eng].append(op)
        return op

    def wait_all_dma(self, eng, bufs):
        op = Op(eng, lambda e: e.nop())
        for b in bufs:
            if b.dma_sem is not None:
                op.dmadeps[b.dma_sem] = b.dma_cnt
        self.ops[eng].append(op)
        return op

    def alias(self, old, new):
        merged = {}
        for b in old:
            toks = list(b.readers.items())
            if b.writer is not None:
                w = b.writer
                toks.append(((("dma", w[1]) if w[0] == "dma" else w[0]), w))
            for k, t in toks:
                cur = merged.get(k)
                if cur is None or (t[0] == "dma" and t[2] > cur[2]) or (t[0] != "dma" and t[1] > cur[1]):
                    merged[k] = t
        for b in new:
            b.writer = None
            b.readers = dict(merged)

    def emit(self):
        nc = self.nc
        for e in ENG:
            for op in self.ops[e]:
                for (de, di) in op.deps:
                    self.ops[de][di].signal = True
        esem = {}
        for e in ENG:
            c = 0
            for op in self.ops[e]:
                if op.signal and not op.dma:
                    c += 1
                    op.count = c
            if c > 0:
                esem[e] = self._new_sem("s_" + e)
        engmap = {"pe": "tensor", "act": "scalar", "dve": "vector", "pool": "gpsimd", "sp": "sync"}
        prog = self

        def make(e):
            def body(eng):
                waited = {}
                for op in prog.ops[e]:
                    need = {}
                    for (de, di) in op.deps:
                        d = prog.ops[de][di]
                        s = esem[de]
                        if need.get(s, 0) < d.count:
                            need[s] = d.count
                    for s, v in op.dmadeps.items():
                        if need.get(s, 0) < v:
                            need[s] = v
                    for s, v in need.items():
                        if waited.get(s, 0) < v:
                            eng.wait_ge(prog.semh[s], v)
                            waited[s] = v
                    ins = op.fn(eng)
                    if op.dma:
                        ins.then_inc(prog.semh[op.dmasem], 16)
                    elif op.signal:
                        ins.then_inc(prog.semh[esem[e]], 1)
            return body

        with nc.Block() as block:
            for e in ENG:
                if self.ops[e]:
                    getattr(block, engmap[e])(make(e))


C_CQ, C_KA, C_VA, C_KI, C_WI, C_QB, C_KB, C_VB, C_GA, C_GB = 0, 512, 1536, 2560, 2624, 2640, 3664, 4688, 5712, 7760


def build(S, stage=99, dbg=()):
    NR = S // 512
    NG = S // 512
    NBLK = S // 256
    nc = bass.Bass("TRN2", target_bir_lowering=False)
    nc.dge_precook = False

    def din(name, shape, dt=F32):
        return nc.dram_tensor(name, list(shape), dt, kind="ExternalInput").ap()

    xall = din("xall", [S, D]); posall = din("posall", [128, S // 128], I32)
    xown = din("xown", [NR * TB, D]); posown = din("posown", [128, max(4, NR)], I32)
    tpos = din("tpos", [128, NR]); tcur = din("tcur", [128, NR])
    mem = din("mem", [256, D])
    w_in = din("w_in", [D, 9808], F32R); w_uq = din("w_uq", [512, 1024], F32R); w_iq = din("w_iq", [512, 1024], F32R)
    w_dsa_o = din("w_dsa_o", [1024, D], F32R); w_moba_o = din("w_moba_o", [1024, D], F32R)
    w_out = din("w_out", [D, D], F32R); w_mem_q = din("w_mem_q", [D, 512], F32R)
    w_mem_kv = din("w_mem_kv", [D, 1024], F32R); w_mem_o = din("w_mem_o", [512, D], F32R)
    w_ff1 = din("w_ff1", [D, 8192], F32R); w_ff2 = din("w_ff2", [8192, D], F32R)
    g_mix = din("g_mix", [1, D]); g_cq = din("g_cq", [1, 512]); g_mem_q = din("g_mem_q", [1, D])
    g_mem_kv = din("g_mem_kv", [1, D]); g_ff = din("g_ff", [1, D]); g_final = din("g_final", [1, D])
    c_ident = din("c_ident", [128, 128]); c_identr = din("c_identr", [128, 128], F32R); c_identb = din("c_identb", [128, 128], BF16)
    c_ones = din("c_ones", [128, 128], F32R); c_iotaw = din("c_iotaw", [128, 512]); c_iotan = din("c_iotan", [128, 32])
    c_invf = din("c_invf", [128, 16]); c_invfi = din("c_invfi", [128, 8]); c_pow2 = din("c_pow2", [128, 48])
    out = nc.dram_tensor("out", [NR * TB, D], F32, kind="ExternalOutput").ap()
    KaT = nc.dram_tensor("KaT", [8, 128, S], F32R, kind="Internal").ap()
    KbT = nc.dram_tensor("KbT", [8, 128, S], F32R, kind="Internal").ap()
    Va = nc.dram_tensor("Va", [S, 1024], F32R, kind="Internal").ap()
    Vb = nc.dram_tensor("Vb", [S, 1024], F32R, kind="Internal").ap()
    KiT = nc.dram_tensor("KiT", [64, S], F32R, kind="Internal").ap()
    x2s = nc.dram_tensor("x2s", [128, D], F32, kind="Internal").ap()

    P = Prog(nc)
    with P.es:
        Wt = [P.sbuf("W%d" % i, [128, KC, 512], F32R) for i in range(2)]
        wi = [0]

        def wslot():
            wi[0] ^= 1
            return Wt[wi[0]]

        XT = [P.sbuf("XT0", [128, D])]
        G = P.sbuf("G", [128, D]); Gq = G
        HTt = P.sbuf("HT", [128, KC, TB], F32R)
        HT = P.view("HTv", HTt.t)
        A1 = P.sbuf("A1", [128, 8192])
        A2 = P.sbuf("A2", [128, 4096])
        A3 = P.sbuf("A3", [128, 4096])
        ident = P.sbuf("ident", [128, 128]); identb = P.sbuf("identb", [128, 128], BF16)
        ones = P.sbuf("ones", [128, 128], F32R); iotaw = P.sbuf("iotaw", [128, 512]); iotan = P.sbuf("iotan", [128, 32])
        invf = P.sbuf("invf", [128, 16]); invfi = P.sbuf("invfi", [128, 8]); pow2 = P.sbuf("pow2", [128, 48])
        kmT = P.sbuf("kmT", [128, 8, 32], F32R)
        KMT = P.sbuf("KMT", [128, 4, 256], F32R); VM = P.sbuf("VM", [128, 2, 512], F32R)
        QAT = P.sbuf("QAT", [128, 8, TB], F32R); QIT = P.sbuf("QIT", [128, 8, TB], F32R)
        QBT = QIT; CQT = P.sbuf("CQT", [128, 4, TB], F32R)
        OAT = P.sbuf("OAT", [128, 8, TB], F32R); OBT = P.sbuf("OBT", [128, 8, TB], F32R)
        QMT = CQT; OMT = QAT
        PT = [P.sbuf("PT%d" % i, [128, ABATCH * TB], F32R) for i in range(2)]
        RB = [P.sbuf("RB%d" % i, [128, 512], F32R) for i in range(2)]
        KIc = [P.sbuf("KIc%d" % i, [128, 512], F32R) for i in range(1)]
        identr = P.sbuf("identr", [128, 128], F32R); LO = P.sbuf("LO", [128, 16]); HI = P.sbuf("HI", [128, 16])
        CB16 = P.sbuf("CB16", [128, 512], BF16)
        TM = [P.sbuf("TM%d" % i, [128, 512]) for i in range(3)]
        T64 = [P.sbuf("T64%d" % i, [128, 64]) for i in range(4)]
        GBT = P.sbuf("GBT", [32, 8, TB], BF16)
        GS = P.sbuf("GS", [128, 8, 32]); GB = GS; NM = P.sbuf("NM", [128, 32]); EQ = P.sbuf("EQ", [128, 32])
        M8 = P.sbuf("M8", [128, 8, 8]); TH = P.sbuf("TH", [128, 8])
        sm = P.sbuf("sm", [128, 64])
        Wk = P.sbuf("Wk", [128, 48])
        posf = P.sbuf("posf", [128, 8]); posA = P.sbuf("posA", [128, S // 128], I32); posO = P.sbuf("posO", [128, max(4, NR)], I32)
        ang = P.sbuf("ang", [128, 4, 16]); kint = P.sbuf("kint", [128, 4, 16], I32); kf = P.sbuf("kf", [128, 4, 16]); mk = P.sbuf("mk", [128, 4, 16])
        cosA = P.sbuf("cosA", [128, 4, 16]); sinA = P.sbuf("sinA", [128, 4, 16])
        cosI = P.sbuf("cosI", [128, 4, 8]); sinI = P.sbuf("sinI", [128, 4, 8])
        WI = P.sbuf("WI", [128, 16]); SG = P.sbuf("SG", [128, 16])
        tposs = P.sbuf("tposs", [128, NR]); tcurs = P.sbuf("tcurs", [128, NR])
        pa = [P.psum("pa%d" % i, [128, 512]) for i in range(2)]
        pt = [P.psum("pt%d" % i, [128, 512]) for i in range(2)]
        pL = [P.psum("pL%d" % i, [128, 512]) for i in range(2)]
        pO = P.psum("pO", [128, 512]); pR = P.psum("pR", [128, 512])
        cnt = {"KB": 0, "VB": 0, "pa": 0, "pt": 0, "pL": 0, "TM": 0, "PT": 0, "RB": 0, "KI": 0, "XT": 0, "cp": 0}

        def rot(lst, key):
            cnt[key] += 1
            return lst[cnt[key] % len(lst)]

        a1 = A1.t[:]
        a2 = A2.t[:]
        a3 = A3.t[:]
        HTG = [P.view("HTG%d" % i, a1[:, i * 2048:(i + 1) * 2048].bitcast(F32R).rearrange("p (k t) -> p k t", k=KC)) for i in range(4)]
        score = P.view("score", a1)
        scoreW = a1.bitcast(F32R)
        MG = P.view("MG", a1[:, 0:2048]); MGW = a1[:, 0:2048].bitcast(F32R); X1 = XT[0]
        AT = P.view("AT", a1[:, 4096:6144].bitcast(F32R).rearrange("p (k t) -> p k t", k=KC))
        AT2 = P.view("AT2", a1[:, 4096:8192].bitcast(F32R).rearrange("p (k t) -> p k t", k=KC))
        MG2 = P.view("MG2", a1[:, 2048:4096]); MG2W = a1[:, 2048:4096].bitcast(F32R)
        MHT = P.view("MHT", a1[:, 0:4096].bitcast(F32R).rearrange("p (k t) -> p k t", k=KC))
        KST = [P.view("KST%d" % i, a2[:, i * 2048:(i + 1) * 2048].bitcast(F32R).rearrange("p (h t) -> p h t", h=4)) for i in range(2)]
        VST = [P.view("VST%d" % i, a3[:, i * 2048:(i + 1) * 2048].bitcast(F32R).rearrange("p (i c) -> p i c", i=4)) for i in range(2)]
        biasM = P.view("biasM", a2[:, 0:4096].bitcast(BF16))
        X2e = P.view("X2e", a2[:, 0:2048])
        HT2 = P.view("HT2", a3[:, 0:4096].bitcast(F32R).rearrange("p (k t) -> p k t", k=KC))
        x2sV = P.view("x2sV", None); x2sS = P.view("x2sS", None)
        KBf = [P.view("KBf%d" % i, a3[:, i * 1024:(i + 1) * 1024].bitcast(F32R)) for i in range(2)]
        VBf = [P.view("VBf%d" % i, a3[:, 2048 + i * 1024:2048 + (i + 1) * 1024].bitcast(F32R).rearrange("p (j d) -> p j d", j=8)) for i in range(2)]

        def ld(dst, src_ap, dst_ap=None, eng="sp"):
            P.dma(eng, dst_ap if dst_ap is not None else dst[:], src_ap, writes=[dst])

        def copy(dst_ap, src_ap, reads, writes):
            cnt["cp"] += 1
            if cnt["cp"] % 2:
                P.op("act", lambda e: e.activation(out=dst_ap, in_=src_ap, func=AF.Copy), reads=reads, writes=writes)
            else:
                P.op("dve", lambda e: e.tensor_copy(out=dst_ap, in_=src_ap), reads=reads, writes=writes)

        def load_w(w_ap, k0, kc, c0, n):
            slot = wslot()
            src = w_ap[k0:k0 + kc * 128, c0:c0 + n].rearrange("(k p) n -> p k n", p=128)
            P.dma("sp", slot[:, 0:kc, 0:n], src, writes=[slot])
            return slot

        def load_g(dst, g_ap, n):
            P.dma("sp", dst[:, 0:n], g_ap[0, :].partition_broadcast(128), writes=[dst])

        def norm(Xb, x_ap, Gb, g_ap, F, out_ap, outb, final_row=None):
            scr = rot(TM, "TM")
            if F > 512:
                npc = F // 512
                for i in range(npc):
                    P.op("act", lambda e, i=i: e.activation(out=scr[:, 0:512], in_=x_ap[:, i * 512:(i + 1) * 512], func=AF.Square,
                                                            accum_out=sm[:, 8 + i:9 + i]), reads=[Xb], writes=[scr, sm])
                P.op("dve", lambda e: e.tensor_reduce(out=sm[:, 0:1], in_=sm[:, 8:8 + npc], axis=AX.X, op=ALU.add), reads=[sm], writes=[sm])
            else:
                P.op("act", lambda e: e.activation(out=scr[:, 0:F], in_=x_ap, func=AF.Square, accum_out=sm[:, 0:1]), reads=[Xb], writes=[scr, sm])
            P.op("dve", lambda e: e.tensor_scalar(out=sm[:, 1:2], in0=sm[:, 0:1], scalar1=1.0 / F, scalar2=EPS, op0=ALU.mult, op1=ALU.add), reads=[sm], writes=[sm])
            P.op("act", lambda e: e.activation(out=sm[:, 2:3], in_=sm[:, 1:2], func=AF.Sqrt), reads=[sm], writes=[sm])
            P.op("dve", lambda e: e.reciprocal(out=sm[:, 3:4], in_=sm[:, 2:3]), reads=[sm], writes=[sm])
            if final_row is not None:
                for i in range(4):
                    o = rot(TM, "TM")
                    P.op("dve", lambda e, o=o, i=i: e.scalar_tensor_tensor(out=o[:, :], in0=x_ap[:, i * 512:(i + 1) * 512], scalar=sm[:, 3:4], in1=Gb[:, i * 512:(i + 1) * 512],
                                                                          op0=ALU.mult, op1=ALU.mult), reads=[Xb, sm, Gb], writes=[o])
                    P.dma("pool", out[final_row * 128:(final_row + 1) * 128, i * 512:(i + 1) * 512], o[:, :], reads=[o], sembuf=o)
                return
            P.op("dve", lambda e: e.scalar_tensor_tensor(out=out_ap, in0=x_ap, scalar=sm[:, 3:4], in1=Gb[:, 0:F], op0=ALU.mult, op1=ALU.mult),
                 reads=[Xb, sm, Gb], writes=[outb])

        def transposes(srcb, src_ap, ncols, dstb, dst_fn, rows=128):
            nb = ncols // rows
            j = 0
            while j < nb:
                nj = min(4, nb - j)
                ps = rot(pt, "pt")
                for jj in range(nj):
                    P.op("pe", lambda e, jj=jj, j=j, ps=ps: e.transpose(out=ps[0:rows, jj * 128:(jj + 1) * 128],
                                                                        in_=src_ap[:, (j + jj) * rows:(j + jj + 1) * rows], identity=ident[:]),
                         reads=[srcb, ident], writes=[ps])
                copy(dst_fn(j, nj), ps[0:rows, 0:nj * 128].rearrange("p (j t) -> p j t", j=nj), [ps], [dstb])
                j += nj

        def mm(ps_ap, psb, lhs_fn, rhs_fn, kc, reads):
            for k in range(kc):
                P.op("pe", lambda e, k=k: e.matmul(ps_ap, lhsT=lhs_fn(k), rhs=rhs_fn(k), start=(k == 0), stop=(k == kc - 1)),
                     reads=reads, writes=[psb])

        def rope_tables(posb, pos_ap, n):
            if 'norope' in dbg:
                return
            if 'notables' in dbg:
                for tb_ in (cosA, sinA, cosI, sinI):
                    P.op("dve", lambda e, tb_=tb_: e.memset(tb_[:], 0.5), writes=[tb_])
                return
            P.op("dve", lambda e: e.tensor_copy(out=posf[:, 0:n], in_=pos_ap), reads=[posb], writes=[posf])
            for (inv, half, cs, sn) in ((invf, 16, cosA, sinA), (invfi, 8, cosI, sinI)):
                a = ang[:, 0:n, 0:half]; ki = kint[:, 0:n, 0:half]; kk = kf[:, 0:n, 0:half]; m_ = mk[:, 0:n, 0:half]
                for i_ in range(n):
                    P.op("dve", lambda e, i_=i_, inv=inv, half=half: e.tensor_scalar(out=ang[:, i_, 0:half], in0=inv[:, 0:half], scalar1=posf[:, i_:i_ + 1], scalar2=None, op0=ALU.mult),
                         reads=[posf, inv], writes=[ang])
                for (shift, dstt) in ((0.0, sn), (PI / 2, cs)):
                    P.op("dve", lambda e, a=a, ki=ki, shift=shift: e.tensor_scalar(out=ki, in0=a, scalar1=shift, scalar2=1.0 / (2 * PI), op0=ALU.add, op1=ALU.mult),
                         reads=[ang], writes=[kint])
                    P.op("dve", lambda e, ki=ki, kk=kk: e.tensor_copy(out=kk, in_=ki), reads=[kint], writes=[kf])
                    P.op("dve", lambda e, a=a, kk=kk: e.scalar_tensor_tensor(out=kk, in0=kk, scalar=-2 * PI, in1=a, op0=ALU.mult, op1=ALU.add),
                         reads=[kf, ang], writes=[kf])
                    if shift != 0.0:
                        P.op("dve", lambda e, kk=kk, shift=shift: e.tensor_scalar(out=kk, in0=kk, scalar1=shift, scalar2=None, op0=ALU.add), reads=[kf], writes=[kf])
                    P.op("dve", lambda e, kk=kk, m_=m_: e.tensor_scalar(out=m_, in0=kk, scalar1=PI, scalar2=-2 * PI, op0=ALU.is_gt, op1=ALU.mult), reads=[kf], writes=[mk])
                    P.op("dve", lambda e, kk=kk, m_=m_: e.tensor_tensor(out=kk, in0=kk, in1=m_, op=ALU.add), reads=[kf, mk], writes=[kf])
                    P.op("dve", lambda e, kk=kk, m_=m_: e.tensor_scalar(out=m_, in0=kk, scalar1=-PI, scalar2=2 * PI, op0=ALU.is_lt, op1=ALU.mult), reads=[kf], writes=[mk])
                    P.op("dve", lambda e, kk=kk, m_=m_: e.tensor_tensor(out=kk, in0=kk, in1=m_, op=ALU.add), reads=[kf, mk], writes=[kf])
                    P.op("dve", lambda e, kk=kk: e.tensor_scalar(out=kk, in0=kk, scalar1=-3.14159, scalar2=3.14159, op0=ALU.max, op1=ALU.min), reads=[kf], writes=[kf])
                    P.op("act", lambda e, kk=kk, dstt=dstt, half=half: e.activation(out=dstt[:, 0:n, 0:half], in_=kk, func=AF.Sin), reads=[kf], writes=[dstt])

        def rope(psb, ps_ap, nh, hd, half, cs_ap, sn_ap, csb, snb, dstb, dst_ap):
            P.op("act", lambda e: e.activation(out=dst_ap, in_=ps_ap, func=AF.Copy), reads=[psb], writes=[dstb])
            if 'norope' in dbg or 'noapply' in dbg:
                return
            p3 = ps_ap.rearrange("p (h d) -> p h d", h=nh)
            d3 = dst_ap.rearrange("p (h d) -> p h d", h=nh)
            x1 = p3[:, :, 0:half]; x2 = p3[:, :, half:2 * half]
            C = cs_ap.unsqueeze(1).broadcast_to([128, nh, half]); Sn = sn_ap.unsqueeze(1).broadcast_to([128, nh, half])
            t = [T64[i][:, 0:nh * half].rearrange("p (h d) -> p h d", h=nh) for i in range(4)]
            P.op("dve", lambda e: e.tensor_tensor(out=t[0], in0=x1, in1=C, op=ALU.mult), reads=[psb, csb, dstb], writes=[T64[0]])
            P.op("dve", lambda e: e.tensor_tensor(out=t[1], in0=x2, in1=Sn, op=ALU.mult), reads=[psb, snb, dstb], writes=[T64[1]])
            P.op("dve", lambda e: e.tensor_tensor(out=t[2], in0=x2, in1=C, op=ALU.mult), reads=[psb, csb, dstb], writes=[T64[2]])
            P.op("dve", lambda e: e.tensor_tensor(out=t[3], in0=x1, in1=Sn, op=ALU.mult), reads=[psb, snb, dstb], writes=[T64[3]])
            if 'apply4' in dbg:
                return
            P.op("dve", lambda e: e.tensor_tensor(out=d3[:, :, 0:half], in0=t[0], in1=t[1], op=ALU.subtract), reads=[T64[0], T64[1]], writes=[dstb])
            P.op("dve", lambda e: e.tensor_tensor(out=d3[:, :, half:2 * half], in0=t[2], in1=t[3], op=ALU.add), reads=[T64[2], T64[3]], writes=[dstb])

        for (b_, a_) in ((ident, c_ident), (identr, c_identr), (identb, c_identb), (ones, c_ones), (iotaw, c_iotaw), (iotan, c_iotan), (invf, c_invf),
                         (invfi, c_invfi), (pow2, c_pow2), (tposs, tpos), (tcurs, tcur), (posA, posall), (posO, posown)):
            ld(b_, a_)

        kv_chunks = [("ka", C_KA, 0), ("ka", C_KA + 512, 4), ("va", C_VA, 0), ("va", C_VA + 512, 512),
                     ("kb", C_KB, 0), ("kb", C_KB + 512, 4), ("vb", C_VB, 0), ("vb", C_VB + 512, 512)]
        stbufs = []
        kisS = P.view("kisS", None)
        load_g(G, g_mix, D)
        for g in range(NG if (stage >= 2 and 'nophaseA' not in dbg) else 0):
            s0 = g * 512
            rope_tables(posA, posA[:, g * 4:(g + 1) * 4], 4)
            for i in range(4):
                X = rot(XT, "XT")
                ld(X, xall[s0 + i * 128:s0 + (i + 1) * 128, :])
                norm(X, X[:], G, g_mix, D, X[:], X)
                transposes(X, X[:], D, HTG[i], lambda j, nj, i=i: HTG[i][:, j:j + nj, :])
            for (kind, c0, aux) in kv_chunks:
                slot = load_w(w_in, 0, KC, c0, 512)
                if kind in ("ka", "kb"):
                    st = rot(KST, "cp")
                else:
                    st = rot(VST, "cp")
                for i in range(4):
                    ps = rot(pa, "pa")
                    mm(ps[:, :], ps, lambda k, i=i: HTG[i][:, k, :], lambda k, slot=slot: slot[:, k, :], KC, [HTG[i], slot])
                    if kind in ("ka", "kb"):
                        tm = rot(TM, "TM")
                        rope(ps, ps[:, :], 4, 128, 16, cosA[:, i, :], sinA[:, i, :], cosA, sinA, tm, tm[:, :])
                        transposes(tm, tm[:, :], 512, st, lambda j, nj, st=st, i=i: st[:, j:j + nj, i * 128:(i + 1) * 128])
                    else:
                        copy(st[:, i, :], ps[:, :], [ps], [st])
                if kind in ("ka", "kb"):
                    dst = KaT if kind == "ka" else KbT
                    P.dma("pool", dst[aux:aux + 4, :, s0:s0 + 512].rearrange("h d s -> d h s"), st[:, :, :], reads=[st], sembuf=st)
                    if kind == "kb":
                        P.op("dve", lambda e, st=st: e.tensor_reduce(out=sm[:, 32:40].rearrange("p (h b) -> p h b", h=4),
                                                                     in_=st[:, :, :].bitcast(F32).rearrange("p h (b t) -> p h b t", b=2), axis=AX.X, op=ALU.add),
                             reads=[st], writes=[sm])
                        P.op("dve", lambda e, aux=aux, g=g: e.tensor_scalar(out=kmT[:, aux:aux + 4, 2 * g:2 * g + 2], in0=sm[:, 32:40].rearrange("p (h b) -> p h b", h=4),
                                                                            scalar1=1.0 / 256, scalar2=None, op0=ALU.mult), reads=[sm], writes=[kmT])
                else:
                    dst = Va if kind == "va" else Vb
                    P.dma("pool", dst[s0:s0 + 512, aux:aux + 512].rearrange("(i p) c -> p i c", p=128), st[:, :, :], reads=[st], sembuf=st)
                if st not in stbufs:
                    stbufs.append(st)
            slot = load_w(w_in, 0, KC, C_KI, 64)
            kis = rot(KIc, "KI")
            for i in range(4):
                ps = rot(pa, "pa")
                mm(ps[:, 0:64], ps, lambda k, i=i: HTG[i][:, k, :], lambda k, slot=slot: slot[:, k, 0:64], KC, [HTG[i], slot])
                tm = rot(TM, "TM")
                rope(ps, ps[:, 0:64], 1, 64, 8, cosI[:, i, :], sinI[:, i, :], cosI, sinI, tm, tm[:, 0:64])
                psq = rot(pt, "pt")
                P.op("pe", lambda e, tm=tm, psq=psq: e.transpose(out=psq[0:64, 0:128], in_=tm[:, 0:64], identity=ident[:]), reads=[tm, ident], writes=[psq])
                copy(kis[0:64, i * 128:(i + 1) * 128], psq[0:64, 0:128], [psq], [kis])
            P.dma("pool", KiT[:, s0:s0 + 512], kis[0:64, :], reads=[kis], sembuf=kisS)
            if kisS not in stbufs:
                stbufs.append(kisS)
        if stage >= 2:
            P.wait_all_dma("sp", stbufs)

        P.alias(HTG, [MHT])
        load_g(G, g_mem_kv, D)
        for mt in range(2):
            X = rot(XT, "XT")
            ld(X, mem[mt * 128:(mt + 1) * 128, :])
            norm(X, X[:], G, g_mem_kv, D, X[:], X)
            transposes(X, X[:], D, MHT, lambda j, nj, mt=mt: MHT[:, j:j + nj, mt * 128:(mt + 1) * 128])
        for c in range(2):
            slot = load_w(w_mem_kv, 0, KC, c * 512, 512)
            for mt in range(2):
                ps = rot(pa, "pa")
                mm(ps[:, :], ps, lambda k, mt=mt: MHT[:, k, mt * 128:(mt + 1) * 128], lambda k, slot=slot: slot[:, k, :], KC, [MHT, slot])
                if c == 0:
                    tm = rot(TM, "TM")
                    copy(tm[:, :], ps[:, :], [ps], [tm])
                    transposes(tm, tm[:, :], 512, KMT, lambda j, nj, mt=mt: KMT[:, j:j + nj, mt * 128:(mt + 1) * 128])
                else:
                    copy(VM[:, mt, :], ps[:, :], [ps], [VM])
        P.alias([MHT] + KST + VST, [score, MG, AT, biasM] + KBf + VBf)

        sc_att = 1.0 / math.sqrt(128.0)

        def attention(QT, nheads, nchunks, kload, bias_fn, OT):
            items = [(h, s0) for h in range(nheads) for s0 in range(0, nchunks, ABATCH)]
            state = {}

            def emit_qk(h, s0):
                nb = min(ABATCH, nchunks - s0)
                L = rot(pL, "pL")
                vaps = []
                for jj in range(nb):
                    j = s0 + jj
                    kap, vap, kb_, vb_ = kload(h, j)
                    vaps.append((vap, vb_))
                    extra = bias_fn(h, j)
                    P.op("pe", lambda e, kap=kap, L=L, h=h, extra=extra, jj=jj: e.matmul(L[:, jj * TB:(jj + 1) * TB], lhsT=kap, rhs=QT[:, h, :], start=True, stop=(len(extra) == 0)),
                         reads=[kb_, QT], writes=[L])
                    for xi, (lh, rh, rb) in enumerate(extra):
                        P.op("pe", lambda e, lh=lh, rh=rh, L=L, xi=xi, extra=extra, jj=jj: e.matmul(L[:, jj * TB:(jj + 1) * TB], lhsT=lh, rhs=rh, start=False, stop=(xi == len(extra) - 1)),
                             reads=rb, writes=[L])
                state[(h, s0)] = (L, vaps, nb)

            def emit_pv(h, s0):
                L, vaps, nb = state.pop((h, s0))
                p_ = rot(PT, "PT")
                P.op("act", lambda e, p_=p_, L=L, nb=nb: e.activation(out=p_[:, 0:nb * TB], in_=L[:, 0:nb * TB], func=AF.Exp, scale=sc_att), reads=[L], writes=[p_])
                for jj in range(nb):
                    j = s0 + jj
                    vap, vb_ = vaps[jj]
                    P.op("pe", lambda e, vap=vap, p_=p_, j=j, jj=jj: e.matmul(pO[:, 0:TB], lhsT=vap, rhs=p_[:, jj * TB:(jj + 1) * TB], start=(j == 0), stop=(j == nchunks - 1)),
                         reads=[vb_, p_], writes=[pO])
                    P.op("pe", lambda e, p_=p_, j=j, jj=jj: e.matmul(pR[:, 0:TB], lhsT=ones[:, :], rhs=p_[:, jj * TB:(jj + 1) * TB], start=(j == 0), stop=(j == nchunks - 1)),
                         reads=[ones, p_], writes=[pR])
                if s0 + ABATCH >= nchunks:
                    REC = rot(TM, "TM")
                    P.op("dve", lambda e, REC=REC: e.reciprocal(out=REC[:, 0:TB], in_=pR[:, 0:TB]), reads=[pR], writes=[REC])
                    P.op("dve", lambda e, h=h, REC=REC: e.tensor_tensor(out=OT[:, h, :], in0=pO[:, 0:TB], in1=REC[:, 0:TB], op=ALU.mult), reads=[pO, REC], writes=[OT])

            for idx in range(len(items) + 1):
                if idx < len(items):
                    emit_qk(*items[idx])
                if idx >= 1:
                    emit_pv(*items[idx - 1])

        for r in range(NR):
            EXT = 512 * (r + 1)
            NCH = EXT // 512
            X = rot(XT, "XT")
            ld(X, xown[r * 128:(r + 1) * 128, :])
            if stage >= 2 and 'norounds2' not in dbg:
                load_g(G, g_mix, D)
                norm(X, X[:], G, g_mix, D, X[:], X)
                transposes(X, X[:], D, HT, lambda j, nj: HT[:, j:j + nj, :])
                if r % 4 == 0:
                    rope_tables(posO, posO[:, r:r + 4], 4)
                rc = r % 4
                load_g(Gq, g_cq, 512)
                slot = load_w(w_in, 0, KC, C_CQ, 512)
                ps = rot(pa, "pa")
                mm(ps[:, :], ps, lambda k: HT[:, k, :], lambda k, slot=slot: slot[:, k, :], KC, [HT, slot])
                tm = rot(TM, "TM")
                copy(tm[:, :], ps[:, :], [ps], [tm])
                norm(tm, tm[:, :], Gq, g_cq, 512, tm[:, :], tm)
                transposes(tm, tm[:, :], 512, CQT, lambda j, nj: CQT[:, j:j + nj, :])
                for c in range(2):
                    slot = load_w(w_uq, 0, 4, c * 512, 512)
                    ps = rot(pa, "pa")
                    mm(ps[:, :], ps, lambda k: CQT[:, k, :], lambda k, slot=slot: slot[:, k, :], 4, [CQT, slot])
                    tm = rot(TM, "TM")
                    rope(ps, ps[:, :], 4, 128, 16, cosA[:, rc, :], sinA[:, rc, :], cosA, sinA, tm, tm[:, :])
                    transposes(tm, tm[:, :], 512, QAT, lambda j, nj, c=c: QAT[:, 4 * c + j:4 * c + j + nj, :])
            if stage >= 2.2:
                slot = load_w(w_in, 0, KC, C_WI, 16)
                ps = rot(pa, "pa")
                mm(ps[:, 0:16], ps, lambda k: HT[:, k, :], lambda k, slot=slot: slot[:, k, 0:16], KC, [HT, slot])
                P.op("dve", lambda e, ps=ps: e.tensor_scalar(out=WI[:, :], in0=ps[:, 0:16], scalar1=1.0 / 32.0, scalar2=None, op0=ALU.mult), reads=[ps], writes=[WI])
                P.op("dve", lambda e: e.tensor_scalar(out=SG[:, :], in0=WI[:, :], scalar1=0.0, scalar2=2.0, op0=ALU.is_ge, op1=ALU.mult), reads=[WI], writes=[SG])
                P.op("dve", lambda e: e.tensor_scalar(out=SG[:, :], in0=SG[:, :], scalar1=-1.0, scalar2=None, op0=ALU.add), reads=[SG], writes=[SG])
                P.op("dve", lambda e: e.tensor_scalar(out=LO[:, :], in0=SG[:, :], scalar1=-1.0, scalar2=0.5e30, op0=ALU.add, op1=ALU.mult), reads=[SG], writes=[LO])
                P.op("dve", lambda e: e.tensor_scalar(out=HI[:, :], in0=SG[:, :], scalar1=1.0, scalar2=0.5e30, op0=ALU.add, op1=ALU.mult), reads=[SG], writes=[HI])
                for c in range(2):
                    slot = load_w(w_iq, 0, 4, c * 512, 512)
                    ps = rot(pa, "pa")
                    mm(ps[:, :], ps, lambda k: CQT[:, k, :], lambda k, slot=slot: slot[:, k, :], 4, [CQT, slot])
                    tm = rot(TM, "TM")
                    rope(ps, ps[:, :], 8, 64, 8, cosI[:, rc, :], sinI[:, rc, :], cosI, sinI, tm, tm[:, :])
                    P.op("dve", lambda e, tm=tm, c=c: e.tensor_tensor(out=tm[:, :].rearrange("p (h d) -> p h d", h=8), in0=tm[:, :].rearrange("p (h d) -> p h d", h=8),
                                                                   in1=WI[:, 8 * c:8 * c + 8].unsqueeze(2).broadcast_to([128, 8, 64]), op=ALU.mult), reads=[tm, WI], writes=[tm])
                    transposes(tm, tm[:, :], 512, QIT, lambda j, nj, c=c: QIT[:, 4 * c + j:4 * c + j + nj, :])
                P.op("dve", lambda e, r=r: e.tensor_scalar(out=sm[:, 20:21], in0=tposs[:, r:r + 1], scalar1=-float(512 * r), scalar2=None, op0=ALU.add), reads=[tposs], writes=[sm])
                P.op("dve", lambda e: e.tensor_scalar(out=CB16[:, :], in0=iotaw[:, :], scalar1=sm[:, 20:21], scalar2=MB, op0=ALU.is_gt, op1=ALU.mult), reads=[iotaw, sm], writes=[CB16])
                for c in range(NCH):
                    kc_ = rot(KIc, "KI")
                    P.dma("sp", kc_[0:64, :], KiT[:, c * 512:(c + 1) * 512], writes=[kc_])
                    P.dma("sp", kc_[64:128, :], KiT[:, c * 512:(c + 1) * 512], writes=[kc_])
                    Lb4 = [pL[0], pL[1], pa[0], pa[1]]
                    Ls, rbs = {}, {}
                    for h in range(18):
                        if h < 16:
                            pr, hf = h // 2, h % 2
                            L = Lb4[h % 4]
                            Ls[h] = L
                            P.op("pe", lambda e, L=L, pr=pr, hf=hf, kc_=kc_: e.matmul(L[:, :], lhsT=QIT[64 * hf:64 * hf + 64, pr, :], rhs=kc_[64 * hf:64 * hf + 64, :], start=True, stop=True),
                                 reads=[QIT, kc_], writes=[L])
                        if 1 <= h <= 16:
                            hh = h - 1
                            L = Ls.pop(hh)
                            rb = RB[hh % 2]
                            rbs[hh] = rb
                            P.op("dve", lambda e, L=L, rb=rb, hh=hh: e.tensor_scalar(out=rb[:, :], in0=L[:, :], scalar1=LO[:, hh:hh + 1], scalar2=HI[:, hh:hh + 1], op0=ALU.max, op1=ALU.min),
                                 reads=[L, LO, HI], writes=[rb])
                        if 2 <= h <= 17:
                            ph = h - 2
                            prb = rbs.pop(ph)
                            P.op("pe", lambda e, prb=prb, ph=ph: e.matmul(pO[:, :], lhsT=identr[:, :], rhs=prb[:, :], start=(ph == 0), stop=(ph == 15)), reads=[identr, prb], writes=[pO])
                    if c == NCH - 1:
                        CB = rot(TM, "TM")
                        P.op("dve", lambda e, CB=CB: e.tensor_scalar(out=CB[:, :], in0=iotaw[:, :], scalar1=sm[:, 20:21], scalar2=NEG, op0=ALU.is_gt, op1=ALU.mult), reads=[iotaw, sm], writes=[CB])
                        P.op("dve", lambda e, c=c, CB=CB: e.tensor_tensor(out=scoreW[:, c * 512:(c + 1) * 512], in0=pO[:, :], in1=CB[:, :], op=ALU.add), reads=[pO, CB], writes=[score])
                    else:
                        P.op("act", lambda e, c=c: e.activation(out=scoreW[:, c * 512:(c + 1) * 512], in_=pO[:, :], func=AF.Copy), reads=[pO], writes=[score])
                NIT = 36 if r == 0 else 22
                if r == 0:
                    P.op("dve", lambda e: e.memset(sm[:, 21:22], -1.0e4), writes=[sm])
                else:
                    P.op("dve", lambda e: e.tensor_reduce(out=sm[:, 21:22], in_=score[:, 0:512], axis=AX.X, op=ALU.min), reads=[score], writes=[sm])
                P.op("dve", lambda e, EXT=EXT: e.tensor_reduce(out=sm[:, 22:23], in_=score[:, 0:EXT], axis=AX.X, op=ALU.max), reads=[score], writes=[sm])
                P.op("dve", lambda e: e.tensor_tensor(out=sm[:, 23:24], in0=sm[:, 22:23], in1=sm[:, 21:22], op=ALU.subtract), reads=[sm], writes=[sm])
                P.op("dve", lambda e: e.tensor_scalar(out=Wk[:, :], in0=pow2[:, :], scalar1=sm[:, 23:24], scalar2=None, op0=ALU.mult), reads=[pow2, sm], writes=[Wk])
                P.op("dve", lambda e: e.tensor_tensor(out=sm[:, 24:25], in0=sm[:, 21:22], in1=Wk[:, 0:1], op=ALU.add), reads=[sm, Wk], writes=[sm])
                jk = biasM
                for k in range(NIT):
                    P.op("dve", lambda e, EXT=EXT: e.tensor_scalar(out=jk[:, 0:EXT], in0=score[:, 0:EXT], scalar1=sm[:, 24:25], scalar2=None, op0=ALU.is_ge, op1=ALU.add,
                                                                   accum_out=sm[:, 25:26]), reads=[score, sm], writes=[jk, sm])
                    P.op("dve", lambda e, k=k: e.scalar_tensor_tensor(out=sm[:, 26:27], in0=sm[:, 25:26], scalar=256.0, in1=Wk[:, k:k + 1], op0=ALU.is_ge, op1=ALU.mult),
                         reads=[sm, Wk], writes=[sm])
                    P.op("dve", lambda e, k=k: e.scalar_tensor_tensor(out=sm[:, 24:25], in0=sm[:, 26:27], scalar=Wk[:, k + 1:k + 2], in1=sm[:, 24:25], op0=ALU.subtract, op1=ALU.add),
                         reads=[sm, Wk], writes=[sm])
                P.op("dve", lambda e, NIT=NIT: e.tensor_tensor(out=sm[:, 27:28], in0=sm[:, 24:25], in1=Wk[:, NIT:NIT + 1], op=ALU.subtract), reads=[sm, Wk], writes=[sm])
                P.op("dve", lambda e, EXT=EXT: e.tensor_scalar(out=biasM[:, 0:EXT], in0=score[:, 0:EXT], scalar1=sm[:, 27:28], scalar2=MB, op0=ALU.is_lt, op1=ALU.mult),
                     reads=[score, sm], writes=[biasM])

            if stage >= 2.4:
                def kload_a(h, j, EXT=EXT):
                    jj = j % 8
                    if jj == 0:
                        n = min(1024, EXT - j * 128)
                        kb_ = rot(KBf, "KB"); vb_ = rot(VBf, "VB")
                        kload_a.cur = (kb_, vb_)
                        P.dma("sp", kb_[:, 0:n], KaT[h, :, j * 128:j * 128 + n], writes=[kb_])
                        P.dma("sp", vb_[:, 0:n // 128, :], Va[j * 128:j * 128 + n, h * 128:(h + 1) * 128].rearrange("(j p) d -> p j d", p=128), writes=[vb_])
                    kb_, vb_ = kload_a.cur
                    return kb_[:, jj * 128:(jj + 1) * 128], vb_[:, jj, :], kb_, vb_

                attention(QAT, 8, EXT // 128, kload_a,
                          lambda h, j: [(biasM[:, j * 128:(j + 1) * 128], identb[:, :], [biasM, identb])], OAT)

            if stage >= 2.6:
                for c in range(2):
                    slot = load_w(w_in, 0, KC, C_QB + c * 512, 512)
                    ps = rot(pa, "pa")
                    mm(ps[:, :], ps, lambda k: HT[:, k, :], lambda k, slot=slot: slot[:, k, :], KC, [HT, slot])
                    tm = rot(TM, "TM")
                    rope(ps, ps[:, :], 4, 128, 16, cosA[:, rc, :], sinA[:, rc, :], cosA, sinA, tm, tm[:, :])
                    transposes(tm, tm[:, :], 512, QBT, lambda j, nj, c=c: QBT[:, 4 * c + j:4 * c + j + nj, :])
                for h in range(8):
                    P.op("pe", lambda e, h=h: e.matmul(pR[:, h * 32:h * 32 + NBLK], lhsT=QBT[:, h, :], rhs=kmT[:, h, 0:NBLK], start=True, stop=True), reads=[QBT, kmT], writes=[pR])
                P.op("dve", lambda e, r=r: e.tensor_scalar(out=NM[:, :], in0=iotan[:, :], scalar1=tcurs[:, r:r + 1], scalar2=NEG, op0=ALU.is_ge, op1=ALU.mult), reads=[iotan, tcurs], writes=[NM])
                P.op("dve", lambda e, r=r: e.tensor_scalar(out=EQ[:, :], in0=iotan[:, :], scalar1=tcurs[:, r:r + 1], scalar2=None, op0=ALU.not_equal), reads=[iotan, tcurs], writes=[EQ])
                P.op("dve", lambda e: e.memset(GS[:, :, :], NEG), writes=[GS])
                P.op("dve", lambda e: e.tensor_tensor(out=GS[:, :, 0:NBLK], in0=pR[:, 0:256].rearrange("p (h n) -> p h n", h=8)[:, :, 0:NBLK],
                                                      in1=NM[:, 0:NBLK].unsqueeze(1).broadcast_to([128, 8, NBLK]), op=ALU.add), reads=[pR, NM], writes=[GS])
                for h in range(8):
                    P.op("dve", lambda e, h=h: e.max(out=M8[:, h, :], in_=GS[:, h, :]), reads=[GS], writes=[M8])
                P.op("dve", lambda e: e.tensor_scalar(out=TH[:, :], in0=M8[:, :, 2], scalar1=-1.0e29, scalar2=None, op0=ALU.max), reads=[M8], writes=[TH])
                P.op("dve", lambda e: e.tensor_tensor(out=GB[:, :, :], in0=GS[:, :, :], in1=TH[:, :].unsqueeze(2).broadcast_to([128, 8, 32]), op=ALU.is_lt), reads=[GS, TH], writes=[GB])
                P.op("dve", lambda e: e.scalar_tensor_tensor(out=GB[:, :, :], in0=GB[:, :, :], scalar=MB, in1=EQ[:, :].unsqueeze(1).broadcast_to([128, 8, 32]), op0=ALU.mult, op1=ALU.mult),
                     reads=[GB, EQ], writes=[GB])
                for h in range(8):
                    psq = rot(pt, "pt")
                    P.op("pe", lambda e, h=h, psq=psq: e.transpose(out=psq[0:32, 0:128], in_=GB[:, h, :], identity=ident[:]), reads=[GB, ident], writes=[psq])
                    copy(GBT[:, h, :], psq[0:32, 0:128], [psq], [GBT])

                def kload_b(h, j, EXT=EXT):
                    jj = j % 8
                    if jj == 0:
                        n = min(1024, EXT - j * 128)
                        kb_ = rot(KBf, "KB"); vb_ = rot(VBf, "VB")
                        kload_b.cur = (kb_, vb_)
                        P.dma("sp", kb_[:, 0:n], KbT[h, :, j * 128:j * 128 + n], writes=[kb_])
                        P.dma("sp", vb_[:, 0:n // 128, :], Vb[j * 128:j * 128 + n, h * 128:(h + 1) * 128].rearrange("(j p) d -> p j d", p=128), writes=[vb_])
                    kb_, vb_ = kload_b.cur
                    return kb_[:, jj * 128:(jj + 1) * 128], vb_[:, jj, :], kb_, vb_

                def bias_b(h, j, r=r):
                    n = j // 2
                    ex = [(identb[0:32, n:n + 1].broadcast_to([32, 128]), GBT[:, h, :], [identb, GBT])]
                    if j >= 4 * r:
                        w = j - 4 * r
                        ex.append((CB16[:, w * 128:(w + 1) * 128], identb[:, :], [CB16, identb]))
                    return ex

                attention(QBT, 8, EXT // 128, kload_b, bias_b, OBT)

            if stage < 3:
                ld(X1, xown[r * 128:(r + 1) * 128, :])
            if stage >= 3:
                for c in range(4):
                    for bi, (OT_, wo, cg) in enumerate(((OAT, w_dsa_o, C_GA), (OBT, w_moba_o, C_GB))):
                        slot = load_w(wo, 0, 8, c * 512, 512)
                        ps = rot(pa, "pa")
                        mm(ps[:, :], ps, lambda k, OT_=OT_: OT_[:, k, :], lambda k, slot=slot: slot[:, k, :], 8, [OT_, slot])
                        y = rot(TM, "TM")
                        P.op("act", lambda e, y=y, ps=ps: e.activation(out=y[:, :], in_=ps[:, :], func=AF.Copy), reads=[ps], writes=[y])
                        slot2 = load_w(w_in, 0, KC, cg + c * 512, 512)
                        ps2 = rot(pa, "pa")
                        mm(ps2[:, :], ps2, lambda k: HT[:, k, :], lambda k, slot2=slot2: slot2[:, k, :], KC, [HT, slot2])
                        sg = rot(TM, "TM")
                        P.op("act", lambda e, sg=sg, ps2=ps2: e.activation(out=sg[:, :], in_=ps2[:, :], func=AF.Sigmoid), reads=[ps2], writes=[sg])
                        if bi == 0:
                            P.op("dve", lambda e, y=y, sg=sg, c=c: e.tensor_tensor(out=MGW[:, c * 512:(c + 1) * 512], in0=y[:, :], in1=sg[:, :], op=ALU.mult), reads=[y, sg], writes=[MG])
                        else:
                            P.op("dve", lambda e, y=y, sg=sg: e.tensor_tensor(out=y[:, :], in0=y[:, :], in1=sg[:, :], op=ALU.mult), reads=[y, sg], writes=[y])
                            P.op("dve", lambda e, y=y, c=c: e.tensor_tensor(out=MGW[:, c * 512:(c + 1) * 512], in0=MG[:, c * 512:(c + 1) * 512], in1=y[:, :], op=ALU.add), reads=[y, MG], writes=[MG])
                transposes(MG, MG[:, :], D, HT, lambda j, nj: HT[:, j:j + nj, :])
                ld(X1, xown[r * 128:(r + 1) * 128, :])
                for c in range(4):
                    slot = load_w(w_out, 0, KC, c * 512, 512)
                    ps = rot(pa, "pa")
                    mm(ps[:, :], ps, lambda k: HT[:, k, :], lambda k, slot=slot: slot[:, k, :], KC, [HT, slot])
                    P.op("dve", lambda e, ps=ps, c=c: e.tensor_tensor(out=X1[:, c * 512:(c + 1) * 512], in0=ps[:, :], in1=X1[:, c * 512:(c + 1) * 512], op=ALU.add), reads=[ps, X1], writes=[X1])

            load_g(G, g_mem_q, D)
            norm(X1, X1[:, :], G, g_mem_q, D, MGW, MG)
            transposes(MG, MG[:, :], D, HT, lambda j, nj: HT[:, j:j + nj, :])
            slot = load_w(w_mem_q, 0, KC, 0, 512)
            ps = rot(pa, "pa")
            mm(ps[:, :], ps, lambda k: HT[:, k, :], lambda k, slot=slot: slot[:, k, :], KC, [HT, slot])
            tm = rot(TM, "TM")
            copy(tm[:, :], ps[:, :], [ps], [tm])
            transposes(tm, tm[:, :], 512, QMT, lambda j, nj: QMT[:, j:j + nj, :])
            attention(QMT, 4, 2, lambda h, j: (KMT[:, h, j * 128:(j + 1) * 128], VM[:, j, h * 128:(h + 1) * 128], KMT, VM), lambda h, j: [], OMT)
            for c in range(4):
                slot = load_w(w_mem_o, 0, 4, c * 512, 512)
                ps = rot(pa, "pa")
                mm(ps[:, :], ps, lambda k: OMT[:, k, :], lambda k, slot=slot: slot[:, k, :], 4, [OMT, slot])
                P.op("dve", lambda e, ps=ps, c=c: e.tensor_tensor(out=X1[:, c * 512:(c + 1) * 512], in0=ps[:, :], in1=X1[:, c * 512:(c + 1) * 512], op=ALU.add), reads=[ps, X1], writes=[X1])

            pair = NR >= 2
            if pair and r % 2 == 0:
                P.dma("pool", x2s, X1[:, :], reads=[X1], writes=[x2sV], sembuf=x2sS)
                continue
            tiles = [(r, X1, MG, MGW)]
            if pair:
                P.alias([biasM], [X2e])
                P.dma("sp", X2e[:, :], x2s, reads=[x2sV], writes=[X2e])
                tiles = [(r - 1, X2e, MG, MGW), (r, X1, MG2, MG2W)]
            nt = len(tiles)
            P.alias(KBf + VBf, [HT2])
            load_g(G, g_ff, D)
            for m, (row, Xb, NB, NBW) in enumerate(tiles):
                norm(Xb, Xb[:, :], G, g_ff, D, NBW, NB)
                transposes(NB, NB[:, :], D, HT2, lambda j, nj, m=m: HT2[:, j:j + nj, m * 128:(m + 1) * 128])
            for qd in range(4):
                for j in range(4):
                    slot = load_w(w_ff1, 0, KC, qd * 2048 + j * 512, 512)
                    for fs in range(4):
                        ps = rot(pa, "pa")
                        mm(ps[:, 0:nt * 128], ps, lambda k, slot=slot, fs=fs: slot[:, k, fs * 128:(fs + 1) * 128], lambda k: HT2[:, k, 0:nt * 128], KC, [HT2, slot])
                        tm = rot(TM, "TM")
                        P.op("act", lambda e, tm=tm, ps=ps: e.activation(out=tm[:, 0:nt * 128], in_=ps[:, 0:nt * 128], func=AF.Relu), reads=[ps], writes=[tm])
                        P.op("dve", lambda e, tm=tm, j=j, fs=fs: e.tensor_tensor(out=AT2[:, j * 4 + fs, 0:nt * 128], in0=tm[:, 0:nt * 128], in1=tm[:, 0:nt * 128], op=ALU.mult), reads=[tm], writes=[AT2])
                for c in range(4):
                    slot = wslot()
                    P.dma("sp", slot[:, :, :], w_ff2[qd * 2048:(qd + 1) * 2048, c * 512:(c + 1) * 512].rearrange("(k p) n -> p k n", p=128), writes=[slot])
                    for m, (row, Xb, NB, NBW) in enumerate(tiles):
                        ps = rot(pa, "pa")
                        mm(ps[:, :], ps, lambda k, m=m: AT2[:, k, m * 128:(m + 1) * 128], lambda k, slot=slot: slot[:, k, :], KC, [AT2, slot])
                        P.op("dve", lambda e, ps=ps, c=c, Xb=Xb: e.tensor_tensor(out=Xb[:, c * 512:(c + 1) * 512], in0=ps[:, :], in1=Xb[:, c * 512:(c + 1) * 512], op=ALU.add), reads=[ps, Xb], writes=[Xb])

            load_g(G, g_final, D)
            for m, (row, Xb, NB, NBW) in enumerate(tiles):
                norm(Xb, Xb[:, :], G, g_final, D, None, None, final_row=row)
            P.alias([HT2], KBf + VBf)
            if pair:
                P.alias([X2e], [biasM])
        P.wait_all_dma("pool", TM)
        P.emit()
    return nc


def host_inputs(inputs, S):
    NR = S // 512
    f32 = np.float32
    consts = {
        "c_ident": np.eye(128, dtype=f32),
        "c_identr": np.eye(128, dtype=f32),
        "c_identb": np.eye(128, dtype=f32).astype(ml_dtypes.bfloat16),
        "c_ones": np.ones((128, 128), f32),
        "c_iotaw": np.tile(np.arange(512, dtype=f32)[None, :], (128, 1)),
        "c_iotan": np.tile(np.arange(32, dtype=f32)[None, :], (128, 1)),
        "c_invf": np.tile((np.float32(500000.0) ** (-np.arange(16, dtype=f32) * f32(2.0 / 32)))[None, :], (128, 1)).astype(f32),
        "c_invfi": np.tile((np.float32(500000.0) ** (-np.arange(8, dtype=f32) * f32(2.0 / 16)))[None, :], (128, 1)).astype(f32),
        "c_pow2": np.tile((0.5 ** np.arange(1, 49, dtype=np.float64)).astype(f32)[None, :], (128, 1)),
    }
    wnames = ["w_in", "w_uq", "w_iq", "w_dsa_o", "w_moba_o", "w_out", "w_mem_q", "w_mem_kv", "w_mem_o", "w_ff1", "w_ff2"]
    gnames = ["g_mix", "g_cq", "g_mem_q", "g_mem_kv", "g_ff"]
    shared = dict(consts)
    for n in wnames:
        shared[n] = np.ascontiguousarray(np.asarray(inputs[n], f32)[0])
    for n in gnames:
        shared[n] = np.ascontiguousarray(np.asarray(inputs[n], f32)[0][None, :])
    shared["g_final"] = np.ascontiguousarray(np.asarray(inputs["g_final"], f32)[None, :])
    x = np.asarray(inputs["x"], f32)
    pos = np.asarray(inputs["positions"], np.int32)
    memv = np.asarray(inputs["mem"], f32)
    maps, owners = [], []
    for c in range(8):
        b, q = c // 4, c % 4
        blks = [4 * r + (q if r % 2 == 0 else 3 - q) for r in range(NR)]
        rows = np.concatenate([np.arange(bk * 128, (bk + 1) * 128) for bk in blks])
        m = dict(shared)
        m["xall"] = np.ascontiguousarray(x[b])
        m["posall"] = np.ascontiguousarray(pos[b].reshape(S // 128, 128).T)
        m["xown"] = np.ascontiguousarray(x[b][rows])
        po = np.zeros((128, max(4, NR)), np.int32)
        po[:, :NR] = pos[b][rows].reshape(NR, 128).T
        m["posown"] = po
        m["tpos"] = np.ascontiguousarray(rows.reshape(NR, 128).T.astype(f32))
        m["tcur"] = np.ascontiguousarray((rows // 256).reshape(NR, 128).T.astype(f32))
        m["mem"] = np.ascontiguousarray(memv[b])
        maps.append(m)
        owners.append((b, rows))
    return maps, owners


_NC_CACHE = {}


def kernel(**inputs):
    S = int(np.asarray(inputs["x"]).shape[1])
    if S not in _NC_CACHE:
        _NC_CACHE[S] = build(S)
    nc = _NC_CACHE[S]
    maps, owners = host_inputs(inputs, S)
    res = run_bass_kernel_spmd(nc, maps, core_ids=list(range(8)))
    outp = np.zeros((2, S, D), np.float32)
    for c, (b, rows) in enumerate(owners):
        outp[b, rows] = np.asarray(res.results[c]["out"], np.float32)
    return outp
```
